# Optimizing a Trainium2 kernel written in Bass

```python
import math
import jax, jax.numpy as jnp
from jax import lax
import numpy as np


D_MODEL = 2048
BATCH = 2
SEQ = 4096
DEPTH = 2
DEC_BATCH = 4
DEC_SEQ = 8192
PAST_LEN = 128

GRID_W = 64
N_MIXERS = 2
N_NA_LAYERS = (DEPTH + N_MIXERS - 1) // N_MIXERS
N_RET_LAYERS = DEPTH // N_MIXERS
NA_HEAD_DIM = 32
NA_HEADS = D_MODEL // NA_HEAD_DIM
NA_WIN_ROWS = 8
NA_WIN_COLS = 16
RET_HEADS = 8
RET_QK_DIM = D_MODEL // RET_HEADS
RET_V_WIDTH = 2 * D_MODEL
RET_V_DIM = RET_V_WIDTH // RET_HEADS
RET_CHUNK = 128
ROPE_BASE = 10000.0
D_FF = 4 * D_MODEL
NORM_EPS = 1e-6
GN_EPS = 1e-5
NEG_INF = -1e30

kernel_name = "hybrid_na_retention_encoder"


def rms_norm(x, w):
    x32 = x.astype(jnp.float32)
    y = x32 * lax.rsqrt(jnp.mean(x32 * x32, axis=-1, keepdims=True) + NORM_EPS)
    return (y * w.astype(jnp.float32)).astype(x.dtype)


def neighbourhood_attention(x, w_qkv, rpb, w_o):
    bsz, length, _ = x.shape
    rows = length // GRID_W
    kh = min(NA_WIN_ROWS, rows)
    qkv = (x @ w_qkv).reshape(bsz, rows, GRID_W, 3, NA_HEADS, NA_HEAD_DIM)
    q = qkv[:, :, :, 0] * (NA_HEAD_DIM ** -0.5)
    k = qkv[:, :, :, 1]
    v = qkv[:, :, :, 2]
    cols = jnp.arange(GRID_W)
    col_start = jnp.clip(cols - NA_WIN_COLS // 2, 0, GRID_W - NA_WIN_COLS)
    col_mask = (cols[None, :] >= col_start[:, None]) & (cols[None, :] < col_start[:, None] + NA_WIN_COLS)
    col_rel = jnp.clip(cols[None, :] - cols[:, None] + NA_WIN_COLS - 1, 0, 2 * NA_WIN_COLS - 2)
    col_bias = rpb.astype(jnp.float32)[:, :, col_rel]

    def one_row(r):
        rs = jnp.clip(r - kh // 2, 0, rows - kh)
        k_blk = lax.dynamic_slice_in_dim(k, rs, kh, axis=1)
        v_blk = lax.dynamic_slice_in_dim(v, rs, kh, axis=1)
        q_r = lax.dynamic_index_in_dim(q, r, axis=1, keepdims=False)
        s = jnp.einsum('bqhd,bakhd->bhqak', q_r.astype(jnp.float32), k_blk.astype(jnp.float32))
        row_rel = rs + jnp.arange(kh) - r + NA_WIN_ROWS - 1
        bias = jnp.transpose(col_bias[:, row_rel], (0, 2, 1, 3))
        s = jnp.where(col_mask[None, None, :, None, :], s + bias[None], NEG_INF)
        p = jax.nn.softmax(s.reshape(bsz, NA_HEADS, GRID_W, kh * GRID_W), axis=-1).reshape(s.shape)
        return jnp.einsum('bhqak,bakhd->bqhd', p.astype(v.dtype), v_blk)

    o = lax.map(one_row, jnp.arange(rows))
    o = jnp.transpose(o, (1, 0, 2, 3, 4)).reshape(bsz, length, D_MODEL)
    return o @ w_o


def rotary(x, positions):
    half = x.shape[-1] // 2
    freqs = ROPE_BASE ** (-jnp.arange(half, dtype=jnp.float32) / half)
    ang = positions.astype(jnp.float32)[:, None] * freqs[None, :]
    cos = jnp.cos(ang)[None, :, None, :]
    sin = jnp.sin(ang)[None, :, None, :]
    x32 = x.astype(jnp.float32)
    x1, x2 = x32[..., :half], x32[..., half:]
    return jnp.concatenate([x1 * cos - x2 * sin, x1 * sin + x2 * cos], axis=-1)


def retention_direction(q, k, v, log_g, strict):
    bsz, length, h, dk = q.shape
    dv = v.shape[-1]
    nc = length // RET_CHUNK

    def chunks(t):
        return t.reshape(bsz, nc, RET_CHUNK, h, t.shape[-1]).transpose(1, 0, 3, 2, 4)

    i = jnp.arange(RET_CHUNK, dtype=jnp.float32)
    diff = i[:, None] - i[None, :]
    mask = (diff > 0) if strict else (diff >= 0)
    intra = jnp.where(mask[None], jnp.exp(log_g[:, None, None] * jnp.maximum(diff, 0.0)[None]), 0.0)
    q_dec = jnp.exp(log_g[:, None] * (i + 1.0)[None])
    k_dec = jnp.exp(log_g[:, None] * (RET_CHUNK - 1.0 - i)[None])
    c_dec = jnp.exp(log_g * RET_CHUNK)

    def step(state, xs):
        qc, kc, vc = xs
        s = jnp.einsum('bhid,bhjd->bhij', qc, kc) * intra[None]
        y = (jnp.einsum('bhij,bhje->bhie', s, vc)
             + jnp.einsum('bhid,bhde->bhie', qc, state) * q_dec[None, :, :, None])
        state = (state * c_dec[None, :, None, None]
                 + jnp.einsum('bhjd,bhje->bhde', kc * k_dec[None, :, :, None], vc))
        return state, y

    s0 = jnp.zeros((bsz, h, dk, dv), jnp.float32)
    _, y = lax.scan(step, s0, (chunks(q), chunks(k), chunks(v)))
    return y.transpose(1, 0, 3, 2, 4).reshape(bsz, length, h, dv)


def retention(x, w_qkvg, decay_f, decay_b, gn_w, w_o):
    bsz, length, _ = x.shape
    proj = x @ w_qkvg
    q, k, v, g = jnp.split(proj, [D_MODEL, 2 * D_MODEL, 2 * D_MODEL + RET_V_WIDTH], axis=-1)
    pos = jnp.arange(length)
    q = rotary(q.reshape(bsz, length, RET_HEADS, RET_QK_DIM), pos)
    k = rotary(k.reshape(bsz, length, RET_HEADS, RET_QK_DIM), pos) * (RET_QK_DIM ** -0.5)
    v = v.reshape(bsz, length, RET_HEADS, RET_V_DIM).astype(jnp.float32)
    log_gf = -jnp.exp(decay_f.astype(jnp.float32))
    log_gb = -jnp.exp(decay_b.astype(jnp.float32))
    y_f = retention_direction(q, k, v, log_gf, False)
    y_b = jnp.flip(retention_direction(jnp.flip(q, 1), jnp.flip(k, 1), jnp.flip(v, 1), log_gb, True), 1)
    y = y_f + y_b
    mean = jnp.mean(y, axis=-1, keepdims=True)
    var = jnp.mean(jnp.square(y - mean), axis=-1, keepdims=True)
    y = (y - mean) * lax.rsqrt(var + GN_EPS) * gn_w.astype(jnp.float32).reshape(RET_HEADS, RET_V_DIM)
    y = (jax.nn.silu(g.astype(jnp.float32)) * y.reshape(bsz, length, RET_V_WIDTH)).astype(x.dtype)
    return y @ w_o


def sq_relu_mlp(x, w_in, w_out):
    h = jnp.square(jax.nn.relu(x @ w_in))
    return h @ w_out


def encoder_trunk(x, norm_mix, na_w_qkv, na_rpb, na_w_o, ret_w_qkvg, ret_decay_fwd, ret_decay_bwd,
                  ret_gn_w, ret_w_o, norm_mlp, mlp_w_in, mlp_w_out, norm_final):
    for layer in range(DEPTH):
        h = rms_norm(x, norm_mix[layer])
        j = layer // N_MIXERS
        if layer % N_MIXERS == 0:
            h = neighbourhood_attention(h, na_w_qkv[j], na_rpb[j], na_w_o[j])
        else:
            h = retention(h, ret_w_qkvg[j], ret_decay_fwd[j], ret_decay_bwd[j], ret_gn_w[j], ret_w_o[j])
        x = x + h
        x = x + sq_relu_mlp(rms_norm(x, norm_mlp[layer]), mlp_w_in[layer], mlp_w_out[layer])
    return rms_norm(x, norm_final)


def setup_inputs(seed: int = 0) -> dict:
    key = jax.random.key(seed)
    ks = jax.random.split(key, 16)
    f32 = jnp.float32

    def nrm(k, shape, scale):
        return jax.random.normal(k, shape, f32) * scale

    gamma0 = 1.0 - 2.0 ** (-5.0 - np.arange(RET_HEADS, dtype=np.float32))
    raw0 = jnp.asarray(np.log(-np.log(gamma0)).astype(np.float32))
    return {
        "x_prompt": nrm(ks[0], (BATCH, SEQ, D_MODEL), 1.0),
        "x_sample": nrm(ks[1], (DEC_BATCH, DEC_SEQ, D_MODEL), 1.0),
        "norm_mix": 1.0 + nrm(ks[2], (DEPTH, D_MODEL), 0.01),
        "na_w_qkv": nrm(ks[3], (N_NA_LAYERS, D_MODEL, 3 * D_MODEL), D_MODEL ** -0.5),
        "na_rpb": nrm(ks[4], (N_NA_LAYERS, NA_HEADS, 2 * NA_WIN_ROWS - 1, 2 * NA_WIN_COLS - 1), 0.02),
        "na_w_o": nrm(ks[5], (N_NA_LAYERS, D_MODEL, D_MODEL), D_MODEL ** -0.5),
        "ret_w_qkvg": nrm(ks[6], (N_RET_LAYERS, D_MODEL, 2 * D_MODEL + 2 * RET_V_WIDTH), D_MODEL ** -0.5),
        "ret_decay_fwd": raw0[None, :] + nrm(ks[7], (N_RET_LAYERS, RET_HEADS), 0.05),
        "ret_decay_bwd": raw0[None, :] + nrm(ks[8], (N_RET_LAYERS, RET_HEADS), 0.05),
        "ret_gn_w": 1.0 + nrm(ks[9], (N_RET_LAYERS, RET_V_WIDTH), 0.01),
        "ret_w_o": nrm(ks[10], (N_RET_LAYERS, RET_V_WIDTH, D_MODEL), RET_V_WIDTH ** -0.5),
        "norm_mlp": 1.0 + nrm(ks[11], (DEPTH, D_MODEL), 0.01),
        "mlp_w_in": nrm(ks[12], (DEPTH, D_MODEL, D_FF), D_MODEL ** -0.5),
        "mlp_w_out": nrm(ks[13], (DEPTH, D_FF, D_MODEL), D_FF ** -0.5),
        "norm_final": 1.0 + nrm(ks[14], (D_MODEL,), 0.01),
    }


def reference(x_prompt, x_sample, norm_mix, na_w_qkv, na_rpb, na_w_o, ret_w_qkvg, ret_decay_fwd,
              ret_decay_bwd, ret_gn_w, ret_w_o, norm_mlp, mlp_w_in, mlp_w_out, norm_final):
    y_prompt = encoder_trunk(x_prompt, norm_mix, na_w_qkv, na_rpb, na_w_o, ret_w_qkvg, ret_decay_fwd,
                             ret_decay_bwd, ret_gn_w, ret_w_o, norm_mlp, mlp_w_in, mlp_w_out, norm_final)
    y_sample = encoder_trunk(x_sample, norm_mix, na_w_qkv, na_rpb, na_w_o, ret_w_qkvg, ret_decay_fwd,
                             ret_decay_bwd, ret_gn_w, ret_w_o, norm_mlp, mlp_w_in, mlp_w_out, norm_final)
    return (y_prompt, y_sample)
```

```python
import math
from contextlib import ExitStack

import numpy as np
import concourse.bass as bass
import concourse.mybir as mybir
from concourse.bass_utils import run_bass_kernel_spmd

F32 = mybir.dt.float32
BF16 = mybir.dt.bfloat16
AF = mybir.ActivationFunctionType
ALU = mybir.AluOpType

D = 2048
NTOK = 8192
import os
NBLK = int(os.environ.get('KB_NBLK', '16'))
KB = os.environ.get('KB', '')
DFF = 8192
NCORES = 8
NORM_EPS = 1e-6
GN_EPS = 1e-5
NA_SCALE = 32 ** -0.5
RET_SCALE = 256 ** -0.5

C_DPOS, C_DNEG, C_MPOS, C_MNEG, C_ROW1, C_ROW2, C_K1, C_K2, C_128, C_CMASK = 0, 128, 256, 384, 512, 640, 768, 769, 770, 771
NCTAB = 771 + 64


class EngW:
    def __init__(self, name, e):
        self.name = name
        self.e = e
        self.sem = None
        self.count = 0
        self.nseq = 0
        self.sigs = []
        self.known = {}
        self.last_ins = None
        self.last_sig = True

    def signal_for(self, seq):
        lo, hi = 0, len(self.sigs)
        while lo < hi:
            mid = (lo + hi) // 2
            if self.sigs[mid][0] >= seq:
                hi = mid
            else:
                lo = mid + 1
        if lo < len(self.sigs):
            return self.sigs[lo][1]
        assert self.last_ins is not None and not self.last_sig, (self.name, seq, self.nseq)
        self.count += 1
        self.last_ins.then_inc(self.sem, 1)
        self.last_sig = True
        self.sigs.append((self.nseq - 1, self.count))
        return self.count


class Buf:
    __slots__ = ("name", "last_w", "reads", "dsem", "dval")

    def __init__(self, name):
        self.name = name
        self.last_w = None
        self.reads = []
        self.dsem = None
        self.dval = 0


class FW:
    def __init__(self, nc, stack):
        self.nc = nc
        self.stack = stack
        self.engs = []
        for name, e in (("pe", nc.tensor), ("act", nc.scalar), ("dve", nc.vector),
                        ("pool", nc.gpsimd), ("sp", nc.sync)):
            w = EngW(name, e)
            w.sem = stack.enter_context(nc.semaphore("sem_" + name))
            setattr(self, name, w)
            self.engs.append(w)
        self.nwaits = 0
        self.swq = []
        self.swq_total = 0
        self.bufs = {}
        self.dsems = []
        self.free_dsems = []

    def buf(self, name):
        if name not in self.bufs:
            self.bufs[name] = Buf(name)
        return self.bufs[name]

    def _resolve(self, acc):
        if acc[0] == 'e':
            return acc[1].sem, acc[1].signal_for(acc[2])
        return acc[1], acc[2]

    def _wait_deps(self, w, reads, writes, same_engine=False):
        deps = []
        for b in reads:
            if b.last_w is not None:
                deps.append((b.last_w, True))
        for b in writes:
            if b.last_w is not None:
                deps.append((b.last_w, False))
            for r in b.reads:
                deps.append((r, False))
        need = {}
        for acc, is_raw in deps:
            if acc[0] == 'e' and acc[1] is w and not same_engine:
                if not is_raw or w.name == "pe":
                    continue
            sem, val = self._resolve(acc)
            k = id(sem)
            if k not in need or need[k][1] < val:
                need[k] = (sem, val)
        for k, (sem, val) in need.items():
            if w.known.get(k, 0) >= val:
                continue
            w.e.wait_ge(sem, val)
            w.known[k] = val
            self.nwaits += 1

    def _record(self, acc, reads, writes):
        for b in writes:
            b.last_w = acc
            b.reads = []
        for b in reads:
            if acc[0] == 'e':
                b.reads = [r for r in b.reads if not (r[0] == 'e' and r[1] is acc[1])]
            else:
                b.reads = [r for r in b.reads if not (r[0] == 'd' and r[1] is acc[1])]
            b.reads.append(acc)

    def op(self, w, fn, reads=(), writes=()):
        self._wait_deps(w, reads, writes)
        ins = fn()
        w.last_ins = ins
        w.last_sig = False
        seq = w.nseq
        w.nseq += 1
        self._record(('e', w, seq), reads, writes)
        return ins

    def dma(self, q, out, in_, owner, reads=(), writes=(), ndesc=128, **kw):
        if (not q.last_sig) and q.last_ins is not None:
            q.signal_for(q.nseq - 1)
        self._wait_deps(q, reads, writes, same_engine=True)
        if owner.dsem is None:
            owner.dsem = self.stack.enter_context(self.nc.semaphore("ds_%s" % owner.name))
            self.dsems.append(owner)
        if q.name == "pool":
            per_eng = ndesc // 16 + 2
            while self.swq and self.swq_total + per_eng > 600:
                sem, val, n = self.swq.pop(0)
                self.swq_total -= n
                k = id(sem)
                if q.known.get(k, 0) < val:
                    q.e.wait_ge(sem, val)
                    q.known[k] = val
                    self.nwaits += 1
        owner.dval += 16
        ins = q.e.dma_start(out=out, in_=in_, **kw).then_inc(owner.dsem, 16)
        if q.name == "pool":
            self.swq.append((owner.dsem, owner.dval, per_eng))
            self.swq_total += per_eng
        q.last_ins = None
        q.last_sig = True
        q.nseq += 1
        self._record(('d', owner.dsem, owner.dval), reads, writes)
        return ins

    def barrier(self):
        for w in self.engs:
            if (not w.last_sig) and w.last_ins is not None:
                w.signal_for(w.nseq - 1)
        for w in self.engs:
            for b in self.dsems:
                k = id(b.dsem)
                if b.dval > 0 and w.known.get(k, 0) < b.dval:
                    w.e.wait_ge(b.dsem, b.dval)
                    w.known[k] = b.dval
                    self.nwaits += 1
            for w2 in self.engs:
                if w2 is w or w2.count == 0:
                    continue
                k = id(w2.sem)
                if w.known.get(k, 0) < w2.count:
                    w.e.wait_ge(w2.sem, w2.count)
                    w.known[k] = w2.count
                    self.nwaits += 1


class Ring:
    def __init__(self, items):
        self.items = items
        self.i = 0

    def next(self):
        it = self.items[self.i % len(self.items)]
        self.i += 1
        return it


def _rs(r, variant):
    if variant == "joined":
        return min(max(r - 4, 0), 120)
    if r < 64:
        return min(max(r - 4, 0), 56)
    return 64 + min(max(r - 64 - 4, 0), 56)


def _pair_entries(s, tiles, variant):
    out = []
    for j, t in enumerate(tiles):
        if t is None:
            continue
        for a in range(2):
            kr = 2 * t + a
            for b in range(2):
                r = 2 * s + b
                rs = _rs(r, variant)
                if rs <= kr <= rs + 7:
                    rr = kr - r + 7
                    assert 0 <= rr <= 14
                    out.append((j, a, b, rr))
    return out


SPECIAL = [0, 1, 30, 31, 32, 33, 62, 63]


def _tiles_for(s):
    if s in SPECIAL:
        return [t if 0 <= t <= 63 else None for t in range(s - 3, s + 4)]
    return list(range(s - 2, s + 3))


def build_program(phases="ABCDE", dbg=False):
    nc = bass.Bass("TRN2", target_bir_lowering=False)

    def din(name, shape, dt=F32):
        return nc.dram_tensor(name, list(shape), dt, kind="ExternalInput").ap()

    def dscr(name, shape, dt):
        return nc.dram_tensor(name, list(shape), dt, kind="Internal").ap()

    x_in = din("x", [NTOK, D])
    w_qkv = din("w_qkv", [D, 3 * D])
    w_o = din("w_o", [D, D])
    w_in0 = din("w_in0", [D, DFF])
    w_out0 = din("w_out0", [DFF, D])
    w_r = din("w_r", [D, 6 * D])
    w_ro = din("w_ro", [2 * D, D])
    w_in1 = din("w_in1", [D, DFF])
    w_out1 = din("w_out1", [DFF, D])
    nrm_col_d = din("nrm_col", [128, 64])
    nrm_fin_d = din("nrm_fin", [128, D])
    gnw_col_d = din("gnw_col", [128, 32])
    rpbf_d = din("rpbf", [64, 465])
    dec_d = din("dec", [128, 16])
    ctab_d = din("ctab", [128, NCTAB])
    cos_d = din("cosT", [128, NTOK])
    sin_d = din("sinT", [128, NTOK])
    flag_d = din("flag", [128, 1])
    y_out = nc.dram_tensor("y", [NTOK, D], F32, kind="ExternalOutput").ap()

    wb_qkv = nc.dram_tensor("wb_qkv", [D, 3 * D], BF16, kind="Internal").ap()
    wb_o = nc.dram_tensor("wb_o", [D, D], BF16, kind="Internal").ap()
    wb_in0 = nc.dram_tensor("wb_in0", [D, DFF], BF16, kind="Internal").ap()
    wb_out0 = nc.dram_tensor("wb_out0", [DFF, D], BF16, kind="Internal").ap()
    wb_r = nc.dram_tensor("wb_r", [D, 6 * D], BF16, kind="Internal").ap()
    wb_ro = nc.dram_tensor("wb_ro", [2 * D, D], BF16, kind="Internal").ap()
    wb_in1 = nc.dram_tensor("wb_in1", [D, DFF], BF16, kind="Internal").ap()
    wb_out1 = nc.dram_tensor("wb_out1", [DFF, D], BF16, kind="Internal").ap()

    s_qT = dscr("s_qT", [D, NTOK], BF16)
    s_kT = dscr("s_kT", [D, NTOK], BF16)
    s_v = dscr("s_v", [NTOK, D], BF16)
    s_o = dscr("s_o", [NTOK, D], BF16)
    s_x2 = dscr("s_x2", [NTOK, D], F32)
    s_rqT = dscr("s_rqT", [D, NTOK], BF16)
    s_rkT = dscr("s_rkT", [D, NTOK], BF16)
    s_rv = dscr("s_rv", [NTOK, 2 * D], BF16)
    s_rg = dscr("s_rg", [NTOK, 2 * D], BF16)
    s_yb = dscr("s_yb", [NTOK, 1024], F32)
    s_yn = dscr("s_yn", [NTOK, 2 * D], BF16)
    s_erp = nc.dram_tensor("s_erp", [1024, 127], BF16, kind="Internal").ap()
    s_G = nc.dram_tensor("s_G", [64, 1024, 64], BF16, kind="Internal").ap()

    with ExitStack() as st:
        fw = FW(nc, st)
        pe, act, dve, pool, sp = fw.pe, fw.act, fw.dve, fw.pool, fw.sp

        uid = [0]

        def sb(name, shape, dt, stack=st):
            uid[0] += 1
            return stack.enter_context(nc.sbuf_tensor("%s_%d" % (name, uid[0]), list(shape), dt))

        def ps(name, shape, dt, stack=st):
            uid[0] += 1
            return stack.enter_context(nc.psum_tensor("%s_%d" % (name, uid[0]), list(shape), dt))

        ident = sb("ident", [128, 128], BF16)
        identf = sb("identf", [128, 128], F32)
        nrm_col = sb("nrm_col_s", [128, 64], F32)
        gnw_col = sb("gnw_col_s", [128, 32], F32)
        flag = sb("flag_s", [128, 1], F32)
        ctab = sb("ctab_s", [128, NCTAB], F32)
        epsn = sb("epsn", [128, 1], F32)
        epsg = sb("epsg", [128, 1], F32)
        Bc = fw.buf("consts")
        fw.op(pool, lambda: nc.gpsimd.memset(identf[:], 0.0), writes=[Bc])
        fw.op(pool, lambda: nc.gpsimd.affine_select(out=identf[:], in_=identf[:], pattern=[[-1, 128]],
                                                     compare_op=ALU.not_equal, fill=1.0, base=0,
                                                     channel_multiplier=1), reads=[Bc], writes=[Bc])
        fw.op(pool, lambda: nc.gpsimd.memset(epsn[:], NORM_EPS), writes=[Bc])
        fw.op(pool, lambda: nc.gpsimd.memset(epsg[:], GN_EPS), writes=[Bc])
        fw.op(dve, lambda: nc.vector.tensor_copy(out=ident[:], in_=identf[:]), reads=[Bc], writes=[Bc])
        Bl = fw.buf("cload")
        for t_s, t_d in ((nrm_col, nrm_col_d), (gnw_col, gnw_col_d), (flag, flag_d), (ctab, ctab_d)):
            fw.dma(sp, t_s[:], t_d, owner=Bl, writes=[Bc])

        Bw = {}

        def cast_weight(key, src, dst):
            b = fw.buf("wc_" + key)
            Bw[key] = b
            if 'nocast2' in KB and key != 'qkv':
                return
            K, N = src.shape
            for k0 in range(0, K, 2048):
                for n0 in range(0, N, 2048):
                    fw.dma(pool, dst[k0:k0 + 2048, n0:n0 + 2048], src[k0:k0 + 2048, n0:n0 + 2048],
                           owner=b, writes=[b], ndesc=4096)

        wring = Ring([])
        pm = Ring([])
        ptr = Ring([])
        stg = Ring([])
        u2T = Ring([])
        rtmp = Ring([])
        xt = []
        hb = []
        TA = [None, None]
        TB = [None, None]
        S = {}
        Bjunk = fw.buf("junk")
        Bss = fw.buf("ss")

        def open_common(full, need_stg=True, nptr=4):
            cs_ = ExitStack()
            pm.items = [(ps("pm%d" % i, [128, 512], F32, cs_), fw.buf("pm%d" % i)) for i in range(4)]
            ptr_t = [ps("ptr%d" % i, [128, 512], BF16, cs_) for i in range(nptr)]
            ptr.items = [(ptr_t[i][:, :], fw.buf("ptr%d" % i)) for i in range(nptr)]
            S["junk"] = sb("junk", [128, D], BF16, cs_)
            if full:
                wring.items = [(sb("wt%d" % i, [128, 16, 512], BF16, cs_), fw.buf("wt%d" % i)) for i in range(3)]
                xt[:] = [(sb("xt%d" % i, [128, D], F32, cs_), fw.buf("xt%d" % i)) for i in range(4)]
                hb[:] = [(sb("hb%d" % i, [128, D], BF16, cs_), fw.buf("hb%d" % i)) for i in range(4)]
                S["ss"] = sb("ss", [128, 4], F32, cs_)
                S["rs"] = sb("rs", [128, 4], F32, cs_)
                S["rstd"] = sb("rstd", [128, 4], F32, cs_)
                TA[0], TA[1] = sb("TA", [128, 16, 512], BF16, cs_), fw.buf("TA")
                TB[0], TB[1] = sb("TB", [128, 16, 512], BF16, cs_), fw.buf("TB")
                if need_stg:
                    stg.items = [(sb("stg%d" % i, [128, 512], BF16, cs_), fw.buf("stg%d" % i)) for i in range(6)]
                u2T.items = [(sb("u2T%d" % i, [128, 16, 512], BF16, cs_), fw.buf("u2T%d" % i)) for i in range(2)]
                rtmp.items = [(sb("rtmp%d" % i, [128, 512], F32, cs_), fw.buf("rtmp%d" % i)) for i in range(2)]
            return cs_

        def load_w(key, wdram, k0, n0):
            t, b = wring.next()
            fw.dma(sp, t[:], wdram[k0:k0 + 2048, n0:n0 + 512].rearrange("(kc p) n -> p kc n", p=128),
                   owner=b, reads=[Bw[key]], writes=[b])
            return t, b

        evac_ctr = [0]

        def evac(out, in_, reads, writes):
            evac_ctr[0] += 1
            if evac_ctr[0] % 2:
                fw.op(act, lambda: nc.scalar.activation(out=out, in_=in_, func=AF.Copy), reads=reads, writes=writes)
            else:
                fw.op(dve, lambda: nc.vector.tensor_copy(out=out, in_=in_), reads=reads, writes=writes)

        def rms_rstd(src_tiles, eps_t, scale):
            for i in range(4):
                t, b = src_tiles[i]
                fw.op(act, lambda: nc.scalar.activation(out=S["junk"][:], in_=t[:], func=AF.Square,
                                                        accum_out=S["ss"][:, i:i + 1]),
                      reads=[b], writes=[Bjunk, Bss])
            fw.op(act, lambda: nc.scalar.activation(out=S["junk"][:, 0:8], in_=epsn[:].to_broadcast([128, 8]), func=AF.Copy),
                  reads=[Bc], writes=[Bjunk, Bss])
            fw.op(act, lambda: nc.scalar.activation(out=S["rs"][:], in_=S["ss"][:], func=AF.Sqrt, scale=scale,
                                                    bias=eps_t[:, 0:1]), reads=[Bss], writes=[Bss])
            fw.op(dve, lambda: nc.vector.reciprocal(out=S["rstd"][:], in_=S["rs"][:]), reads=[Bss], writes=[Bss])

        def transpose_to(dst, src_tiles, nchunks, colscale, col0):
            dT, dB = dst
            for kc in range(1 if 'onekc' in KB else nchunks):
                pt, pb = ptr.next()
                for i in range(4):
                    t, b = src_tiles[i]
                    fw.op(pe, lambda: nc.tensor.transpose(out=pt[:, i * 128:(i + 1) * 128],
                                                          in_=t[:, kc * 128:(kc + 1) * 128], identity=ident[:]),
                          reads=[b, Bc], writes=[pb])
                if 'noevac' in KB:
                    continue
                if colscale is None:
                    evac(dT[:, kc, :], pt, [pb], [dB])
                elif (kc % 2 or 'actonly' in KB) and 'dveonly' not in KB:
                    fw.op(act, lambda: nc.scalar.activation(out=dT[:, kc, :], in_=pt, func=(AF.Identity if 'ident' in KB else AF.Copy),
                                                            scale=colscale[:, col0 + kc:col0 + kc + 1]),
                          reads=[pb, Bc], writes=[dB])
                else:
                    fw.op(dve, lambda: nc.vector.tensor_scalar(out=dT[:, kc, :], in0=pt,
                                                               scalar1=colscale[:, col0 + kc:col0 + kc + 1],
                                                               scalar2=None, op0=ALU.mult),
                          reads=[pb, Bc], writes=[dB])

        def norm_T(dst, ncol):
            rms_rstd(xt, epsn, 1.0 / D)
            for i in range(4):
                fw.op(dve, lambda: nc.vector.tensor_scalar(out=hb[i][0][:], in0=xt[i][0][:],
                                                           scalar1=S["rstd"][:, i:i + 1], scalar2=None, op0=ALU.mult),
                      reads=[xt[i][1], Bss], writes=[hb[i][1]])
            transpose_to(dst, hb, 16, None if 'noscale' in KB else nrm_col, ncol * 16)

        def load_x(src, blk):
            for i in range(4):
                r0 = blk * 512 + i * 128
                fw.dma(sp, xt[i][0][:], src[r0:r0 + 128, :], owner=xt[i][1], writes=[xt[i][1]])


        def proj_featmajor(src, key, wdram, col_tiles, dsts, blk, post=None):
            sT, sB = src
            for wt in col_tiles:
                wtile, wbuf = load_w(key, wdram, 0, wt * 512)
                for oc in range(4):
                    pt, pb = pm.next()
                    for kc in range(16):
                        fw.op(pe, lambda: nc.tensor.matmul(pt[:], lhsT=wtile[:, kc, oc * 128:(oc + 1) * 128],
                                                           rhs=sT[:, kc, :], start=(kc == 0), stop=(kc == 15)),
                              reads=[wbuf, sB], writes=[pb])
                    post(wt, oc, pt, pb)

        def mlp(src, key_in, wdin, key_out, wdout):
            sT, sB = src
            for part in range(4):
                uT, uB = u2T.next()
                for wq in range(4):
                    wtile, wbuf = load_w(key_in, wdin, 0, part * 2048 + wq * 512)
                    for oc in range(4):
                        pt, pb = pm.next()
                        for kc in range(16):
                            fw.op(pe, lambda: nc.tensor.matmul(pt[:], lhsT=wtile[:, kc, oc * 128:(oc + 1) * 128],
                                                               rhs=sT[:, kc, :], start=(kc == 0), stop=(kc == 15)),
                                  reads=[wbuf, sB], writes=[pb])
                        rt, rb = rtmp.next()
                        fw.op(act, lambda: nc.scalar.activation(out=rt[:], in_=pt[:], func=AF.Relu),
                              reads=[pb], writes=[rb])
                        fw.op(pool, lambda: nc.gpsimd.tensor_tensor(out=uT[:, wq * 4 + oc, :], in0=rt[:], in1=rt[:],
                                                                    op=ALU.mult), reads=[rb], writes=[uB])
                for n in range(4):
                    wtile, wbuf = load_w(key_out, wdout, part * 2048, n * 512)
                    for i in range(4):
                        pt, pb = pm.next()
                        for kc in range(16):
                            fw.op(pe, lambda: nc.tensor.matmul(pt[:], lhsT=uT[:, kc, i * 128:(i + 1) * 128],
                                                               rhs=wtile[:, kc, :], start=(kc == 0), stop=(kc == 15)),
                                  reads=[wbuf, uB], writes=[pb])
                        fw.op(dve, lambda: nc.vector.tensor_tensor(out=xt[i][0][:, n * 512:(n + 1) * 512],
                                                                   in0=xt[i][0][:, n * 512:(n + 1) * 512],
                                                                   in1=pt[:], op=ALU.add),
                              reads=[pb, xt[i][1]], writes=[xt[i][1]])

        def proj_tokmajor_add(src, nk, key, wdram):
            sT_list = src
            for n in range(4):
                for kg in range(nk):
                    wtile, wbuf = load_w(key, wdram, kg * 2048, n * 512)
                    sT, sB = sT_list[kg]
                    for i in range(4):
                        pt, pb = pm.next()
                        for kc in range(16):
                            fw.op(pe, lambda: nc.tensor.matmul(pt[:], lhsT=sT[:, kc, i * 128:(i + 1) * 128],
                                                               rhs=wtile[:, kc, :], start=(kc == 0), stop=(kc == 15)),
                                  reads=[wbuf, sB], writes=[pb])
                        fw.op(dve, lambda: nc.vector.tensor_tensor(out=xt[i][0][:, n * 512:(n + 1) * 512],
                                                                   in0=xt[i][0][:, n * 512:(n + 1) * 512],
                                                                   in1=pt[:], op=ALU.add),
                              reads=[pb, xt[i][1]], writes=[xt[i][1]])

        cast_weight("qkv", w_qkv, wb_qkv)
        if "A" in phases:
            cmn = open_common(True)
            for blk in range(NBLK):
                t0 = blk * 512
                load_x(x_in, blk)
                if 'a1' in KB:
                    continue
                if 'a2' in KB:
                    rms_rstd(xt, epsn, 1.0 / D)
                    if 'a2b' in KB:
                        for i in range(4):
                            fw.op(dve, lambda: nc.vector.tensor_scalar(out=hb[i][0][:], in0=xt[i][0][:],
                                                                       scalar1=S["rstd"][:, i:i + 1], scalar2=None, op0=ALU.mult),
                                  reads=[xt[i][1], Bss], writes=[hb[i][1]])
                    continue
                norm_T(TA, 0)
                if 'a3' in KB:
                    continue

                def post_qk(wt, oc, pt, pb):
                    stt, stb = stg.next()
                    evac(stt[:], pt[:], [pb], [stb])
                    dst = s_qT if wt < 4 else s_kT
                    f0 = (wt % 4) * 512 + oc * 128
                    fw.dma(pool, dst[f0:f0 + 128, t0:t0 + 512], stt[:], owner=stb, reads=[stb])

                proj_featmajor(TA, "qkv", wb_qkv, range(8), None, blk, post=post_qk)
                for wt in range(8, 12):
                    wtile, wbuf = load_w("qkv", wb_qkv, 0, wt * 512)
                    for i in range(4):
                        pt, pb = pm.next()
                        for kc in range(16):
                            fw.op(pe, lambda: nc.tensor.matmul(pt[:], lhsT=TA[0][:, kc, i * 128:(i + 1) * 128],
                                                               rhs=wtile[:, kc, :], start=(kc == 0), stop=(kc == 15)),
                                  reads=[wbuf, TA[1]], writes=[pb])
                        stt, stb = stg.next()
                        evac(stt[:], pt[:], [pb], [stb])
                        fw.dma(pool, s_v[t0 + i * 128:t0 + (i + 1) * 128, (wt - 8) * 512:(wt - 7) * 512], stt[:],
                               owner=stb, reads=[stb])
            fw.barrier()
            cmn.close()

        cast_weight("o", w_o, wb_o)
        cast_weight("in0", w_in0, wb_in0)
        cast_weight("out0", w_out0, wb_out0)
        cast_weight("r", w_r, wb_r)

        if "B" in phases:
            with ExitStack() as sB_:
                rp = sb("rp", [64, 465], F32, sB_)
                rpe = sb("rpe", [64, 15, 31], BF16, sB_)
                zt = sb("zt", [128, 8 * 127], BF16, sB_)
                Brp = fw.buf("rp")
                Bz = fw.buf("zt")
                Berp = fw.buf("erp")
                BG = fw.buf("G")
                fw.dma(sp, rp[:], rpbf_d, owner=Brp, writes=[Brp])
                fw.op(act, lambda: nc.scalar.activation(out=rpe[:].rearrange("p a b -> p (a b)"), in_=rp[:], func=AF.Exp),
                      reads=[Brp], writes=[Brp])
                fw.op(dve, lambda: nc.vector.memset(zt[:], 0.0), writes=[Bz])
                fw.dma(sp, s_erp.rearrange("(p j) c -> p (j c)", p=128), zt[:], owner=Bz, reads=[Bz], writes=[Berp])
                fw.dma(sp, s_erp.rearrange("(h r) c -> h r c", r=16)[:, 0:15, 48:79], rpe[:], owner=Brp,
                       reads=[Brp, Berp], writes=[Berp])
                for kc in range(64):
                    fw.dma(sp, s_G[kc], s_erp[:, 63 - kc:127 - kc], owner=BG, reads=[Berp], writes=[BG])

                qh = Ring([(sb("qh%d" % i, [32, NTOK], BF16, sB_), fw.buf("qh%d" % i)) for i in range(2)])
                kh = Ring([(sb("kh%d" % i, [32, NTOK], BF16, sB_), fw.buf("kh%d" % i)) for i in range(2)])
                vraw = (sb("vraw", [128, 64, 128], BF16, sB_), fw.buf("vraw"))
                vaug = Ring([(sb("vaug%d" % i, [128, 64, 4, 34], BF16, sB_), fw.buf("vaug%d" % i)) for i in range(2)])
                G2 = Ring([(sb("G2_%d" % i, [128, 16, 64], BF16, sB_), fw.buf("G2_%d" % i)) for i in range(2)])
                Eint = Ring([(sb("Eint%d" % i, [128, 5, 128], BF16, sB_), fw.buf("Eint%d" % i)) for i in range(2)])
                Esp = Ring([(sb("Esp%d" % i, [128, 8, 7, 128], BF16, sB_), [fw.buf("Esp%d_%d" % (i, k)) for k in range(8)])
                            for i in range(2)])
                Etmp = Ring([(sb("Etmp%d" % i, [128, 7, 128], BF16, sB_), fw.buf("Etmp%d" % i)) for i in range(2)])
                exr = Ring([(sb("ex%d" % i, [128, 896], BF16, sB_), fw.buf("ex%d" % i)) for i in range(4)])
                pTr = Ring([(sb("pT%d" % i, [128, 896], BF16, sB_), fw.buf("pT%d" % i)) for i in range(4)])
                ostg = Ring([(sb("ostg%d" % i, [128, 64, 128], BF16, sB_), fw.buf("ostg%d" % i)) for i in range(2)])
                rec = Ring([(sb("rec%d" % i, [128, 8], F32, sB_), fw.buf("rec%d" % i)) for i in range(2)])
                psc_t = [ps("psc%d" % i, [128, 1024], F32, sB_) for i in range(3)]
                psc = Ring([(psc_t[i], fw.buf("psc%d" % i)) for i in range(3)])
                po = Ring([(ps("po%d" % i, [128, 8, 64], F32, sB_), fw.buf("po%d" % i)) for i in range(2)])
                cm_b = ctab[:, C_CMASK:C_CMASK + 64].unsqueeze(1).to_broadcast([128, 16, 64])
                for t_, b_ in vaug.items:
                    fw.op(pool, lambda: nc.gpsimd.memset(t_[:], 1.0), writes=[b_])

                def build_E(eng, dstT, dstB, entries, g2t, g2b, ops):
                    byjb = {}
                    for (j, a, b, rr) in entries:
                        byjb.setdefault((j, b), {})[a] = rr
                    for (j, b), d_ in sorted(byjb.items()):
                        if 0 in d_ and 1 in d_:
                            assert d_[1] == d_[0] + 1
                            o_ap = dstT[:, j, b * 64:(b + 1) * 64]
                            i_ap = g2t[:, d_[0], :]
                        elif 0 in d_:
                            o_ap = dstT[0:64, j, b * 64:(b + 1) * 64]
                            i_ap = g2t[0:64, d_[0], :]
                        else:
                            assert d_[1] >= 1
                            o_ap = dstT[64:128, j, b * 64:(b + 1) * 64]
                            i_ap = g2t[64:128, d_[1] - 1, :]
                        if eng is act:
                            ops.append(lambda o_ap=o_ap, i_ap=i_ap: fw.op(
                                act, lambda: nc.scalar.activation(out=o_ap, in_=i_ap, func=AF.Copy), reads=[g2b], writes=[dstB]))
                        else:
                            ops.append(lambda o_ap=o_ap, i_ap=i_ap: fw.op(
                                dve, lambda: nc.vector.tensor_copy(out=o_ap, in_=i_ap), reads=[g2b], writes=[dstB]))

                vstate = {}

                def prep_head(h):
                    g, hh = divmod(h, 4)
                    c = {}
                    ops = []
                    if hh == 0:
                        fw.dma(sp, vraw[0][:], s_v[:, g * 128:(g + 1) * 128].rearrange("(t p) d -> p t d", p=128),
                               owner=vraw[1], writes=[vraw[1]])
                        va, vab = vaug.next()
                        fw.op(pool, lambda: nc.gpsimd.tensor_copy(
                            out=va[:, :, :, 0:32], in_=vraw[0][:].rearrange("p t (h d) -> p t h d", h=4)),
                            reads=[vraw[1]], writes=[vab])
                        vstate["va"] = (va, vab)
                        vstate["og"] = ostg.next()
                    c["va"], c["vab"] = vstate["va"]
                    c["og"], c["ogb"] = vstate["og"]
                    qt_, qb_ = qh.next()
                    kt_, kb_ = kh.next()
                    fw.dma(sp, qt_[:], s_qT[h * 32:(h + 1) * 32, :], owner=qb_, writes=[qb_])
                    fw.dma(sp, kt_[:], s_kT[h * 32:(h + 1) * 32, :], owner=kb_, writes=[kb_])
                    c["q"], c["qb"], c["k"], c["kb"] = qt_, qb_, kt_, kb_
                    g2t, g2b = G2.next()
                    fw.dma(sp, g2t[0:64, :, :], s_G[:, h * 16:(h + 1) * 16, :], owner=g2b, reads=[BG], writes=[g2b])
                    fw.dma(sp, g2t[64:128, 0:15, :], s_G[:, h * 16 + 1:(h + 1) * 16, :], owner=g2b, reads=[BG], writes=[g2b])
                    fw.op(pool, lambda: nc.gpsimd.tensor_tensor(out=g2t[:, 0:15, :], in0=g2t[:, 0:15, :], in1=cm_b[:, 0:15, :], op=ALU.mult),
                          reads=[g2b, Bc], writes=[g2b])
                    ei, eib = Eint.next()
                    es, esbs = Esp.next()
                    fw.op(pool, lambda: nc.gpsimd.memset(ei[:], 0.0), writes=[eib])
                    fw.op(pool, lambda: nc.gpsimd.memset(es[:], 0.0), writes=list(esbs))
                    build_E(act, ei[:], eib, _pair_entries(10, _tiles_for(10), "joined"), g2t, g2b, ops)
                    for si, s in enumerate(SPECIAL):
                        tl = _tiles_for(s)
                        e_split = _pair_entries(s, tl, "split")
                        e_join = _pair_entries(s, tl, "joined")
                        if sorted(e_split) == sorted(e_join):
                            build_E(act, es[:, si], esbs[si], e_split, g2t, g2b, ops)
                        else:
                            et, etb = Etmp.next()
                            ops.append(lambda et=et, etb=etb: fw.op(pool, lambda: nc.gpsimd.memset(et[:], 0.0), writes=[etb]))
                            build_E(dve, es[:, si], esbs[si], e_split, g2t, g2b, ops)
                            build_E(act, et[:], etb, e_join, g2t, g2b, ops)
                            ops.append(lambda et=et, etb=etb, si=si: fw.op(
                                dve, lambda: nc.vector.tensor_tensor(out=et[:], in0=et[:], in1=es[:, si], op=ALU.subtract),
                                reads=[esbs[si], etb], writes=[etb]))
                            ops.append(lambda et=et, etb=etb, si=si: fw.op(
                                dve, lambda: nc.vector.scalar_tensor_tensor(out=es[:, si], in0=et[:], scalar=flag[:, 0:1],
                                                                            in1=es[:, si], op0=ALU.mult, op1=ALU.add),
                                reads=[etb, esbs[si], Bc], writes=[esbs[si]]))
                    c["ei"], c["eib"], c["es"], c["esb"] = ei, eib, es, esbs
                    return c, ops

                def scores(c, s):
                    tl = _tiles_for(s)
                    nt = len(tl)
                    if s in SPECIAL:
                        Et = c["es"][:, SPECIAL.index(s)].rearrange("p j q -> p (j q)")
                        Eb = c["esb"][SPECIAL.index(s)]
                    else:
                        Et = c["ei"][:].rearrange("p j q -> p (j q)")
                        Eb = c["eib"]
                    pst, psb = psc.next()
                    for j, t in enumerate(tl):
                        tt = t if t is not None else 0
                        fw.op(pe, lambda: nc.tensor.matmul(pst[:, j * 128:(j + 1) * 128],
                                                           lhsT=c["k"][:, tt * 128:(tt + 1) * 128],
                                                           rhs=c["q"][:, s * 128:(s + 1) * 128], start=True, stop=True),
                              reads=[c["kb"], c["qb"]], writes=[psb])
                    ext, exb = exr.next()
                    fw.op(act, lambda: nc.scalar.activation(out=ext[:, 0:nt * 128], in_=pst[:, 0:nt * 128],
                                                            func=AF.Exp, scale=NA_SCALE),
                          reads=[psb], writes=[exb])
                    ptt, ptb = pTr.next()
                    fw.op(dve, lambda: nc.vector.tensor_tensor(out=ptt[:, 0:nt * 128], in0=ext[:, 0:nt * 128], in1=Et,
                                                               op=ALU.mult), reads=[exb, Eb], writes=[ptb])
                    return ptt, ptb, tl

                pstate = {}

                def pv(c, s, hh, ptt, ptb, tl):
                    slot = s % 8
                    if slot == 0:
                        pstate["po"] = po.next()
                    pot, pob = pstate["po"]
                    js = [j for j, t in enumerate(tl) if t is not None]
                    for j in js:
                        fw.op(pe, lambda: nc.tensor.matmul(pot[:, slot, 0:34], lhsT=ptt[:, j * 128:(j + 1) * 128],
                                                           rhs=c["va"][:, tl[j], hh, :], start=(j == js[0]),
                                                           stop=(j == js[-1])),
                              reads=[ptb, c["vab"]], writes=[pob])
                    if slot == 7:
                        rt_, rb_ = rec.next()
                        fw.op(dve, lambda: nc.vector.reciprocal(out=rt_[:], in_=pot[:, :, 32]), reads=[pob], writes=[rb_])
                        fw.op(dve, lambda: nc.vector.tensor_tensor(
                            out=c["og"][:, s - 7:s + 1, hh * 32:(hh + 1) * 32], in0=pot[:, :, 0:32],
                            in1=rt_[:].unsqueeze(2).to_broadcast([128, 8, 32]), op=ALU.mult),
                            reads=[pob, rb_], writes=[c["ogb"]])

                nxt, ops0 = prep_head(0)
                for o_ in ops0:
                    o_()
                for h in range(64):
                    g, hh = divmod(h, 4)
                    c = nxt
                    pops = []
                    if h + 1 < 64:
                        nxt, pops = prep_head(h + 1)
                    per = (len(pops) + 59) // 60
                    pend = [scores(c, 0), scores(c, 1)]
                    for s in range(64):
                        cur = pend.pop(0)
                        if s + 2 < 64:
                            pend.append(scores(c, s + 2))
                        pv(c, s, hh, *cur)
                        for o_ in pops[:per]:
                            o_()
                        pops = pops[per:]
                    for o_ in pops:
                        o_()
                    if hh == 3:
                        fw.dma(pool, s_o[:, g * 128:(g + 1) * 128].rearrange("(t p) d -> p t d", p=128), c["og"][:],
                               owner=c["ogb"], reads=[c["ogb"]], ndesc=8192)
            fw.barrier()

        cast_weight("ro", w_ro, wb_ro)
        cast_weight("in1", w_in1, wb_in1)
        cast_weight("out1", w_out1, wb_out1)
        if "C" in phases:
            cmn = open_common(True)
            with ExitStack() as sC_:
                cs = [(sb("cos%d" % i, [128, 512], F32, sC_), sb("sin%d" % i, [128, 512], F32, sC_), fw.buf("cs%d" % i)) for i in range(2)]
                rot = Ring([(sb("rot%d" % i, [128, 512], F32, sC_), fw.buf("rot%d" % i)) for i in range(4)])
                qsave = Ring([(sb("qsv%d" % i, [128, 512], F32, sC_), fw.buf("qsv%d" % i)) for i in range(2)])
                gt = Ring([(sb("gt%d" % i, [128, 512], F32, sC_), fw.buf("gt%d" % i)) for i in range(2)])
                for blk in range(NBLK):
                    t0 = blk * 512
                    for i in range(4):
                        fw.dma(sp, hb[i][0][:], s_o[t0 + i * 128:t0 + (i + 1) * 128, :], owner=hb[i][1], writes=[hb[i][1]])
                    load_x(x_in, blk)
                    cst, snt, csb = cs[blk % 2]
                    fw.dma(sp, cst[:], cos_d[:, t0:t0 + 512], owner=csb, writes=[csb])
                    fw.dma(sp, snt[:], sin_d[:, t0:t0 + 512], owner=csb, writes=[csb])
                    transpose_to(TA, hb, 16, None, 0)
                    proj_tokmajor_add([TA], 1, "o", wb_o)
                    norm_T(TB, 1)
                    mlp(TB, "in0", wb_in0, "out0", wb_out0)
                    for i in range(4):
                        fw.dma(pool, s_x2[t0 + i * 128:t0 + (i + 1) * 128, :], xt[i][0][:], owner=xt[i][1], reads=[xt[i][1]])
                    norm_T(TA, 2)
                    saved = {}

                    def post_rqk(wt, oc, pt, pb):
                        chunk = (wt % 4) * 4 + oc
                        dst = s_rqT if wt < 4 else s_rkT
                        if chunk % 2 == 0:
                            qs, qsb = qsave.next()
                            evac(qs[:], pt[:], [pb], [qsb])
                            saved["x1"] = (qs, qsb)
                        else:
                            q1, q1b = saved["x1"]
                            t1, t1b = rot.next()
                            t2, t2b = rot.next()
                            fw.op(dve, lambda: nc.vector.tensor_tensor(out=t1[:], in0=pt[:], in1=snt[:], op=ALU.mult),
                                  reads=[pb, csb], writes=[t1b])
                            fw.op(dve, lambda: nc.vector.tensor_tensor(out=t2[:], in0=pt[:], in1=cst[:], op=ALU.mult),
                                  reads=[pb, csb], writes=[t2b])
                            t3, t3b = rot.next()
                            t4, t4b = rot.next()
                            fw.op(pool, lambda: nc.gpsimd.tensor_tensor(out=t3[:], in0=q1[:], in1=cst[:], op=ALU.mult),
                                  reads=[q1b, csb], writes=[t3b])
                            fw.op(pool, lambda: nc.gpsimd.tensor_tensor(out=t4[:], in0=q1[:], in1=snt[:], op=ALU.mult),
                                  reads=[q1b, csb], writes=[t4b])
                            s1, s1b = stg.next()
                            s2, s2b = stg.next()
                            fw.op(pool, lambda: nc.gpsimd.tensor_tensor(out=s1[:], in0=t3[:], in1=t1[:], op=ALU.subtract),
                                  reads=[t3b, t1b], writes=[s1b])
                            fw.op(pool, lambda: nc.gpsimd.tensor_tensor(out=s2[:], in0=t4[:], in1=t2[:], op=ALU.add),
                                  reads=[t4b, t2b], writes=[s2b])
                            f0 = (chunk - 1) * 128
                            fw.dma(pool, dst[f0:f0 + 128, t0:t0 + 512], s1[:], owner=s1b, reads=[s1b])
                            fw.dma(pool, dst[f0 + 128:f0 + 256, t0:t0 + 512], s2[:], owner=s2b, reads=[s2b])

                    proj_featmajor(TA, "r", wb_r, range(8), None, blk, post=post_rqk)
                    for wt in range(8, 24):
                        wtile, wbuf = load_w("r", wb_r, 0, wt * 512)
                        for i in range(4):
                            pt, pb = pm.next()
                            for kc in range(16):
                                fw.op(pe, lambda: nc.tensor.matmul(pt[:], lhsT=TA[0][:, kc, i * 128:(i + 1) * 128],
                                                                   rhs=wtile[:, kc, :], start=(kc == 0), stop=(kc == 15)),
                                      reads=[wbuf, TA[1]], writes=[pb])
                            stt, stb = stg.next()
                            if wt < 16:
                                evac(stt[:], pt[:], [pb], [stb])
                                fw.dma(pool, s_rv[t0 + i * 128:t0 + (i + 1) * 128, (wt - 8) * 512:(wt - 7) * 512], stt[:],
                                       owner=stb, reads=[stb])
                            else:
                                fw.op(act, lambda: nc.scalar.activation(out=stt[:], in_=pt[:], func=AF.Silu),
                                      reads=[pb], writes=[stb])
                                fw.dma(pool, s_rg[t0 + i * 128:t0 + (i + 1) * 128, (wt - 16) * 512:(wt - 15) * 512], stt[:],
                                       owner=stb, reads=[stb])
            fw.barrier()
            cmn.close()

        if "D" in phases:
            cmn = open_common(False, nptr=2)
            with ExitStack() as sD_:
                dec = sb("dec", [128, 16], F32, sD_)
                lg = sb("lg", [128, 16], F32, sD_)
                MT = sb("MT", [128, 8, 128], F32, sD_)
                mtmp = sb("mtmp", [128, 128], F32, sD_)
                qdf = sb("qdf", [128, 8, 128], F32, sD_)
                qdb = sb("qdb", [128, 8, 128], F32, sD_)
                kdf = sb("kdf", [128, 8], F32, sD_)
                kdb = sb("kdb", [128, 8], F32, sD_)
                cdf = sb("cdf", [128, 8], F32, sD_)
                cdb = sb("cdb", [128, 8], F32, sD_)
                Bt = fw.buf("rtabs")
                fw.dma(sp, dec[:], dec_d, owner=Bt, writes=[Bt])
                fw.op(act, lambda: nc.scalar.activation(out=lg[:], in_=dec[:], func=AF.Exp), reads=[Bt], writes=[Bt])
                fw.op(dve, lambda: nc.vector.tensor_scalar(out=lg[:], in0=lg[:], scalar1=-1.0, scalar2=None, op0=ALU.mult),
                      reads=[Bt], writes=[Bt])
                for hd in range(8):
                    lf = lg[:, hd:hd + 1]
                    lb = lg[:, 8 + hd:9 + hd]
                    fw.op(act, lambda: nc.scalar.activation(out=MT[:, hd, :], in_=ctab[:, C_DPOS:C_DPOS + 128], func=AF.Exp, scale=lf),
                          reads=[Bt, Bc], writes=[Bt])
                    fw.op(dve, lambda: nc.vector.scalar_tensor_tensor(out=MT[:, hd, :], in0=MT[:, hd, :], scalar=RET_SCALE,
                                                                      in1=ctab[:, C_MPOS:C_MPOS + 128], op0=ALU.mult, op1=ALU.mult),
                          reads=[Bt, Bc], writes=[Bt])
                    fw.op(act, lambda: nc.scalar.activation(out=mtmp[:], in_=ctab[:, C_DNEG:C_DNEG + 128], func=AF.Exp, scale=lb),
                          reads=[Bt, Bc], writes=[Bt])
                    fw.op(dve, lambda: nc.vector.scalar_tensor_tensor(out=mtmp[:], in0=mtmp[:], scalar=RET_SCALE,
                                                                      in1=ctab[:, C_MNEG:C_MNEG + 128], op0=ALU.mult, op1=ALU.mult),
                          reads=[Bt, Bc], writes=[Bt])
                    fw.op(dve, lambda: nc.vector.tensor_tensor(out=MT[:, hd, :], in0=MT[:, hd, :], in1=mtmp[:], op=ALU.add),
                          reads=[Bt], writes=[Bt])
                    fw.op(act, lambda: nc.scalar.activation(out=qdf[:, hd, :], in_=ctab[:, C_ROW1:C_ROW1 + 128], func=AF.Exp, scale=lf),
                          reads=[Bt, Bc], writes=[Bt])
                    fw.op(act, lambda: nc.scalar.activation(out=qdb[:, hd, :], in_=ctab[:, C_ROW2:C_ROW2 + 128], func=AF.Exp, scale=lb),
                          reads=[Bt, Bc], writes=[Bt])
                    fw.op(act, lambda: nc.scalar.activation(out=kdf[:, hd:hd + 1], in_=ctab[:, C_K1:C_K1 + 1], func=AF.Exp, scale=lf),
                          reads=[Bt, Bc], writes=[Bt])
                    fw.op(act, lambda: nc.scalar.activation(out=kdb[:, hd:hd + 1], in_=ctab[:, C_K2:C_K2 + 1], func=AF.Exp, scale=lb),
                          reads=[Bt, Bc], writes=[Bt])
                    fw.op(act, lambda: nc.scalar.activation(out=cdf[:, hd:hd + 1], in_=ctab[:, C_128:C_128 + 1], func=AF.Exp, scale=lf),
                          reads=[Bt, Bc], writes=[Bt])
                    fw.op(act, lambda: nc.scalar.activation(out=cdb[:, hd:hd + 1], in_=ctab[:, C_128:C_128 + 1], func=AF.Exp, scale=lb),
                          reads=[Bt, Bc], writes=[Bt])
                fw.op(dve, lambda: nc.vector.tensor_scalar(out=kdf[:], in0=kdf[:], scalar1=RET_SCALE, scalar2=None, op0=ALU.mult),
                      reads=[Bt], writes=[Bt])
                fw.op(dve, lambda: nc.vector.tensor_scalar(out=kdb[:], in0=kdb[:], scalar1=RET_SCALE, scalar2=None, op0=ALU.mult),
                      reads=[Bt], writes=[Bt])

                HS = []
                for u in range(2):
                    HS.append(dict(
                        grp=Ring([(sb("gq%d_%d" % (u, i), [128, 2, 1024], BF16, sD_), sb("gk%d_%d" % (u, i), [128, 2, 1024], BF16, sD_),
                                   sb("gv%d_%d" % (u, i), [128, 8, 512], BF16, sD_), fw.buf("grp%d_%d" % (u, i))) for i in range(2)]),
                        qsc=Ring([(sb("qsc%d_%d" % (u, i), [128, 2, 1024], BF16, sD_), fw.buf("qsc%d_%d" % (u, i))) for i in range(2)]),
                        state=(sb("state%d" % u, [128, 2, 512], F32, sD_), fw.buf("state%d" % u)),
                        stbf=(sb("stbf%d" % u, [128, 2, 512], BF16, sD_), fw.buf("stbf%d" % u)),
                        u=u))
                ktl = Ring([(sb("ktl%d" % i, [128, 256], BF16, sD_), fw.buf("ktl%d" % i)) for i in range(3)])
                smk = Ring([(sb("smk%d" % i, [128, 128], BF16, sD_), fw.buf("smk%d" % i)) for i in range(3)])
                ybo = Ring([(sb("ybo%d" % i, [128, 512], F32, sD_), fw.buf("ybo%d" % i)) for i in range(3)])
                ybi = Ring([(sb("ybi%d" % i, [128, 512], F32, sD_), fw.buf("ybi%d" % i)) for i in range(4)])
                ysum = Ring([(sb("ysum%d" % i, [128, 512], F32, sD_), fw.buf("ysum%d" % i)) for i in range(4)])
                yno = Ring([(sb("yno%d" % i, [128, 512], BF16, sD_), fw.buf("yno%d" % i)) for i in range(3)])
                gst = Ring([(sb("gst%d" % i, [128, 8, 2], F32, sD_), fw.buf("gst%d" % i)) for i in range(3)])
                pS = Ring([(pm.items[0][0][:, k * 128:(k + 1) * 128], fw.buf("pS%d" % k)) for k in range(4)])
                pY = Ring([pm.items[1], pm.items[2]])
                pSt = Ring([(pm.items[3][0], fw.buf("pstA")), (ps("pst1", [128, 512], F32, sD_), fw.buf("pstB")),
                            (ps("pst2", [128, 512], F32, sD_), fw.buf("pstC"))])
                ptr.items = [(ptr.items[i // 2][0][:, (i % 2) * 256:(i % 2) * 256 + 256], fw.buf("ptrD%d" % i)) for i in range(4)]

                def load_group(hs, hd, gi):
                    gq, gk, gv, gb = hs["grp"].next()
                    c0 = gi * 1024
                    fw.dma(sp, gq[:], s_rqT[hd * 256:(hd + 1) * 256, c0:c0 + 1024].rearrange("(x p) t -> p x t", p=128),
                           owner=gb, writes=[gb])
                    fw.dma(sp, gk[:], s_rkT[hd * 256:(hd + 1) * 256, c0:c0 + 1024].rearrange("(x p) t -> p x t", p=128),
                           owner=gb, writes=[gb])
                    fw.dma(sp, gv[:], s_rv[c0:c0 + 1024, hd * 512:(hd + 1) * 512].rearrange("(c p) e -> p c e", p=128),
                           owner=gb, writes=[gb])
                    hs["g"] = (gq, gk, gv, gb)

                def scale_q(hs, qd_tab, hd):
                    gq, gk, gv, gb = hs["g"]
                    qs, qsb = hs["qsc"].next()
                    fw.op(pool, lambda: nc.gpsimd.tensor_tensor(
                        out=qs[:].rearrange("p x (c i) -> p (x c) i", i=128),
                        in0=gq[:].rearrange("p x (c i) -> p (x c) i", i=128),
                        in1=qd_tab[:, hd, :].unsqueeze(1).to_broadcast([128, 16, 128]), op=ALU.mult),
                        reads=[gb, Bt], writes=[qsb])
                    hs["qs"] = (qs, qsb)

                def state_T(hs, cl, kd, hd):
                    gq, gk, gv, gb = hs["g"]
                    ptk, ptkb = ptr.next()
                    for x in range(2):
                        fw.op(pe, lambda: nc.tensor.transpose(out=ptk[:, x * 128:(x + 1) * 128],
                                                              in_=gk[:, x, cl * 128:(cl + 1) * 128], identity=ident[:]),
                              reads=[gb, Bc], writes=[ptkb])
                    kt2, kt2b = ktl.next()
                    fw.op(act, lambda: nc.scalar.activation(out=kt2[:], in_=ptk[:, 0:256], func=AF.Copy, scale=kd[:, hd:hd + 1]),
                          reads=[ptkb, Bt], writes=[kt2b])
                    hs["kt2"] = (kt2, kt2b)

                def state_update(hs, cl, cd, hd):
                    gq, gk, gv, gb = hs["g"]
                    state, stbf = hs["state"], hs["stbf"]
                    kt2, kt2b = hs["kt2"]
                    for x in range(2):
                        pst_, pstb_ = pSt.next()
                        fw.op(pe, lambda: nc.tensor.matmul(pst_[:], lhsT=kt2[:, x * 128:(x + 1) * 128], rhs=gv[:, cl, :],
                                                           start=True, stop=True), reads=[kt2b, gb], writes=[pstb_])
                        fw.op(dve, lambda: nc.vector.scalar_tensor_tensor(out=state[0][:, x, :], in0=state[0][:, x, :],
                                                                          scalar=cd[:, hd:hd + 1], in1=pst_[:],
                                                                          op0=ALU.mult, op1=ALU.add),
                              reads=[pstb_, state[1], Bt], writes=[state[1]])
                    fw.op(act, lambda: nc.scalar.activation(out=stbf[0][:], in_=state[0][:], func=AF.Copy),
                          reads=[state[1]], writes=[stbf[1]])

                def reset_state(hs):
                    fw.op(dve, lambda: nc.vector.memset(hs["state"][0][:], 0.0), writes=[hs["state"][1]])
                    fw.op(pool, lambda: nc.gpsimd.memset(hs["stbf"][0][:], 0.0), writes=[hs["stbf"][1]])

                def carry_boundary(hs):
                    state, stbf = hs["state"], hs["stbf"]
                    fw.op(dve, lambda: nc.vector.tensor_scalar(out=state[0][:], in0=state[0][:], scalar1=flag[:, 0:1],
                                                               scalar2=None, op0=ALU.mult),
                          reads=[state[1], Bc], writes=[state[1]])
                    fw.op(act, lambda: nc.scalar.activation(out=stbf[0][:], in_=state[0][:], func=AF.Copy),
                          reads=[state[1]], writes=[stbf[1]])

                def bwd_step(hs, hd, c, cl):
                    gq, gk, gv, gb = hs["g"]
                    qs, qsb = hs["qs"]
                    stbf = hs["stbf"]
                    if c == 31:
                        carry_boundary(hs)
                    state_T(hs, cl, kdb, hd)
                    yt, ytb = pY.next()
                    for x in range(2):
                        fw.op(pe, lambda: nc.tensor.matmul(yt[:], lhsT=qs[:, x, cl * 128:(cl + 1) * 128],
                                                           rhs=stbf[0][:, x, :], start=(x == 0), stop=(x == 1)),
                              reads=[qsb, stbf[1]], writes=[ytb])
                    state_update(hs, cl, cdb, hd)
                    yo, yob = ybo.next()
                    evac(yo[:], yt[:], [ytb], [yob])
                    fw.dma(pool, s_yb[c * 128:(c + 1) * 128, hs["u"] * 512:(hs["u"] + 1) * 512], yo[:], owner=yob, reads=[yob])

                def fwd_step(hs, hd, c, cl, rnd):
                    gq, gk, gv, gb = hs["g"]
                    qs, qsb = hs["qs"]
                    stbf = hs["stbf"]
                    if c == 32:
                        carry_boundary(hs)
                    yi, yib = ybi.next()
                    fw.dma(sp, yi[:], s_yb[c * 128:(c + 1) * 128, hs["u"] * 512:(hs["u"] + 1) * 512], owner=yib, writes=[yib])
                    sT_, sTb = pS.next()
                    for x in range(2):
                        fw.op(pe, lambda: nc.tensor.matmul(sT_, lhsT=gk[:, x, cl * 128:(cl + 1) * 128],
                                                           rhs=gq[:, x, cl * 128:(cl + 1) * 128],
                                                           start=(x == 0), stop=(x == 1)),
                              reads=[gb], writes=[sTb])
                    state_T(hs, cl, kdf, hd)
                    sm, smb = smk.next()
                    fw.op(dve, lambda: nc.vector.tensor_tensor(out=sm[:], in0=sT_, in1=MT[:, hd, :], op=ALU.mult),
                          reads=[sTb, Bt], writes=[smb])
                    yt, ytb = pY.next()
                    fw.op(pe, lambda: nc.tensor.matmul(yt[:], lhsT=sm[:], rhs=gv[:, cl, :], start=True, stop=False),
                          reads=[smb, gb], writes=[ytb])
                    for x in range(2):
                        fw.op(pe, lambda: nc.tensor.matmul(yt[:], lhsT=qs[:, x, cl * 128:(cl + 1) * 128],
                                                           rhs=stbf[0][:, x, :], start=False, stop=(x == 1)),
                              reads=[qsb, stbf[1]], writes=[ytb])
                    state_update(hs, cl, cdf, hd)
                    ysm, ysb = ysum.next()
                    fw.op(dve, lambda: nc.vector.tensor_tensor(out=ysm[:], in0=yt[:], in1=yi[:], op=ALU.add),
                          reads=[ytb, yib], writes=[ysb])
                    gs, gsb = rnd["gs"]
                    u = hs["u"]
                    fw.op(act, lambda: nc.scalar.activation(out=S["junk"][:, 0:512], in_=ysm[:], func=AF.Copy,
                                                            accum_out=gs[:, 0, u:u + 1]), reads=[ysb], writes=[Bjunk, gsb])
                    fw.op(act, lambda: nc.scalar.activation(out=S["junk"][:, 0:512], in_=ysm[:], func=AF.Square,
                                                            accum_out=gs[:, 1, u:u + 1]), reads=[ysb], writes=[Bjunk, gsb])
                    rnd["items"].append((hs, hd, c, ysm, ysb))

                def gn_tail(rnd):
                    gs, gsb = rnd["gs"]
                    fw.op(act, lambda: nc.scalar.activation(out=S["junk"][:, 0:8], in_=epsg[:].to_broadcast([128, 8]), func=AF.Copy),
                          reads=[Bc], writes=[Bjunk, gsb])
                    fw.op(dve, lambda: nc.vector.tensor_scalar(out=gs[:, 2, :], in0=gs[:, 0, :], scalar1=1.0 / 512, scalar2=None,
                                                               op0=ALU.mult), reads=[gsb], writes=[gsb])
                    fw.op(dve, lambda: nc.vector.tensor_tensor(out=gs[:, 3, :], in0=gs[:, 2, :], in1=gs[:, 2, :], op=ALU.mult),
                          reads=[gsb], writes=[gsb])
                    fw.op(dve, lambda: nc.vector.scalar_tensor_tensor(out=gs[:, 4, :], in0=gs[:, 1, :], scalar=1.0 / 512,
                                                                      in1=gs[:, 3, :], op0=ALU.mult, op1=ALU.subtract),
                          reads=[gsb], writes=[gsb])
                    fw.op(act, lambda: nc.scalar.activation(out=gs[:, 5, :], in_=gs[:, 4, :], func=AF.Sqrt, bias=epsg[:, 0:1]),
                          reads=[gsb, Bc], writes=[gsb])
                    fw.op(dve, lambda: nc.vector.reciprocal(out=gs[:, 6, :], in_=gs[:, 5, :]), reads=[gsb], writes=[gsb])
                    fw.op(dve, lambda: nc.vector.scalar_tensor_tensor(out=gs[:, 7, :], in0=gs[:, 2, :], scalar=-1.0, in1=gs[:, 6, :],
                                                                      op0=ALU.mult, op1=ALU.mult), reads=[gsb], writes=[gsb])
                    for hs, hd, c, ysm, ysb in rnd["items"]:
                        u = hs["u"]
                        yn_, ynb = yno.next()
                        fw.op(pool, lambda: nc.gpsimd.tensor_scalar(out=yn_[:], in0=ysm[:], scalar1=gs[:, 6, u:u + 1],
                                                                    scalar2=gs[:, 7, u:u + 1], op0=ALU.mult, op1=ALU.add),
                              reads=[ysb, gsb], writes=[ynb])
                        fw.dma(pool, s_yn[c * 128:(c + 1) * 128, hd * 512:(hd + 1) * 512], yn_[:], owner=ynb, reads=[ynb])

                for hp in range(4):
                    heads = [(HS[0], 2 * hp), (HS[1], 2 * hp + 1)]
                    for hs, hd in heads:
                        reset_state(hs)
                    for gi in range(7, -1, -1):
                        for hs, hd in heads:
                            load_group(hs, hd, gi)
                            scale_q(hs, qdb, hd)
                        for cl in range(7, -1, -1):
                            for hs, hd in heads:
                                bwd_step(hs, hd, gi * 8 + cl, cl)
                    fw.barrier()
                    for hs, hd in heads:
                        reset_state(hs)
                    for gi in range(8):
                        for hs, hd in heads:
                            load_group(hs, hd, gi)
                            scale_q(hs, qdf, hd)
                        for cl in range(8):
                            rnd = {"gs": gst.next(), "items": []}
                            for hs, hd in heads:
                                fwd_step(hs, hd, gi * 8 + cl, cl, rnd)
                            gn_tail(rnd)
                    fw.barrier()
            cmn.close()

        if "E" in phases:
            cmn = open_common(True, need_stg=False)
            with ExitStack() as sE_:
                gg = [(sb("gg%d" % i, [128, D], BF16, sE_), fw.buf("gg%d" % i)) for i in range(4)]
                nfin = sb("nfin", [128, D], F32, sE_)
                fw.dma(sp, nfin[:], nrm_fin_d, owner=Bl, writes=[Bc])
                for blk in range(NBLK):
                    t0 = blk * 512
                    load_x(s_x2, blk)
                    for half, dstT in enumerate((TA, TB)):
                        for i in range(4):
                            r0 = t0 + i * 128
                            fw.dma(sp, hb[i][0][:], s_yn[r0:r0 + 128, half * D:(half + 1) * D], owner=hb[i][1], writes=[hb[i][1]])
                            fw.dma(sp, gg[i][0][:], s_rg[r0:r0 + 128, half * D:(half + 1) * D], owner=gg[i][1], writes=[gg[i][1]])
                        for i in range(4):
                            if i % 2:
                                fw.op(dve, lambda: nc.vector.tensor_tensor(out=hb[i][0][:], in0=hb[i][0][:], in1=gg[i][0][:], op=ALU.mult),
                                      reads=[hb[i][1], gg[i][1]], writes=[hb[i][1]])
                            else:
                                fw.op(pool, lambda: nc.gpsimd.tensor_tensor(out=hb[i][0][:], in0=hb[i][0][:], in1=gg[i][0][:], op=ALU.mult),
                                      reads=[hb[i][1], gg[i][1]], writes=[hb[i][1]])
                        transpose_to(dstT, hb, 16, gnw_col, half * 16)
                    proj_tokmajor_add([TA, TB], 2, "ro", wb_ro)
                    norm_T(TA, 3)
                    mlp(TA, "in1", wb_in1, "out1", wb_out1)
                    rms_rstd(xt, epsn, 1.0 / D)
                    for i in range(4):
                        ut, ub = u2T.items[i // 2]
                        ov = ut[:].rearrange("p a b -> p (a b)").bitcast(F32)[:, (i % 2) * D:(i % 2 + 1) * D]
                        fw.op(dve, lambda: nc.vector.scalar_tensor_tensor(out=ov, in0=xt[i][0][:], scalar=S["rstd"][:, i:i + 1],
                                                                          in1=nfin[:], op0=ALU.mult, op1=ALU.mult),
                              reads=[xt[i][1], Bss, Bc], writes=[ub])
                        fw.dma(pool, y_out[t0 + i * 128:t0 + (i + 1) * 128, :], ov, owner=ub, reads=[ub])
            fw.barrier()
            cmn.close()
        fw.barrier()
        if dbg and 'nodbg' not in KB:
            Bd = fw.buf("dbgout")
            for nm, t_, fm in (("qT", s_qT, True), ("kT", s_kT, True), ("v", s_v, False), ("o", s_o, False),
                               ("x2", s_x2, False), ("rqT", s_rqT, True), ("rkT", s_rkT, True), ("rv", s_rv, False),
                               ("rg", s_rg, False), ("yn", s_yn, False)):
                segs = [(0, 256), (1024, 1152), (1536, 1664), (3968, 4224)]
                tot = sum(e - b_ for b_, e in segs)
                if fm:
                    dd = nc.dram_tensor("dbg_" + nm, [t_.shape[0], tot], t_.dtype, kind="ExternalOutput").ap()
                else:
                    dd = nc.dram_tensor("dbg_" + nm, [tot, t_.shape[1]], t_.dtype, kind="ExternalOutput").ap()
                o_ = 0
                for b_, e in segs:
                    if fm:
                        fw.dma(sp, dd[:, o_:o_ + e - b_], t_[:, b_:e], owner=Bd)
                    else:
                        fw.dma(sp, dd[o_:o_ + e - b_, :], t_[b_:e, :], owner=Bd)
                    o_ += e - b_
            fw.barrier()
        stats = dict(nwaits=fw.nwaits, **{w.name: w.nseq for w in fw.engs})
    return nc, stats


def _const_tables():
    ct = np.zeros((128, NCTAB), np.float32)
    j = np.arange(128)[:, None].astype(np.float32)
    i = np.arange(128)[None, :].astype(np.float32)
    ct[:, C_DPOS:C_DPOS + 128] = np.maximum(i - j, 0)
    ct[:, C_DNEG:C_DNEG + 128] = np.maximum(j - i, 0)
    ct[:, C_MPOS:C_MPOS + 128] = (i >= j)
    ct[:, C_MNEG:C_MNEG + 128] = (j > i)
    ct[:, C_ROW1:C_ROW1 + 128] = np.broadcast_to(i + 1.0, (128, 128))
    ct[:, C_ROW2:C_ROW2 + 128] = np.broadcast_to(128.0 - i, (128, 128))
    ct[:, C_K1] = 127.0 - np.arange(128)
    ct[:, C_K2] = np.arange(128)
    ct[:, C_128] = 128.0
    kc = np.arange(64)[:, None]
    qc = np.arange(64)[None, :]
    cs = np.clip(qc - 8, 0, 48)
    cm = ((kc >= cs) & (kc < cs + 16)).astype(np.float32)
    ct[0:64, C_CMASK:C_CMASK + 64] = cm
    ct[64:128, C_CMASK:C_CMASK + 64] = cm
    return ct


def _rot_tables(pos):
    half = 128
    freqs = (np.float32(10000.0) ** (-np.arange(half, dtype=np.float32) / np.float32(half))).astype(np.float32)
    ang = pos.astype(np.float32)[None, :] * freqs[:, None]
    return np.cos(ang).astype(np.float32), np.sin(ang).astype(np.float32)


_CACHE = {}


def kernel(x_prompt, x_sample, norm_mix, na_w_qkv, na_rpb, na_w_o, ret_w_qkvg, ret_decay_fwd,
           ret_decay_bwd, ret_gn_w, ret_w_o, norm_mlp, mlp_w_in, mlp_w_out, norm_final):
    f = lambda a: np.ascontiguousarray(np.asarray(a, dtype=np.float32))
    x_prompt, x_sample = f(x_prompt), f(x_sample)
    if "nc" not in _CACHE:
        _CACHE["nc"] = build_program()[0]
    nc = _CACHE["nc"]
    nrm = np.stack([f(norm_mix)[0], f(norm_mlp)[0], f(norm_mix)[1], f(norm_mlp)[1]], 0)
    nrm_col = np.ascontiguousarray(nrm.reshape(4, 16, 128).transpose(2, 0, 1).reshape(128, 64))
    nrm_fin = np.ascontiguousarray(np.broadcast_to(f(norm_final)[None, :], (128, D)))
    gnw_col = np.ascontiguousarray(f(ret_gn_w)[0].reshape(32, 128).T)
    rpbf = np.ascontiguousarray(f(na_rpb)[0][:, :, ::-1].reshape(64, 465))
    dec = np.ascontiguousarray(np.broadcast_to(np.concatenate([f(ret_decay_fwd)[0], f(ret_decay_bwd)[0]])[None, :], (128, 16)))
    ctab = _const_tables()
    cos_j, sin_j = _rot_tables(np.arange(NTOK))
    cos_s, sin_s = _rot_tables(np.concatenate([np.arange(4096), np.arange(4096)]))
    common = {
        "w_qkv": f(na_w_qkv)[0], "w_o": f(na_w_o)[0], "w_in0": f(mlp_w_in)[0], "w_out0": f(mlp_w_out)[0],
        "w_r": f(ret_w_qkvg)[0], "w_ro": f(ret_w_o)[0], "w_in1": f(mlp_w_in)[1], "w_out1": f(mlp_w_out)[1],
        "nrm_col": nrm_col, "nrm_fin": nrm_fin, "gnw_col": gnw_col, "rpbf": rpbf, "dec": dec, "ctab": ctab,
    }
    xp = np.ascontiguousarray(x_prompt.reshape(NTOK, D))
    in_maps = []
    for c in range(NCORES):
        m = dict(common)
        if c < 4:
            m["x"] = x_sample[c]
            joined = 1.0
        else:
            m["x"] = xp
            joined = 0.0
        m["cosT"], m["sinT"] = (cos_j, sin_j) if joined else (cos_s, sin_s)
        m["flag"] = np.full((128, 1), joined, np.float32)
        in_maps.append(m)
    if _CACHE.get("return_maps"):
        return in_maps
    res = run_bass_kernel_spmd(nc, in_maps, core_ids=list(range(NCORES)))
    y_sample = np.stack([np.asarray(res.results[c]["y"], dtype=np.float32) for c in range(4)], 0)
    y_prompt = np.asarray(res.results[4]["y"], dtype=np.float32).reshape(2, 4096, D)
    return (y_prompt, y_sample)
```

```python
import math
from contextlib import ExitStack

import numpy as np
import concourse.bass as bass
import concourse.mybir as mybir
from concourse.bass_utils import run_bass_kernel_spmd

F32 = mybir.dt.float32
BF16 = mybir.dt.bfloat16
AF = mybir.ActivationFunctionType
ALU = mybir.AluOpType

D = 2048
NTOK = 8192
import os
NBLK = int(os.environ.get('KB_NBLK', '16'))
KB = os.environ.get('KB', '')
DFF = 8192
NCORES = 8
NORM_EPS = 1e-6
GN_EPS = 1e-5
NA_SCALE = 32 ** -0.5
RET_SCALE = 256 ** -0.5

C_DPOS, C_DNEG, C_MPOS, C_MNEG, C_ROW1, C_ROW2, C_K1, C_K2, C_128, C_CMASK = 0, 128, 256, 384, 512, 640, 768, 769, 770, 771
NCTAB = 771 + 64


class EngW:
    def __init__(self, name, e):
        self.name = name
        self.e = e
        self.sem = None
        self.count = 0
        self.nseq = 0
        self.sigs = []
        self.known = {}
        self.last_ins = None
        self.last_sig = True

    def signal_for(self, seq):
        lo, hi = 0, len(self.sigs)
        while lo < hi:
            mid = (lo + hi) // 2
            if self.sigs[mid][0] >= seq:
                hi = mid
            else:
                lo = mid + 1
        if lo < len(self.sigs):
            return self.sigs[lo][1]
        assert self.last_ins is not None and not self.last_sig, (self.name, seq, self.nseq)
        self.count += 1
        self.last_ins.then_inc(self.sem, 1)
        self.last_sig = True
        self.sigs.append((self.nseq - 1, self.count))
        return self.count


class Buf:
    __slots__ = ("name", "last_w", "reads", "dsem", "dval")

    def __init__(self, name):
        self.name = name
        self.last_w = None
        self.reads = []
        self.dsem = None
        self.dval = 0


class FW:
    def __init__(self, nc, stack):
        self.nc = nc
        self.stack = stack
        self.engs = []
        for name, e in (("pe", nc.tensor), ("act", nc.scalar), ("dve", nc.vector),
                        ("pool", nc.gpsimd), ("sp", nc.sync)):
            w = EngW(name, e)
            w.sem = stack.enter_context(nc.semaphore("sem_" + name))
            setattr(self, name, w)
            self.engs.append(w)
        self.nwaits = 0
        self.swq = []
        self.swq_total = 0
        self.bufs = {}
        self.dsems = []
        self.free_dsems = []

    def buf(self, name):
        if name not in self.bufs:
            self.bufs[name] = Buf(name)
        return self.bufs[name]

    def _resolve(self, acc):
        if acc[0] == 'e':
            return acc[1].sem, acc[1].signal_for(acc[2])
        return acc[1], acc[2]

    def _wait_deps(self, w, reads, writes, same_engine=False):
        deps = []
        for b in reads:
            if b.last_w is not None:
                deps.append((b.last_w, True))
        for b in writes:
            if b.last_w is not None:
                deps.append((b.last_w, False))
            for r in b.reads:
                deps.append((r, False))
        need = {}
        for acc, is_raw in deps:
            if acc[0] == 'e' and acc[1] is w and not same_engine:
                if not is_raw or w.name == "pe":
                    continue
            sem, val = self._resolve(acc)
            k = id(sem)
            if k not in need or need[k][1] < val:
                need[k] = (sem, val)
        for k, (sem, val) in need.items():
            if w.known.get(k, 0) >= val:
                continue
            w.e.wait_ge(sem, val)
            w.known[k] = val
            self.nwaits += 1

    def _record(self, acc, reads, writes):
        for b in writes:
            b.last_w = acc
            b.reads = []
        for b in reads:
            if acc[0] == 'e':
                b.reads = [r for r in b.reads if not (r[0] == 'e' and r[1] is acc[1])]
            else:
                b.reads = [r for r in b.reads if not (r[0] == 'd' and r[1] is acc[1])]
            b.reads.append(acc)

    def op(self, w, fn, reads=(), writes=()):
        self._wait_deps(w, reads, writes)
        ins = fn()
        w.last_ins = ins
        w.last_sig = False
        seq = w.nseq
        w.nseq += 1
        self._record(('e', w, seq), reads, writes)
        return ins

    def dma(self, q, out, in_, owner, reads=(), writes=(), ndesc=128, **kw):
        if (not q.last_sig) and q.last_ins is not None:
            q.signal_for(q.nseq - 1)
        self._wait_deps(q, reads, writes, same_engine=True)
        if owner.dsem is None:
            owner.dsem = self.stack.enter_context(self.nc.semaphore("ds_%s" % owner.name))
            self.dsems.append(owner)
        if q.name == "pool":
            per_eng = ndesc // 16 + 2
            while self.swq and self.swq_total + per_eng > 600:
                sem, val, n = self.swq.pop(0)
                self.swq_total -= n
                k = id(sem)
                if q.known.get(k, 0) < val:
                    q.e.wait_ge(sem, val)
                    q.known[k] = val
                    self.nwaits += 1
        owner.dval += 16
        ins = q.e.dma_start(out=out, in_=in_, **kw).then_inc(owner.dsem, 16)
        if q.name == "pool":
            self.swq.append((owner.dsem, owner.dval, per_eng))
            self.swq_total += per_eng
        q.last_ins = None
        q.last_sig = True
        q.nseq += 1
        self._record(('d', owner.dsem, owner.dval), reads, writes)
        return ins

    def barrier(self):
        for w in self.engs:
            if (not w.last_sig) and w.last_ins is not None:
                w.signal_for(w.nseq - 1)
        for w in self.engs:
            for b in self.dsems:
                k = id(b.dsem)
                if b.dval > 0 and w.known.get(k, 0) < b.dval:
                    w.e.wait_ge(b.dsem, b.dval)
                    w.known[k] = b.dval
                    self.nwaits += 1
            for w2 in self.engs:
                if w2 is w or w2.count == 0:
                    continue
                k = id(w2.sem)
                if w.known.get(k, 0) < w2.count:
                    w.e.wait_ge(w2.sem, w2.count)
                    w.known[k] = w2.count
                    self.nwaits += 1


class Ring:
    def __init__(self, items):
        self.items = items
        self.i = 0

    def next(self):
        it = self.items[self.i % len(self.items)]
        self.i += 1
        return it


def _rs(r, variant):
    if variant == "joined":
        return min(max(r - 4, 0), 120)
    if r < 64:
        return min(max(r - 4, 0), 56)
    return 64 + min(max(r - 64 - 4, 0), 56)


def _pair_entries(s, tiles, variant):
    out = []
    for j, t in enumerate(tiles):
        if t is None:
            continue
        for a in range(2):
            kr = 2 * t + a
            for b in range(2):
                r = 2 * s + b
                rs = _rs(r, variant)
                if rs <= kr <= rs + 7:
                    rr = kr - r + 7
                    assert 0 <= rr <= 14
                    out.append((j, a, b, rr))
    return out


SPECIAL = [0, 1, 30, 31, 32, 33, 62, 63]


def _tiles_for(s):
    if s in SPECIAL:
        return [t if 0 <= t <= 63 else None for t in range(s - 3, s + 4)]
    return list(range(s - 2, s + 3))


def build_program(phases="ABCDE", dbg=False):
    nc = bass.Bass("TRN2", target_bir_lowering=False)

    def din(name, shape, dt=F32):
        return nc.dram_tensor(name, list(shape), dt, kind="ExternalInput").ap()

    def dscr(name, shape, dt):
        return nc.dram_tensor(name, list(shape), dt, kind="Internal").ap()

    x_in = din("x", [NTOK, D])
    w_qkv = din("w_qkv", [D, 3 * D])
    w_o = din("w_o", [D, D])
    w_in0 = din("w_in0", [D, DFF])
    w_out0 = din("w_out0", [DFF, D])
    w_r = din("w_r", [D, 6 * D])
    w_ro = din("w_ro", [2 * D, D])
    w_in1 = din("w_in1", [D, DFF])
    w_out1 = din("w_out1", [DFF, D])
    nrm_col_d = din("nrm_col", [128, 64])
    nrm_fin_d = din("nrm_fin", [128, D])
    gnw_col_d = din("gnw_col", [128, 32])
    rpbf_d = din("rpbf", [64, 465])
    dec_d = din("dec", [128, 16])
    ctab_d = din("ctab", [128, NCTAB])
    cos_d = din("cosT", [128, NTOK])
    sin_d = din("sinT", [128, NTOK])
    flag_d = din("flag", [128, 1])
    y_out = nc.dram_tensor("y", [NTOK, D], F32, kind="ExternalOutput").ap()

    wb_qkv = nc.dram_tensor("wb_qkv", [D, 3 * D], BF16, kind="Internal").ap()
    wb_o = nc.dram_tensor("wb_o", [D, D], BF16, kind="Internal").ap()
    wb_in0 = nc.dram_tensor("wb_in0", [D, DFF], BF16, kind="Internal").ap()
    wb_out0 = nc.dram_tensor("wb_out0", [DFF, D], BF16, kind="Internal").ap()
    wb_r = nc.dram_tensor("wb_r", [D, 6 * D], BF16, kind="Internal").ap()
    wb_ro = nc.dram_tensor("wb_ro", [2 * D, D], BF16, kind="Internal").ap()
    wb_in1 = nc.dram_tensor("wb_in1", [D, DFF], BF16, kind="Internal").ap()
    wb_out1 = nc.dram_tensor("wb_out1", [DFF, D], BF16, kind="Internal").ap()

    s_qT = dscr("s_qT", [D, NTOK], BF16)
    s_kT = dscr("s_kT", [D, NTOK], BF16)
    s_v = dscr("s_v", [NTOK, D], BF16)
    s_o = dscr("s_o", [NTOK, D], BF16)
    s_x2 = dscr("s_x2", [NTOK, D], F32)
    s_rqT = dscr("s_rqT", [D, NTOK], BF16)
    s_rkT = dscr("s_rkT", [D, NTOK], BF16)
    s_rv = dscr("s_rv", [NTOK, 2 * D], BF16)
    s_rg = dscr("s_rg", [NTOK, 2 * D], BF16)
    s_yb = dscr("s_yb", [NTOK, 1024], F32)
    s_yn = dscr("s_yn", [NTOK, 2 * D], BF16)
    s_erp = nc.dram_tensor("s_erp", [1024, 127], BF16, kind="Internal").ap()
    s_G = nc.dram_tensor("s_G", [64, 1024, 64], BF16, kind="Internal").ap()

    with ExitStack() as st:
        fw = FW(nc, st)
        pe, act, dve, pool, sp = fw.pe, fw.act, fw.dve, fw.pool, fw.sp

        uid = [0]

        def sb(name, shape, dt, stack=st):
            uid[0] += 1
            return stack.enter_context(nc.sbuf_tensor("%s_%d" % (name, uid[0]), list(shape), dt))

        def ps(name, shape, dt, stack=st):
            uid[0] += 1
            return stack.enter_context(nc.psum_tensor("%s_%d" % (name, uid[0]), list(shape), dt))

        ident = sb("ident", [128, 128], BF16)
        identf = sb("identf", [128, 128], F32)
        nrm_col = sb("nrm_col_s", [128, 64], F32)
        gnw_col = sb("gnw_col_s", [128, 32], F32)
        flag = sb("flag_s", [128, 1], F32)
        ctab = sb("ctab_s", [128, NCTAB], F32)
        epsn = sb("epsn", [128, 1], F32)
        epsg = sb("epsg", [128, 1], F32)
        Bc = fw.buf("consts")
        fw.op(pool, lambda: nc.gpsimd.memset(identf[:], 0.0), writes=[Bc])
        fw.op(pool, lambda: nc.gpsimd.affine_select(out=identf[:], in_=identf[:], pattern=[[-1, 128]],
                                                     compare_op=ALU.not_equal, fill=1.0, base=0,
                                                     channel_multiplier=1), reads=[Bc], writes=[Bc])
        fw.op(pool, lambda: nc.gpsimd.memset(epsn[:], NORM_EPS), writes=[Bc])
        fw.op(pool, lambda: nc.gpsimd.memset(epsg[:], GN_EPS), writes=[Bc])
        fw.op(dve, lambda: nc.vector.tensor_copy(out=ident[:], in_=identf[:]), reads=[Bc], writes=[Bc])
        Bl = fw.buf("cload")
        for t_s, t_d in ((nrm_col, nrm_col_d), (gnw_col, gnw_col_d), (flag, flag_d), (ctab, ctab_d)):
            fw.dma(sp, t_s[:], t_d, owner=Bl, writes=[Bc])

        Bw = {}

        def cast_weight(key, src, dst):
            b = fw.buf("wc_" + key)
            Bw[key] = b
            if 'nocast2' in KB and key != 'qkv':
                return
            K, N = src.shape
            for k0 in range(0, K, 2048):
                for n0 in range(0, N, 2048):
                    fw.dma(pool, dst[k0:k0 + 2048, n0:n0 + 2048], src[k0:k0 + 2048, n0:n0 + 2048],
                           owner=b, writes=[b], ndesc=4096)

        wring = Ring([])
        pm = Ring([])
        ptr = Ring([])
        stg = Ring([])
        u2T = Ring([])
        rtmp = Ring([])
        xt = []
        hb = []
        TA = [None, None]
        TB = [None, None]
        S = {}
        Bjunk = fw.buf("junk")
        Bss = fw.buf("ss")

        def open_common(full, need_stg=True, nptr=4):
            cs_ = ExitStack()
            pm.items = [(ps("pm%d" % i, [128, 512], F32, cs_), fw.buf("pm%d" % i)) for i in range(4)]
            ptr_t = [ps("ptr%d" % i, [128, 512], BF16, cs_) for i in range(nptr)]
            ptr.items = [(ptr_t[i][:, :], fw.buf("ptr%d" % i)) for i in range(nptr)]
            S["junk"] = sb("junk", [128, D], BF16, cs_)
            if full:
                wring.items = [(sb("wt%d" % i, [128, 16, 512], BF16, cs_), fw.buf("wt%d" % i)) for i in range(3)]
                xt[:] = [(sb("xt%d" % i, [128, D], F32, cs_), fw.buf("xt%d" % i)) for i in range(4)]
                hb[:] = [(sb("hb%d" % i, [128, D], BF16, cs_), fw.buf("hb%d" % i)) for i in range(4)]
                S["ss"] = sb("ss", [128, 4], F32, cs_)
                S["rs"] = sb("rs", [128, 4], F32, cs_)
                S["rstd"] = sb("rstd", [128, 4], F32, cs_)
                TA[0], TA[1] = sb("TA", [128, 16, 512], BF16, cs_), fw.buf("TA")
                TB[0], TB[1] = sb("TB", [128, 16, 512], BF16, cs_), fw.buf("TB")
                if need_stg:
                    stg.items = [(sb("stg%d" % i, [128, 512], BF16, cs_), fw.buf("stg%d" % i)) for i in range(6)]
                u2T.items = [(sb("u2T%d" % i, [128, 16, 512], BF16, cs_), fw.buf("u2T%d" % i)) for i in range(2)]
                rtmp.items = [(sb("rtmp%d" % i, [128, 512], F32, cs_), fw.buf("rtmp%d" % i)) for i in range(2)]
            return cs_

        def load_w(key, wdram, k0, n0):
            t, b = wring.next()
            fw.dma(sp, t[:], wdram[k0:k0 + 2048, n0:n0 + 512].rearrange("(kc p) n -> p kc n", p=128),
                   owner=b, reads=[Bw[key]], writes=[b])
            return t, b

        evac_ctr = [0]

        def evac(out, in_, reads, writes):
            evac_ctr[0] += 1
            if evac_ctr[0] % 2:
                fw.op(act, lambda: nc.scalar.activation(out=out, in_=in_, func=AF.Copy), reads=reads, writes=writes)
            else:
                fw.op(dve, lambda: nc.vector.tensor_copy(out=out, in_=in_), reads=reads, writes=writes)

        def rms_rstd(src_tiles, eps_t, scale):
            for i in range(4):
                t, b = src_tiles[i]
                fw.op(act, lambda: nc.scalar.activation(out=S["junk"][:], in_=t[:], func=AF.Square,
                                                        accum_out=S["ss"][:, i:i + 1]),
                      reads=[b], writes=[Bjunk, Bss])
            fw.op(act, lambda: nc.scalar.activation(out=S["junk"][:, 0:8], in_=epsn[:].to_broadcast([128, 8]), func=AF.Copy),
                  reads=[Bc], writes=[Bjunk, Bss])
            fw.op(act, lambda: nc.scalar.activation(out=S["rs"][:], in_=S["ss"][:], func=AF.Sqrt, scale=scale,
                                                    bias=eps_t[:, 0:1]), reads=[Bss], writes=[Bss])
            fw.op(dve, lambda: nc.vector.reciprocal(out=S["rstd"][:], in_=S["rs"][:]), reads=[Bss], writes=[Bss])

        def transpose_to(dst, src_tiles, nchunks, colscale, col0):
            dT, dB = dst
            for kc in range(1 if 'onekc' in KB else nchunks):
                pt, pb = ptr.next()
                for i in range(4):
                    t, b = src_tiles[i]
                    fw.op(pe, lambda: nc.tensor.transpose(out=pt[:, i * 128:(i + 1) * 128],
                                                          in_=t[:, kc * 128:(kc + 1) * 128], identity=ident[:]),
                          reads=[b, Bc], writes=[pb])
                if 'noevac' in KB:
                    continue
                if colscale is None:
                    evac(dT[:, kc, :], pt, [pb], [dB])
                elif (kc % 2 or 'actonly' in KB) and 'dveonly' not in KB:
                    fw.op(act, lambda: nc.scalar.activation(out=dT[:, kc, :], in_=pt, func=(AF.Identity if 'ident' in KB else AF.Copy),
                                                            scale=colscale[:, col0 + kc:col0 + kc + 1]),
                          reads=[pb, Bc], writes=[dB])
                else:
                    fw.op(dve, lambda: nc.vector.tensor_scalar(out=dT[:, kc, :], in0=pt,
                                                               scalar1=colscale[:, col0 + kc:col0 + kc + 1],
                                                               scalar2=None, op0=ALU.mult),
                          reads=[pb, Bc], writes=[dB])

        def norm_T(dst, ncol):
            rms_rstd(xt, epsn, 1.0 / D)
            for i in range(4):
                fw.op(dve, lambda: nc.vector.tensor_scalar(out=hb[i][0][:], in0=xt[i][0][:],
                                                           scalar1=S["rstd"][:, i:i + 1], scalar2=None, op0=ALU.mult),
                      reads=[xt[i][1], Bss], writes=[hb[i][1]])
            transpose_to(dst, hb, 16, None if 'noscale' in KB else nrm_col, ncol * 16)

        def load_x(src, blk):
            for i in range(4):
                r0 = blk * 512 + i * 128
                fw.dma(sp, xt[i][0][:], src[r0:r0 + 128, :], owner=xt[i][1], writes=[xt[i][1]])


        def proj_featmajor(src, key, wdram, col_tiles, dsts, blk, post=None):
            sT, sB = src
            for wt in col_tiles:
                wtile, wbuf = load_w(key, wdram, 0, wt * 512)
                for oc in range(4):
                    pt, pb = pm.next()
                    for kc in range(16):
                        fw.op(pe, lambda: nc.tensor.matmul(pt[:], lhsT=wtile[:, kc, oc * 128:(oc + 1) * 128],
                                                           rhs=sT[:, kc, :], start=(kc == 0), stop=(kc == 15)),
                              reads=[wbuf, sB], writes=[pb])
                    post(wt, oc, pt, pb)

        def mlp(src, key_in, wdin, key_out, wdout):
            sT, sB = src
            for part in range(4):
                uT, uB = u2T.next()
                for wq in range(4):
                    wtile, wbuf = load_w(key_in, wdin, 0, part * 2048 + wq * 512)
                    for oc in range(4):
                        pt, pb = pm.next()
                        for kc in range(16):
                            fw.op(pe, lambda: nc.tensor.matmul(pt[:], lhsT=wtile[:, kc, oc * 128:(oc + 1) * 128],
                                                               rhs=sT[:, kc, :], start=(kc == 0), stop=(kc == 15)),
                                  reads=[wbuf, sB], writes=[pb])
                        rt, rb = rtmp.next()
                        fw.op(act, lambda: nc.scalar.activation(out=rt[:], in_=pt[:], func=AF.Relu),
                              reads=[pb], writes=[rb])
                        fw.op(pool, lambda: nc.gpsimd.tensor_tensor(out=uT[:, wq * 4 + oc, :], in0=rt[:], in1=rt[:],
                                                                    op=ALU.mult), reads=[rb], writes=[uB])
                for n in range(4):
                    wtile, wbuf = load_w(key_out, wdout, part * 2048, n * 512)
                    for i in range(4):
                        pt, pb = pm.next()
                        for kc in range(16):
                            fw.op(pe, lambda: nc.tensor.matmul(pt[:], lhsT=uT[:, kc, i * 128:(i + 1) * 128],
                                                               rhs=wtile[:, kc, :], start=(kc == 0), stop=(kc == 15)),
                                  reads=[wbuf, uB], writes=[pb])
                        fw.op(dve, lambda: nc.vector.tensor_tensor(out=xt[i][0][:, n * 512:(n + 1) * 512],
                                                                   in0=xt[i][0][:, n * 512:(n + 1) * 512],
                                                                   in1=pt[:], op=ALU.add),
                              reads=[pb, xt[i][1]], writes=[xt[i][1]])

        def proj_tokmajor_add(src, nk, key, wdram):
            sT_list = src
            for n in range(4):
                for kg in range(nk):
                    wtile, wbuf = load_w(key, wdram, kg * 2048, n * 512)
                    sT, sB = sT_list[kg]
                    for i in range(4):
                        pt, pb = pm.next()
                        for kc in range(16):
                            fw.op(pe, lambda: nc.tensor.matmul(pt[:], lhsT=sT[:, kc, i * 128:(i + 1) * 128],
                                                               rhs=wtile[:, kc, :], start=(kc == 0), stop=(kc == 15)),
                                  reads=[wbuf, sB], writes=[pb])
                        fw.op(dve, lambda: nc.vector.tensor_tensor(out=xt[i][0][:, n * 512:(n + 1) * 512],
                                                                   in0=xt[i][0][:, n * 512:(n + 1) * 512],
                                                                   in1=pt[:], op=ALU.add),
                              reads=[pb, xt[i][1]], writes=[xt[i][1]])

        cast_weight("qkv", w_qkv, wb_qkv)
        if "A" in phases:
            cmn = open_common(True)
            for blk in range(NBLK):
                t0 = blk * 512
                load_x(x_in, blk)
                if 'a1' in KB:
                    continue
                if 'a2' in KB:
                    rms_rstd(xt, epsn, 1.0 / D)
                    if 'a2b' in KB:
                        for i in range(4):
                            fw.op(dve, lambda: nc.vector.tensor_scalar(out=hb[i][0][:], in0=xt[i][0][:],
                                                                       scalar1=S["rstd"][:, i:i + 1], scalar2=None, op0=ALU.mult),
                                  reads=[xt[i][1], Bss], writes=[hb[i][1]])
                    continue
                norm_T(TA, 0)
                if 'a3' in KB:
                    continue

                def post_qk(wt, oc, pt, pb):
                    stt, stb = stg.next()
                    evac(stt[:], pt[:], [pb], [stb])
                    dst = s_qT if wt < 4 else s_kT
                    f0 = (wt % 4) * 512 + oc * 128
                    fw.dma(pool, dst[f0:f0 + 128, t0:t0 + 512], stt[:], owner=stb, reads=[stb])

                proj_featmajor(TA, "qkv", wb_qkv, range(8), None, blk, post=post_qk)
                for wt in range(8, 12):
                    wtile, wbuf = load_w("qkv", wb_qkv, 0, wt * 512)
                    for i in range(4):
                        pt, pb = pm.next()
                        for kc in range(16):
                            fw.op(pe, lambda: nc.tensor.matmul(pt[:], lhsT=TA[0][:, kc, i * 128:(i + 1) * 128],
                                                               rhs=wtile[:, kc, :], start=(kc == 0), stop=(kc == 15)),
                                  reads=[wbuf, TA[1]], writes=[pb])
                        stt, stb = stg.next()
                        evac(stt[:], pt[:], [pb], [stb])
                        fw.dma(pool, s_v[t0 + i * 128:t0 + (i + 1) * 128, (wt - 8) * 512:(wt - 7) * 512], stt[:],
                               owner=stb, reads=[stb])
            fw.barrier()
            cmn.close()

        cast_weight("o", w_o, wb_o)
        cast_weight("in0", w_in0, wb_in0)
        cast_weight("out0", w_out0, wb_out0)
        cast_weight("r", w_r, wb_r)

        if "B" in phases:
            with ExitStack() as sB_:
                rp = sb("rp", [64, 465], F32, sB_)
                rpe = sb("rpe", [64, 15, 31], BF16, sB_)
                zt = sb("zt", [128, 8 * 127], BF16, sB_)
                Brp = fw.buf("rp")
                Bz = fw.buf("zt")
                Berp = fw.buf("erp")
                BG = fw.buf("G")
                fw.dma(sp, rp[:], rpbf_d, owner=Brp, writes=[Brp])
                fw.op(act, lambda: nc.scalar.activation(out=rpe[:].rearrange("p a b -> p (a b)"), in_=rp[:], func=AF.Exp),
                      reads=[Brp], writes=[Brp])
                fw.op(dve, lambda: nc.vector.memset(zt[:], 0.0), writes=[Bz])
                fw.dma(sp, s_erp.rearrange("(p j) c -> p (j c)", p=128), zt[:], owner=Bz, reads=[Bz], writes=[Berp])
                fw.dma(sp, s_erp.rearrange("(h r) c -> h r c", r=16)[:, 0:15, 48:79], rpe[:], owner=Brp,
                       reads=[Brp, Berp], writes=[Berp])
                for kc in range(64):
                    fw.dma(sp, s_G[kc], s_erp[:, 63 - kc:127 - kc], owner=BG, reads=[Berp], writes=[BG])

                qh = Ring([(sb("qh%d" % i, [32, NTOK], BF16, sB_), fw.buf("qh%d" % i)) for i in range(2)])
                kh = Ring([(sb("kh%d" % i, [32, NTOK], BF16, sB_), fw.buf("kh%d" % i)) for i in range(2)])
                vraw = (sb("vraw", [128, 64, 128], BF16, sB_), fw.buf("vraw"))
                vaug = Ring([(sb("vaug%d" % i, [128, 64, 4, 34], BF16, sB_), fw.buf("vaug%d" % i)) for i in range(2)])
                G2 = Ring([(sb("G2_%d" % i, [128, 16, 64], BF16, sB_), fw.buf("G2_%d" % i)) for i in range(2)])
                Eint = Ring([(sb("Eint%d" % i, [128, 5, 128], BF16, sB_), fw.buf("Eint%d" % i)) for i in range(2)])
                Esp = Ring([(sb("Esp%d" % i, [128, 8, 7, 128], BF16, sB_), [fw.buf("Esp%d_%d" % (i, k)) for k in range(8)])
                            for i in range(2)])
                Etmp = Ring([(sb("Etmp%d" % i, [128, 7, 128], BF16, sB_), fw.buf("Etmp%d" % i)) for i in range(2)])
                exr = Ring([(sb("ex%d" % i, [128, 896], BF16, sB_), fw.buf("ex%d" % i)) for i in range(4)])
                pTr = Ring([(sb("pT%d" % i, [128, 896], BF16, sB_), fw.buf("pT%d" % i)) for i in range(4)])
                ostg = Ring([(sb("ostg%d" % i, [128, 64, 128], BF16, sB_), fw.buf("ostg%d" % i)) for i in range(2)])
                rec = Ring([(sb("rec%d" % i, [128, 8], F32, sB_), fw.buf("rec%d" % i)) for i in range(2)])
                psc_t = [ps("psc%d" % i, [128, 1024], F32, sB_) for i in range(3)]
                psc = Ring([(psc_t[i], fw.buf("psc%d" % i)) for i in range(3)])
                po = Ring([(ps("po%d" % i, [128, 8, 64], F32, sB_), fw.buf("po%d" % i)) for i in range(2)])
                cm_b = ctab[:, C_CMASK:C_CMASK + 64].unsqueeze(1).to_broadcast([128, 16, 64])
                for t_, b_ in vaug.items:
                    fw.op(pool, lambda: nc.gpsimd.memset(t_[:], 1.0), writes=[b_])

                def build_E(eng, dstT, dstB, entries, g2t, g2b, ops):
                    byjb = {}
                    for (j, a, b, rr) in entries:
                        byjb.setdefault((j, b), {})[a] = rr
                    for (j, b), d_ in sorted(byjb.items()):
                        if 0 in d_ and 1 in d_:
                            assert d_[1] == d_[0] + 1
                            o_ap = dstT[:, j, b * 64:(b + 1) * 64]
                            i_ap = g2t[:, d_[0], :]
                        elif 0 in d_:
                            o_ap = dstT[0:64, j, b * 64:(b + 1) * 64]
                            i_ap = g2t[0:64, d_[0], :]
                        else:
                            assert d_[1] >= 1
                            o_ap = dstT[64:128, j, b * 64:(b + 1) * 64]
                            i_ap = g2t[64:128, d_[1] - 1, :]
                        if eng is act:
                            ops.append(lambda o_ap=o_ap, i_ap=i_ap: fw.op(
                                act, lambda: nc.scalar.activation(out=o_ap, in_=i_ap, func=AF.Copy), reads=[g2b], writes=[dstB]))
                        else:
                            ops.append(lambda o_ap=o_ap, i_ap=i_ap: fw.op(
                                dve, lambda: nc.vector.tensor_copy(out=o_ap, in_=i_ap), reads=[g2b], writes=[dstB]))

                vstate = {}

                def prep_head(h):
                    g, hh = divmod(h, 4)
                    c = {}
                    ops = []
                    if hh == 0:
                        fw.dma(sp, vraw[0][:], s_v[:, g * 128:(g + 1) * 128].rearrange("(t p) d -> p t d", p=128),
                               owner=vraw[1], writes=[vraw[1]])
                        va, vab = vaug.next()
                        fw.op(pool, lambda: nc.gpsimd.tensor_copy(
                            out=va[:, :, :, 0:32], in_=vraw[0][:].rearrange("p t (h d) -> p t h d", h=4)),
                            reads=[vraw[1]], writes=[vab])
                        vstate["va"] = (va, vab)
                        vstate["og"] = ostg.next()
                    c["va"], c["vab"] = vstate["va"]
                    c["og"], c["ogb"] = vstate["og"]
                    qt_, qb_ = qh.next()
                    kt_, kb_ = kh.next()
                    fw.dma(sp, qt_[:], s_qT[h * 32:(h + 1) * 32, :], owner=qb_, writes=[qb_])
                    fw.dma(sp, kt_[:], s_kT[h * 32:(h + 1) * 32, :], owner=kb_, writes=[kb_])
                    c["q"], c["qb"], c["k"], c["kb"] = qt_, qb_, kt_, kb_
                    g2t, g2b = G2.next()
                    fw.dma(sp, g2t[0:64, :, :], s_G[:, h * 16:(h + 1) * 16, :], owner=g2b, reads=[BG], writes=[g2b])
                    fw.dma(sp, g2t[64:128, 0:15, :], s_G[:, h * 16 + 1:(h + 1) * 16, :], owner=g2b, reads=[BG], writes=[g2b])
                    fw.op(pool, lambda: nc.gpsimd.tensor_tensor(out=g2t[:, 0:15, :], in0=g2t[:, 0:15, :], in1=cm_b[:, 0:15, :], op=ALU.mult),
                          reads=[g2b, Bc], writes=[g2b])
                    ei, eib = Eint.next()
                    es, esbs = Esp.next()
                    fw.op(pool, lambda: nc.gpsimd.memset(ei[:], 0.0), writes=[eib])
                    fw.op(pool, lambda: nc.gpsimd.memset(es[:], 0.0), writes=list(esbs))
                    build_E(act, ei[:], eib, _pair_entries(10, _tiles_for(10), "joined"), g2t, g2b, ops)
                    for si, s in enumerate(SPECIAL):
                        tl = _tiles_for(s)
                        e_split = _pair_entries(s, tl, "split")
                        e_join = _pair_entries(s, tl, "joined")
                        if sorted(e_split) == sorted(e_join):
                            build_E(act, es[:, si], esbs[si], e_split, g2t, g2b, ops)
                        else:
                            et, etb = Etmp.next()
                            ops.append(lambda et=et, etb=etb: fw.op(pool, lambda: nc.gpsimd.memset(et[:], 0.0), writes=[etb]))
                            build_E(dve, es[:, si], esbs[si], e_split, g2t, g2b, ops)
                            build_E(act, et[:], etb, e_join, g2t, g2b, ops)
                            ops.append(lambda et=et, etb=etb, si=si: fw.op(
                                dve, lambda: nc.vector.tensor_tensor(out=et[:], in0=et[:], in1=es[:, si], op=ALU.subtract),
                                reads=[esbs[si], etb], writes=[etb]))
                            ops.append(lambda et=et, etb=etb, si=si: fw.op(
                                dve, lambda: nc.vector.scalar_tensor_tensor(out=es[:, si], in0=et[:], scalar=flag[:, 0:1],
                                                                            in1=es[:, si], op0=ALU.mult, op1=ALU.add),
                                reads=[etb, esbs[si], Bc], writes=[esbs[si]]))
                    c["ei"], c["eib"], c["es"], c["esb"] = ei, eib, es, esbs
                    return c, ops

                def scores(c, s):
                    tl = _tiles_for(s)
                    nt = len(tl)
                    if s in SPECIAL:
                        Et = c["es"][:, SPECIAL.index(s)].rearrange("p j q -> p (j q)")
                        Eb = c["esb"][SPECIAL.index(s)]
                    else:
                        Et = c["ei"][:].rearrange("p j q -> p (j q)")
                        Eb = c["eib"]
                    pst, psb = psc.next()
                    for j, t in enumerate(tl):
                        tt = t if t is not None else 0
                        fw.op(pe, lambda: nc.tensor.matmul(pst[:, j * 128:(j + 1) * 128],
                                                           lhsT=c["k"][:, tt * 128:(tt + 1) * 128],
                                                           rhs=c["q"][:, s * 128:(s + 1) * 128], start=True, stop=True),
                              reads=[c["kb"], c["qb"]], writes=[psb])
                    ext, exb = exr.next()
                    fw.op(act, lambda: nc.scalar.activation(out=ext[:, 0:nt * 128], in_=pst[:, 0:nt * 128],
                                                            func=AF.Exp, scale=NA_SCALE),
                          reads=[psb], writes=[exb])
                    ptt, ptb = pTr.next()
                    fw.op(dve, lambda: nc.vector.tensor_tensor(out=ptt[:, 0:nt * 128], in0=ext[:, 0:nt * 128], in1=Et,
                                                               op=ALU.mult), reads=[exb, Eb], writes=[ptb])
                    return ptt, ptb, tl

                pstate = {}

                def pv(c, s, hh, ptt, ptb, tl):
                    slot = s % 8
                    if slot == 0:
                        pstate["po"] = po.next()
                    pot, pob = pstate["po"]
                    js = [j for j, t in enumerate(tl) if t is not None]
                    for j in js:
                        fw.op(pe, lambda: nc.tensor.matmul(pot[:, slot, 0:34], lhsT=ptt[:, j * 128:(j + 1) * 128],
                                                           rhs=c["va"][:, tl[j], hh, :], start=(j == js[0]),
                                                           stop=(j == js[-1])),
                              reads=[ptb, c["vab"]], writes=[pob])
                    if slot == 7:
                        rt_, rb_ = rec.next()
                        fw.op(dve, lambda: nc.vector.reciprocal(out=rt_[:], in_=pot[:, :, 32]), reads=[pob], writes=[rb_])
                        fw.op(dve, lambda: nc.vector.tensor_tensor(
                            out=c["og"][:, s - 7:s + 1, hh * 32:(hh + 1) * 32], in0=pot[:, :, 0:32],
                            in1=rt_[:].unsqueeze(2).to_broadcast([128, 8, 32]), op=ALU.mult),
                            reads=[pob, rb_], writes=[c["ogb"]])

                nxt, ops0 = prep_head(0)
                for o_ in ops0:
                    o_()
                for h in range(64):
                    g, hh = divmod(h, 4)
                    c = nxt
                    pops = []
                    if h + 1 < 64:
                        nxt, pops = prep_head(h + 1)
                    per = (len(pops) + 59) // 60
                    pend = [scores(c, 0), scores(c, 1)]
                    for s in range(64):
                        cur = pend.pop(0)
                        if s + 2 < 64:
                            pend.append(scores(c, s + 2))
                        pv(c, s, hh, *cur)
                        for o_ in pops[:per]:
                            o_()
                        pops = pops[per:]
                    for o_ in pops:
                        o_()
                    if hh == 3:
                        fw.dma(pool, s_o[:, g * 128:(g + 1) * 128].rearrange("(t p) d -> p t d", p=128), c["og"][:],
                               owner=c["ogb"], reads=[c["ogb"]], ndesc=8192)
            fw.barrier()

        cast_weight("ro", w_ro, wb_ro)
        cast_weight("in1", w_in1, wb_in1)
        cast_weight("out1", w_out1, wb_out1)
        if "C" in phases:
            cmn = open_common(True)
            with ExitStack() as sC_:
                cs = [(sb("cos%d" % i, [128, 512], F32, sC_), sb("sin%d" % i, [128, 512], F32, sC_), fw.buf("cs%d" % i)) for i in range(2)]
                rot = Ring([(sb("rot%d" % i, [128, 512], F32, sC_), fw.buf("rot%d" % i)) for i in range(4)])
                qsave = Ring([(sb("qsv%d" % i, [128, 512], F32, sC_), fw.buf("qsv%d" % i)) for i in range(2)])
                gt = Ring([(sb("gt%d" % i, [128, 512], F32, sC_), fw.buf("gt%d" % i)) for i in range(2)])
                for blk in range(NBLK):
                    t0 = blk * 512
                    for i in range(4):
                        fw.dma(sp, hb[i][0][:], s_o[t0 + i * 128:t0 + (i + 1) * 128, :], owner=hb[i][1], writes=[hb[i][1]])
                    load_x(x_in, blk)
                    cst, snt, csb = cs[blk % 2]
                    fw.dma(sp, cst[:], cos_d[:, t0:t0 + 512], owner=csb, writes=[csb])
                    fw.dma(sp, snt[:], sin_d[:, t0:t0 + 512], owner=csb, writes=[csb])
                    transpose_to(TA, hb, 16, None, 0)
                    proj_tokmajor_add([TA], 1, "o", wb_o)
                    norm_T(TB, 1)
                    mlp(TB, "in0", wb_in0, "out0", wb_out0)
                    for i in range(4):
                        fw.dma(pool, s_x2[t0 + i * 128:t0 + (i + 1) * 128, :], xt[i][0][:], owner=xt[i][1], reads=[xt[i][1]])
                    norm_T(TA, 2)
                    saved = {}

                    def post_rqk(wt, oc, pt, pb):
                        chunk = (wt % 4) * 4 + oc
                        dst = s_rqT if wt < 4 else s_rkT
                        if chunk % 2 == 0:
                            qs, qsb = qsave.next()
                            evac(qs[:], pt[:], [pb], [qsb])
                            saved["x1"] = (qs, qsb)
                        else:
                            q1, q1b = saved["x1"]
                            t1, t1b = rot.next()
                            t2, t2b = rot.next()
                            fw.op(dve, lambda: nc.vector.tensor_tensor(out=t1[:], in0=pt[:], in1=snt[:], op=ALU.mult),
                                  reads=[pb, csb], writes=[t1b])
                            fw.op(dve, lambda: nc.vector.tensor_tensor(out=t2[:], in0=pt[:], in1=cst[:], op=ALU.mult),
                                  reads=[pb, csb], writes=[t2b])
                            t3, t3b = rot.next()
                            t4, t4b = rot.next()
                            fw.op(pool, lambda: nc.gpsimd.tensor_tensor(out=t3[:], in0=q1[:], in1=cst[:], op=ALU.mult),
                                  reads=[q1b, csb], writes=[t3b])
                            fw.op(pool, lambda: nc.gpsimd.tensor_tensor(out=t4[:], in0=q1[:], in1=snt[:], op=ALU.mult),
                                  reads=[q1b, csb], writes=[t4b])
                            s1, s1b = stg.next()
                            s2, s2b = stg.next()
                            fw.op(pool, lambda: nc.gpsimd.tensor_tensor(out=s1[:], in0=t3[:], in1=t1[:], op=ALU.subtract),
                                  reads=[t3b, t1b], writes=[s1b])
                            fw.op(pool, lambda: nc.gpsimd.tensor_tensor(out=s2[:], in0=t4[:], in1=t2[:], op=ALU.add),
                                  reads=[t4b, t2b], writes=[s2b])
                            f0 = (chunk - 1) * 128
                            fw.dma(pool, dst[f0:f0 + 128, t0:t0 + 512], s1[:], owner=s1b, reads=[s1b])
                            fw.dma(pool, dst[f0 + 128:f0 + 256, t0:t0 + 512], s2[:], owner=s2b, reads=[s2b])

                    proj_featmajor(TA, "r", wb_r, range(8), None, blk, post=post_rqk)
                    for wt in range(8, 24):
                        wtile, wbuf = load_w("r", wb_r, 0, wt * 512)
                        for i in range(4):
                            pt, pb = pm.next()
                            for kc in range(16):
                                fw.op(pe, lambda: nc.tensor.matmul(pt[:], lhsT=TA[0][:, kc, i * 128:(i + 1) * 128],
                                                                   rhs=wtile[:, kc, :], start=(kc == 0), stop=(kc == 15)),
                                      reads=[wbuf, TA[1]], writes=[pb])
                            stt, stb = stg.next()
                            if wt < 16:
                                evac(stt[:], pt[:], [pb], [stb])
                                fw.dma(pool, s_rv[t0 + i * 128:t0 + (i + 1) * 128, (wt - 8) * 512:(wt - 7) * 512], stt[:],
                                       owner=stb, reads=[stb])
                            else:
                                fw.op(act, lambda: nc.scalar.activation(out=stt[:], in_=pt[:], func=AF.Silu),
                                      reads=[pb], writes=[stb])
                                fw.dma(pool, s_rg[t0 + i * 128:t0 + (i + 1) * 128, (wt - 16) * 512:(wt - 15) * 512], stt[:],
                                       owner=stb, reads=[stb])
            fw.barrier()
            cmn.close()

        if "D" in phases:
            cmn = open_common(False, nptr=1)
            with ExitStack() as sD_:
                dec = sb("dec", [128, 16], F32, sD_)
                lg = sb("lg", [128, 16], F32, sD_)
                MT = sb("MT", [128, 8, 128], F32, sD_)
                mtmp = sb("mtmp", [128, 128], F32, sD_)
                qdf = sb("qdf", [128, 8, 128], F32, sD_)
                qdb = sb("qdb", [128, 8, 128], F32, sD_)
                kdf = sb("kdf", [128, 8], F32, sD_)
                kdb = sb("kdb", [128, 8], F32, sD_)
                cdf = sb("cdf", [128, 8], F32, sD_)
                cdb = sb("cdb", [128, 8], F32, sD_)
                Bt = fw.buf("rtabs")
                fw.dma(sp, dec[:], dec_d, owner=Bt, writes=[Bt])
                fw.op(act, lambda: nc.scalar.activation(out=lg[:], in_=dec[:], func=AF.Exp), reads=[Bt], writes=[Bt])
                fw.op(dve, lambda: nc.vector.tensor_scalar(out=lg[:], in0=lg[:], scalar1=-1.0, scalar2=None, op0=ALU.mult),
                      reads=[Bt], writes=[Bt])
                for hd in range(8):
                    lf = lg[:, hd:hd + 1]
                    lb = lg[:, 8 + hd:9 + hd]
                    fw.op(act, lambda: nc.scalar.activation(out=MT[:, hd, :], in_=ctab[:, C_DPOS:C_DPOS + 128], func=AF.Exp, scale=lf),
                          reads=[Bt, Bc], writes=[Bt])
                    fw.op(dve, lambda: nc.vector.scalar_tensor_tensor(out=MT[:, hd, :], in0=MT[:, hd, :], scalar=RET_SCALE,
                                                                      in1=ctab[:, C_MPOS:C_MPOS + 128], op0=ALU.mult, op1=ALU.mult),
                          reads=[Bt, Bc], writes=[Bt])
                    fw.op(act, lambda: nc.scalar.activation(out=mtmp[:], in_=ctab[:, C_DNEG:C_DNEG + 128], func=AF.Exp, scale=lb),
                          reads=[Bt, Bc], writes=[Bt])
                    fw.op(dve, lambda: nc.vector.scalar_tensor_tensor(out=mtmp[:], in0=mtmp[:], scalar=RET_SCALE,
                                                                      in1=ctab[:, C_MNEG:C_MNEG + 128], op0=ALU.mult, op1=ALU.mult),
                          reads=[Bt, Bc], writes=[Bt])
                    fw.op(dve, lambda: nc.vector.tensor_tensor(out=MT[:, hd, :], in0=MT[:, hd, :], in1=mtmp[:], op=ALU.add),
                          reads=[Bt], writes=[Bt])
                    fw.op(act, lambda: nc.scalar.activation(out=qdf[:, hd, :], in_=ctab[:, C_ROW1:C_ROW1 + 128], func=AF.Exp, scale=lf),
                          reads=[Bt, Bc], writes=[Bt])
                    fw.op(act, lambda: nc.scalar.activation(out=qdb[:, hd, :], in_=ctab[:, C_ROW2:C_ROW2 + 128], func=AF.Exp, scale=lb),
                          reads=[Bt, Bc], writes=[Bt])
                    fw.op(act, lambda: nc.scalar.activation(out=kdf[:, hd:hd + 1], in_=ctab[:, C_K1:C_K1 + 1], func=AF.Exp, scale=lf),
                          reads=[Bt, Bc], writes=[Bt])
                    fw.op(act, lambda: nc.scalar.activation(out=kdb[:, hd:hd + 1], in_=ctab[:, C_K2:C_K2 + 1], func=AF.Exp, scale=lb),
                          reads=[Bt, Bc], writes=[Bt])
                    fw.op(act, lambda: nc.scalar.activation(out=cdf[:, hd:hd + 1], in_=ctab[:, C_128:C_128 + 1], func=AF.Exp, scale=lf),
                          reads=[Bt, Bc], writes=[Bt])
                    fw.op(act, lambda: nc.scalar.activation(out=cdb[:, hd:hd + 1], in_=ctab[:, C_128:C_128 + 1], func=AF.Exp, scale=lb),
                          reads=[Bt, Bc], writes=[Bt])
                fw.op(dve, lambda: nc.vector.tensor_scalar(out=kdf[:], in0=kdf[:], scalar1=RET_SCALE, scalar2=None, op0=ALU.mult),
                      reads=[Bt], writes=[Bt])
                fw.op(dve, lambda: nc.vector.tensor_scalar(out=kdb[:], in0=kdb[:], scalar1=RET_SCALE, scalar2=None, op0=ALU.mult),
                      reads=[Bt], writes=[Bt])

                HS = []
                for u in range(2):
                    HS.append(dict(
                        grp=Ring([(sb("gq%d_%d" % (u, i), [128, 2, 1024], BF16, sD_), sb("gk%d_%d" % (u, i), [128, 2, 1024], BF16, sD_),
                                   sb("gv%d_%d" % (u, i), [128, 8, 512], BF16, sD_), fw.buf("grp%d_%d" % (u, i))) for i in range(2)]),
                        qsc=Ring([(sb("qsc%d_%d" % (u, i), [128, 2, 1024], BF16, sD_), fw.buf("qsc%d_%d" % (u, i))) for i in range(2)]),
                        state=(sb("state%d" % u, [128, 2, 512], F32, sD_), fw.buf("state%d" % u)),
                        stbf=(sb("stbf%d" % u, [128, 2, 512], BF16, sD_), fw.buf("stbf%d" % u)),
                        u=u))
                ktl = Ring([(sb("ktl%d" % i, [128, 256], BF16, sD_), fw.buf("ktl%d" % i)) for i in range(3)])
                smk = Ring([(sb("smk%d" % i, [128, 128], BF16, sD_), fw.buf("smk%d" % i)) for i in range(3)])
                ybo = Ring([(sb("ybo%d" % i, [128, 512], F32, sD_), fw.buf("ybo%d" % i)) for i in range(3)])
                ybi = Ring([(sb("ybi%d" % i, [128, 512], F32, sD_), fw.buf("ybi%d" % i)) for i in range(4)])
                ysum = Ring([(sb("ysum%d" % i, [128, 512], F32, sD_), fw.buf("ysum%d" % i)) for i in range(4)])
                yno = Ring([(sb("yno%d" % i, [128, 512], BF16, sD_), fw.buf("yno%d" % i)) for i in range(3)])
                gst = Ring([(sb("gst%d" % i, [128, 8, 2], F32, sD_), fw.buf("gst%d" % i)) for i in range(3)])
                pS = Ring([(pm.items[0][0][:, k * 128:(k + 1) * 128], fw.buf("pS%d" % k)) for k in range(4)])
                pY = Ring([pm.items[1], pm.items[2]])
                pSt = Ring([(pm.items[3][0], fw.buf("pstA")), (ps("pst1", [128, 512], F32, sD_), fw.buf("pstB")),
                            (ps("pst2", [128, 512], F32, sD_), fw.buf("pstC")), (ps("pst3", [128, 512], F32, sD_), fw.buf("pstD"))])
                ptr.items = [(ptr.items[0][0][:, i * 256:i * 256 + 256], fw.buf("ptrD%d" % i)) for i in range(2)]

                def load_group(hs, hd, gi):
                    gq, gk, gv, gb = hs["grp"].next()
                    c0 = gi * 1024
                    fw.dma(sp, gq[:], s_rqT[hd * 256:(hd + 1) * 256, c0:c0 + 1024].rearrange("(x p) t -> p x t", p=128),
                           owner=gb, writes=[gb])
                    fw.dma(sp, gk[:], s_rkT[hd * 256:(hd + 1) * 256, c0:c0 + 1024].rearrange("(x p) t -> p x t", p=128),
                           owner=gb, writes=[gb])
                    fw.dma(sp, gv[:], s_rv[c0:c0 + 1024, hd * 512:(hd + 1) * 512].rearrange("(c p) e -> p c e", p=128),
                           owner=gb, writes=[gb])
                    hs["g"] = (gq, gk, gv, gb)

                def scale_q(hs, qd_tab, hd):
                    gq, gk, gv, gb = hs["g"]
                    qs, qsb = hs["qsc"].next()
                    fw.op(pool, lambda: nc.gpsimd.tensor_tensor(
                        out=qs[:].rearrange("p x (c i) -> p (x c) i", i=128),
                        in0=gq[:].rearrange("p x (c i) -> p (x c) i", i=128),
                        in1=qd_tab[:, hd, :].unsqueeze(1).to_broadcast([128, 16, 128]), op=ALU.mult),
                        reads=[gb, Bt], writes=[qsb])
                    hs["qs"] = (qs, qsb)

                def state_T(hs, cl, kd, hd):
                    gq, gk, gv, gb = hs["g"]
                    ptk, ptkb = ptr.next()
                    for x in range(2):
                        fw.op(pe, lambda: nc.tensor.transpose(out=ptk[:, x * 128:(x + 1) * 128],
                                                              in_=gk[:, x, cl * 128:(cl + 1) * 128], identity=ident[:]),
                              reads=[gb, Bc], writes=[ptkb])
                    kt2, kt2b = ktl.next()
                    fw.op(act, lambda: nc.scalar.activation(out=kt2[:], in_=ptk[:, 0:256], func=AF.Copy, scale=kd[:, hd:hd + 1]),
                          reads=[ptkb, Bt], writes=[kt2b])
                    hs["kt2"] = (kt2, kt2b)

                def state_update(hs, cl, cd, hd):
                    gq, gk, gv, gb = hs["g"]
                    state, stbf = hs["state"], hs["stbf"]
                    kt2, kt2b = hs["kt2"]
                    for x in range(2):
                        pst_, pstb_ = pSt.next()
                        fw.op(pe, lambda: nc.tensor.matmul(pst_[:], lhsT=kt2[:, x * 128:(x + 1) * 128], rhs=gv[:, cl, :],
                                                           start=True, stop=True), reads=[kt2b, gb], writes=[pstb_])
                        fw.op(dve, lambda: nc.vector.scalar_tensor_tensor(out=state[0][:, x, :], in0=state[0][:, x, :],
                                                                          scalar=cd[:, hd:hd + 1], in1=pst_[:],
                                                                          op0=ALU.mult, op1=ALU.add),
                              reads=[pstb_, state[1], Bt], writes=[state[1]])
                    fw.op(act, lambda: nc.scalar.activation(out=stbf[0][:], in_=state[0][:], func=AF.Copy),
                          reads=[state[1]], writes=[stbf[1]])

                def reset_state(hs):
                    fw.op(dve, lambda: nc.vector.memset(hs["state"][0][:], 0.0), writes=[hs["state"][1]])
                    fw.op(pool, lambda: nc.gpsimd.memset(hs["stbf"][0][:], 0.0), writes=[hs["stbf"][1]])

                def carry_boundary(hs):
                    state, stbf = hs["state"], hs["stbf"]
                    fw.op(dve, lambda: nc.vector.tensor_scalar(out=state[0][:], in0=state[0][:], scalar1=flag[:, 0:1],
                                                               scalar2=None, op0=ALU.mult),
                          reads=[state[1], Bc], writes=[state[1]])
                    fw.op(act, lambda: nc.scalar.activation(out=stbf[0][:], in_=state[0][:], func=AF.Copy),
                          reads=[state[1]], writes=[stbf[1]])

                def su_mm(hs, cl):
                    gq, gk, gv, gb = hs["g"]
                    kt2, kt2b = hs["kt2"]
                    hs["pst"] = []
                    for x in range(2):
                        pst_, pstb_ = pSt.next()
                        fw.op(pe, lambda: nc.tensor.matmul(pst_[:], lhsT=kt2[:, x * 128:(x + 1) * 128], rhs=gv[:, cl, :],
                                                           start=True, stop=True), reads=[kt2b, gb], writes=[pstb_])
                        hs["pst"].append((pst_, pstb_))

                def su_acc(hs, cd, hd):
                    state = hs["state"]
                    for x in range(2):
                        pst_, pstb_ = hs["pst"][x]
                        fw.op(dve, lambda: nc.vector.scalar_tensor_tensor(out=state[0][:, x, :], in0=state[0][:, x, :],
                                                                          scalar=cd[:, hd:hd + 1], in1=pst_[:],
                                                                          op0=ALU.mult, op1=ALU.add),
                              reads=[pstb_, state[1], Bt], writes=[state[1]])

                def su_bf(hs):
                    state, stbf = hs["state"], hs["stbf"]
                    fw.op(act, lambda: nc.scalar.activation(out=stbf[0][:], in_=state[0][:], func=AF.Copy),
                          reads=[state[1]], writes=[stbf[1]])

                def y_inter(hs, cl, first):
                    qs, qsb = hs["qs"]
                    stbf = hs["stbf"]
                    yt, ytb = hs["yt"]
                    for x in range(2):
                        fw.op(pe, lambda: nc.tensor.matmul(yt[:], lhsT=qs[:, x, cl * 128:(cl + 1) * 128],
                                                           rhs=stbf[0][:, x, :], start=(first and x == 0), stop=(x == 1)),
                              reads=[qsb, stbf[1]], writes=[ytb])

                def bwd_round(heads, c, cl):
                    for hs, hd in heads:
                        if c == 31:
                            carry_boundary(hs)
                        state_T(hs, cl, kdb, hd)
                    for hs, hd in heads:
                        hs["yt"] = pY.next()
                        y_inter(hs, cl, True)
                        su_mm(hs, cl)
                    for hs, hd in heads:
                        su_acc(hs, cdb, hd)
                    for hs, hd in heads:
                        su_bf(hs)
                    for hs, hd in heads:
                        yt, ytb = hs["yt"]
                        yo, yob = ybo.next()
                        evac(yo[:], yt[:], [ytb], [yob])
                        fw.dma(pool, s_yb[c * 128:(c + 1) * 128, hs["u"] * 512:(hs["u"] + 1) * 512], yo[:], owner=yob, reads=[yob])

                def fwd_round(heads, c, cl):
                    rnd = {"gs": gst.next(), "items": []}
                    for hs, hd in heads:
                        gq, gk, gv, gb = hs["g"]
                        if c == 32:
                            carry_boundary(hs)
                        yi, yib = ybi.next()
                        fw.dma(sp, yi[:], s_yb[c * 128:(c + 1) * 128, hs["u"] * 512:(hs["u"] + 1) * 512], owner=yib, writes=[yib])
                        hs["yi"] = (yi, yib)
                        sT_, sTb = pS.next()
                        for x in range(2):
                            fw.op(pe, lambda: nc.tensor.matmul(sT_, lhsT=gk[:, x, cl * 128:(cl + 1) * 128],
                                                               rhs=gq[:, x, cl * 128:(cl + 1) * 128],
                                                               start=(x == 0), stop=(x == 1)),
                                  reads=[gb], writes=[sTb])
                        hs["sT"] = (sT_, sTb)
                        state_T(hs, cl, kdf, hd)
                    for hs, hd in heads:
                        gq, gk, gv, gb = hs["g"]
                        sT_, sTb = hs["sT"]
                        sm, smb = smk.next()
                        fw.op(dve, lambda: nc.vector.tensor_tensor(out=sm[:], in0=sT_, in1=MT[:, hd, :], op=ALU.mult),
                              reads=[sTb, Bt], writes=[smb])
                        hs["sm"] = (sm, smb)
                    for hs, hd in heads:
                        gq, gk, gv, gb = hs["g"]
                        sm, smb = hs["sm"]
                        hs["yt"] = pY.next()
                        yt, ytb = hs["yt"]
                        fw.op(pe, lambda: nc.tensor.matmul(yt[:], lhsT=sm[:], rhs=gv[:, cl, :], start=True, stop=False),
                              reads=[smb, gb], writes=[ytb])
                        y_inter(hs, cl, False)
                        su_mm(hs, cl)
                    for hs, hd in heads:
                        su_acc(hs, cdf, hd)
                    for hs, hd in heads:
                        su_bf(hs)
                    for hs, hd in heads:
                        yt, ytb = hs["yt"]
                        yi, yib = hs["yi"]
                        ysm, ysb = ysum.next()
                        fw.op(dve, lambda: nc.vector.tensor_tensor(out=ysm[:], in0=yt[:], in1=yi[:], op=ALU.add),
                              reads=[ytb, yib], writes=[ysb])
                        gs, gsb = rnd["gs"]
                        u = hs["u"]
                        fw.op(act, lambda: nc.scalar.activation(out=S["junk"][:, 0:512], in_=ysm[:], func=AF.Copy,
                                                                accum_out=gs[:, 0, u:u + 1]), reads=[ysb], writes=[Bjunk, gsb])
                        fw.op(act, lambda: nc.scalar.activation(out=S["junk"][:, 0:512], in_=ysm[:], func=AF.Square,
                                                                accum_out=gs[:, 1, u:u + 1]), reads=[ysb], writes=[Bjunk, gsb])
                        rnd["items"].append((hs, hd, c, ysm, ysb))
                    gn_tail(rnd)

                def gn_tail(rnd):
                    gs, gsb = rnd["gs"]
                    fw.op(act, lambda: nc.scalar.activation(out=S["junk"][:, 0:8], in_=epsg[:].to_broadcast([128, 8]), func=AF.Copy),
                          reads=[Bc], writes=[Bjunk, gsb])
                    fw.op(dve, lambda: nc.vector.tensor_scalar(out=gs[:, 2, :], in0=gs[:, 0, :], scalar1=1.0 / 512, scalar2=None,
                                                               op0=ALU.mult), reads=[gsb], writes=[gsb])
                    fw.op(dve, lambda: nc.vector.tensor_tensor(out=gs[:, 3, :], in0=gs[:, 2, :], in1=gs[:, 2, :], op=ALU.mult),
                          reads=[gsb], writes=[gsb])
                    fw.op(dve, lambda: nc.vector.scalar_tensor_tensor(out=gs[:, 4, :], in0=gs[:, 1, :], scalar=1.0 / 512,
                                                                      in1=gs[:, 3, :], op0=ALU.mult, op1=ALU.subtract),
                          reads=[gsb], writes=[gsb])
                    fw.op(act, lambda: nc.scalar.activation(out=gs[:, 5, :], in_=gs[:, 4, :], func=AF.Sqrt, bias=epsg[:, 0:1]),
                          reads=[gsb, Bc], writes=[gsb])
                    fw.op(dve, lambda: nc.vector.reciprocal(out=gs[:, 6, :], in_=gs[:, 5, :]), reads=[gsb], writes=[gsb])
                    fw.op(dve, lambda: nc.vector.scalar_tensor_tensor(out=gs[:, 7, :], in0=gs[:, 2, :], scalar=-1.0, in1=gs[:, 6, :],
                                                                      op0=ALU.mult, op1=ALU.mult), reads=[gsb], writes=[gsb])
                    for hs, hd, c, ysm, ysb in rnd["items"]:
                        u = hs["u"]
                        yn_, ynb = yno.next()
                        fw.op(pool, lambda: nc.gpsimd.tensor_scalar(out=yn_[:], in0=ysm[:], scalar1=gs[:, 6, u:u + 1],
                                                                    scalar2=gs[:, 7, u:u + 1], op0=ALU.mult, op1=ALU.add),
                              reads=[ysb, gsb], writes=[ynb])
                        fw.dma(pool, s_yn[c * 128:(c + 1) * 128, hd * 512:(hd + 1) * 512], yn_[:], owner=ynb, reads=[ynb])

                def prefetch(heads, gi, qd):
                    for hs, hd in heads:
                        cur_g, cur_q = hs.get("g"), hs.get("qs")
                        load_group(hs, hd, gi)
                        scale_q(hs, qd, hd)
                        hs["g_n"], hs["qs_n"] = hs["g"], hs["qs"]
                        hs["g"], hs["qs"] = cur_g, cur_q

                def swap_in(heads):
                    for hs, hd in heads:
                        hs["g"], hs["qs"] = hs["g_n"], hs["qs_n"]

                for hp in range(4):
                    heads = [(HS[0], 2 * hp), (HS[1], 2 * hp + 1)]
                    for hs, hd in heads:
                        reset_state(hs)
                    prefetch(heads, 7, qdb)
                    for gi in range(7, -1, -1):
                        swap_in(heads)
                        for cl in range(7, -1, -1):
                            if cl == 5 and gi > 0:
                                prefetch(heads, gi - 1, qdb)
                            bwd_round(heads, gi * 8 + cl, cl)
                    fw.barrier()
                    for hs, hd in heads:
                        reset_state(hs)
                    prefetch(heads, 0, qdf)
                    for gi in range(8):
                        swap_in(heads)
                        for cl in range(8):
                            if cl == 2 and gi < 7:
                                prefetch(heads, gi + 1, qdf)
                            fwd_round(heads, gi * 8 + cl, cl)
                    fw.barrier()
            cmn.close()

        if "E" in phases:
            cmn = open_common(True, need_stg=False)
            with ExitStack() as sE_:
                gg = [(sb("gg%d" % i, [128, D], BF16, sE_), fw.buf("gg%d" % i)) for i in range(4)]
                nfin = sb("nfin", [128, D], F32, sE_)
                fw.dma(sp, nfin[:], nrm_fin_d, owner=Bl, writes=[Bc])
                for blk in range(NBLK):
                    t0 = blk * 512
                    load_x(s_x2, blk)
                    for half, dstT in enumerate((TA, TB)):
                        for i in range(4):
                            r0 = t0 + i * 128
                            fw.dma(sp, hb[i][0][:], s_yn[r0:r0 + 128, half * D:(half + 1) * D], owner=hb[i][1], writes=[hb[i][1]])
                            fw.dma(sp, gg[i][0][:], s_rg[r0:r0 + 128, half * D:(half + 1) * D], owner=gg[i][1], writes=[gg[i][1]])
                        for i in range(4):
                            if i % 2:
                                fw.op(dve, lambda: nc.vector.tensor_tensor(out=hb[i][0][:], in0=hb[i][0][:], in1=gg[i][0][:], op=ALU.mult),
                                      reads=[hb[i][1], gg[i][1]], writes=[hb[i][1]])
                            else:
                                fw.op(pool, lambda: nc.gpsimd.tensor_tensor(out=hb[i][0][:], in0=hb[i][0][:], in1=gg[i][0][:], op=ALU.mult),
                                      reads=[hb[i][1], gg[i][1]], writes=[hb[i][1]])
                        transpose_to(dstT, hb, 16, gnw_col, half * 16)
                    proj_tokmajor_add([TA, TB], 2, "ro", wb_ro)
                    norm_T(TA, 3)
                    mlp(TA, "in1", wb_in1, "out1", wb_out1)
                    rms_rstd(xt, epsn, 1.0 / D)
                    for i in range(4):
                        ut, ub = u2T.items[i // 2]
                        ov = ut[:].rearrange("p a b -> p (a b)").bitcast(F32)[:, (i % 2) * D:(i % 2 + 1) * D]
                        fw.op(dve, lambda: nc.vector.scalar_tensor_tensor(out=ov, in0=xt[i][0][:], scalar=S["rstd"][:, i:i + 1],
                                                                          in1=nfin[:], op0=ALU.mult, op1=ALU.mult),
                              reads=[xt[i][1], Bss, Bc], writes=[ub])
                        fw.dma(pool, y_out[t0 + i * 128:t0 + (i + 1) * 128, :], ov, owner=ub, reads=[ub])
            fw.barrier()
            cmn.close()
        fw.barrier()
        if dbg and 'nodbg' not in KB:
            Bd = fw.buf("dbgout")
            for nm, t_, fm in (("qT", s_qT, True), ("kT", s_kT, True), ("v", s_v, False), ("o", s_o, False),
                               ("x2", s_x2, False), ("rqT", s_rqT, True), ("rkT", s_rkT, True), ("rv", s_rv, False),
                               ("rg", s_rg, False), ("yn", s_yn, False)):
                segs = [(0, 256), (1024, 1152), (1536, 1664), (3968, 4224)]
                tot = sum(e - b_ for b_, e in segs)
                if fm:
                    dd = nc.dram_tensor("dbg_" + nm, [t_.shape[0], tot], t_.dtype, kind="ExternalOutput").ap()
                else:
                    dd = nc.dram_tensor("dbg_" + nm, [tot, t_.shape[1]], t_.dtype, kind="ExternalOutput").ap()
                o_ = 0
                for b_, e in segs:
                    if fm:
                        fw.dma(sp, dd[:, o_:o_ + e - b_], t_[:, b_:e], owner=Bd)
                    else:
                        fw.dma(sp, dd[o_:o_ + e - b_, :], t_[b_:e, :], owner=Bd)
                    o_ += e - b_
            fw.barrier()
        stats = dict(nwaits=fw.nwaits, **{w.name: w.nseq for w in fw.engs})
    return nc, stats


def _const_tables():
    ct = np.zeros((128, NCTAB), np.float32)
    j = np.arange(128)[:, None].astype(np.float32)
    i = np.arange(128)[None, :].astype(np.float32)
    ct[:, C_DPOS:C_DPOS + 128] = np.maximum(i - j, 0)
    ct[:, C_DNEG:C_DNEG + 128] = np.maximum(j - i, 0)
    ct[:, C_MPOS:C_MPOS + 128] = (i >= j)
    ct[:, C_MNEG:C_MNEG + 128] = (j > i)
    ct[:, C_ROW1:C_ROW1 + 128] = np.broadcast_to(i + 1.0, (128, 128))
    ct[:, C_ROW2:C_ROW2 + 128] = np.broadcast_to(128.0 - i, (128, 128))
    ct[:, C_K1] = 127.0 - np.arange(128)
    ct[:, C_K2] = np.arange(128)
    ct[:, C_128] = 128.0
    kc = np.arange(64)[:, None]
    qc = np.arange(64)[None, :]
    cs = np.clip(qc - 8, 0, 48)
    cm = ((kc >= cs) & (kc < cs + 16)).astype(np.float32)
    ct[0:64, C_CMASK:C_CMASK + 64] = cm
    ct[64:128, C_CMASK:C_CMASK + 64] = cm
    return ct


def _rot_tables(pos):
    half = 128
    freqs = (np.float32(10000.0) ** (-np.arange(half, dtype=np.float32) / np.float32(half))).astype(np.float32)
    ang = pos.astype(np.float32)[None, :] * freqs[:, None]
    return np.cos(ang).astype(np.float32), np.sin(ang).astype(np.float32)


_CACHE = {}


def kernel(x_prompt, x_sample, norm_mix, na_w_qkv, na_rpb, na_w_o, ret_w_qkvg, ret_decay_fwd,
           ret_decay_bwd, ret_gn_w, ret_w_o, norm_mlp, mlp_w_in, mlp_w_out, norm_final):
    f = lambda a: np.ascontiguousarray(np.asarray(a, dtype=np.float32))
    x_prompt, x_sample = f(x_prompt), f(x_sample)
    if "nc" not in _CACHE:
        _CACHE["nc"] = build_program()[0]
    nc = _CACHE["nc"]
    nrm = np.stack([f(norm_mix)[0], f(norm_mlp)[0], f(norm_mix)[1], f(norm_mlp)[1]], 0)
    nrm_col = np.ascontiguousarray(nrm.reshape(4, 16, 128).transpose(2, 0, 1).reshape(128, 64))
    nrm_fin = np.ascontiguousarray(np.broadcast_to(f(norm_final)[None, :], (128, D)))
    gnw_col = np.ascontiguousarray(f(ret_gn_w)[0].reshape(32, 128).T)
    rpbf = np.ascontiguousarray(f(na_rpb)[0][:, :, ::-1].reshape(64, 465))
    dec = np.ascontiguousarray(np.broadcast_to(np.concatenate([f(ret_decay_fwd)[0], f(ret_decay_bwd)[0]])[None, :], (128, 16)))
    ctab = _const_tables()
    cos_j, sin_j = _rot_tables(np.arange(NTOK))
    cos_s, sin_s = _rot_tables(np.concatenate([np.arange(4096), np.arange(4096)]))
    common = {
        "w_qkv": f(na_w_qkv)[0], "w_o": f(na_w_o)[0], "w_in0": f(mlp_w_in)[0], "w_out0": f(mlp_w_out)[0],
        "w_r": f(ret_w_qkvg)[0], "w_ro": f(ret_w_o)[0], "w_in1": f(mlp_w_in)[1], "w_out1": f(mlp_w_out)[1],
        "nrm_col": nrm_col, "nrm_fin": nrm_fin, "gnw_col": gnw_col, "rpbf": rpbf, "dec": dec, "ctab": ctab,
    }
    xp = np.ascontiguousarray(x_prompt.reshape(NTOK, D))
    in_maps = []
    for c in range(NCORES):
        m = dict(common)
        if c < 4:
            m["x"] = x_sample[c]
            joined = 1.0
        else:
            m["x"] = xp
            joined = 0.0
        m["cosT"], m["sinT"] = (cos_j, sin_j) if joined else (cos_s, sin_s)
        m["flag"] = np.full((128, 1), joined, np.float32)
        in_maps.append(m)
    if _CACHE.get("return_maps"):
        return in_maps
    res = run_bass_kernel_spmd(nc, in_maps, core_ids=list(range(NCORES)))
    y_sample = np.stack([np.asarray(res.results[c]["y"], dtype=np.float32) for c in range(4)], 0)
    y_prompt = np.asarray(res.results[4]["y"], dtype=np.float32).reshape(2, 4096, D)
    return (y_prompt, y_sample)
```

```python
import math
from contextlib import ExitStack

import numpy as np
import concourse.bass as bass
import concourse.mybir as mybir
from concourse.bass_utils import run_bass_kernel_spmd

F32 = mybir.dt.float32
BF16 = mybir.dt.bfloat16
AF = mybir.ActivationFunctionType
ALU = mybir.AluOpType

D = 2048
NTOK = 8192
import os
NBLK = int(os.environ.get('KB_NBLK', '16'))
KB = os.environ.get('KB', '')
DFF = 8192
NCORES = 8
NORM_EPS = 1e-6
GN_EPS = 1e-5
NA_SCALE = 32 ** -0.5
RET_SCALE = 256 ** -0.5

C_DPOS, C_DNEG, C_MPOS, C_MNEG, C_ROW1, C_ROW2, C_K1, C_K2, C_128, C_CMASK = 0, 128, 256, 384, 512, 640, 768, 769, 770, 771
NCTAB = 771 + 64


class EngW:
    def __init__(self, name, e):
        self.name = name
        self.e = e
        self.sem = None
        self.count = 0
        self.nseq = 0
        self.sigs = []
        self.known = {}
        self.last_ins = None
        self.last_sig = True

    def signal_for(self, seq):
        lo, hi = 0, len(self.sigs)
        while lo < hi:
            mid = (lo + hi) // 2
            if self.sigs[mid][0] >= seq:
                hi = mid
            else:
                lo = mid + 1
        if lo < len(self.sigs):
            return self.sigs[lo][1]
        assert self.last_ins is not None and not self.last_sig, (self.name, seq, self.nseq)
        self.count += 1
        self.last_ins.then_inc(self.sem, 1)
        self.last_sig = True
        self.sigs.append((self.nseq - 1, self.count))
        return self.count


class Buf:
    __slots__ = ("name", "last_w", "reads", "dsem", "dval")

    def __init__(self, name):
        self.name = name
        self.last_w = None
        self.reads = []
        self.dsem = None
        self.dval = 0


class FW:
    def __init__(self, nc, stack):
        self.nc = nc
        self.stack = stack
        self.engs = []
        for name, e in (("pe", nc.tensor), ("act", nc.scalar), ("dve", nc.vector),
                        ("pool", nc.gpsimd), ("sp", nc.sync)):
            w = EngW(name, e)
            w.sem = stack.enter_context(nc.semaphore("sem_" + name))
            setattr(self, name, w)
            self.engs.append(w)
        self.nwaits = 0
        self.swq = []
        self.swq_total = 0
        self.bufs = {}
        self.dsems = []
        self.free_dsems = []

    def buf(self, name):
        if name not in self.bufs:
            self.bufs[name] = Buf(name)
        return self.bufs[name]

    def _resolve(self, acc):
        if acc[0] == 'e':
            return acc[1].sem, acc[1].signal_for(acc[2])
        return acc[1], acc[2]

    def _wait_deps(self, w, reads, writes, same_engine=False):
        deps = []
        for b in reads:
            if b.last_w is not None:
                deps.append((b.last_w, True))
        for b in writes:
            if b.last_w is not None:
                deps.append((b.last_w, False))
            for r in b.reads:
                deps.append((r, False))
        need = {}
        for acc, is_raw in deps:
            if acc[0] == 'e' and acc[1] is w and not same_engine:
                if not is_raw or w.name == "pe":
                    continue
            sem, val = self._resolve(acc)
            k = id(sem)
            if k not in need or need[k][1] < val:
                need[k] = (sem, val)
        for k, (sem, val) in need.items():
            if w.known.get(k, 0) >= val:
                continue
            w.e.wait_ge(sem, val)
            w.known[k] = val
            self.nwaits += 1

    def _record(self, acc, reads, writes):
        for b in writes:
            b.last_w = acc
            b.reads = []
        for b in reads:
            if acc[0] == 'e':
                b.reads = [r for r in b.reads if not (r[0] == 'e' and r[1] is acc[1])]
            else:
                b.reads = [r for r in b.reads if not (r[0] == 'd' and r[1] is acc[1])]
            b.reads.append(acc)

    def op(self, w, fn, reads=(), writes=()):
        self._wait_deps(w, reads, writes)
        ins = fn()
        w.last_ins = ins
        w.last_sig = False
        seq = w.nseq
        w.nseq += 1
        self._record(('e', w, seq), reads, writes)
        return ins

    def dma(self, q, out, in_, owner, reads=(), writes=(), ndesc=128, **kw):
        if (not q.last_sig) and q.last_ins is not None:
            q.signal_for(q.nseq - 1)
        self._wait_deps(q, reads, writes, same_engine=True)
        if owner.dsem is None:
            owner.dsem = self.stack.enter_context(self.nc.semaphore("ds_%s" % owner.name))
            self.dsems.append(owner)
        if q.name == "pool":
            per_eng = ndesc // 16 + 2
            while self.swq and self.swq_total + per_eng > 600:
                sem, val, n = self.swq.pop(0)
                self.swq_total -= n
                k = id(sem)
                if q.known.get(k, 0) < val:
                    q.e.wait_ge(sem, val)
                    q.known[k] = val
                    self.nwaits += 1
        owner.dval += 16
        ins = q.e.dma_start(out=out, in_=in_, **kw).then_inc(owner.dsem, 16)
        if q.name == "pool":
            self.swq.append((owner.dsem, owner.dval, per_eng))
            self.swq_total += per_eng
        q.last_ins = None
        q.last_sig = True
        q.nseq += 1
        self._record(('d', owner.dsem, owner.dval), reads, writes)
        return ins

    def barrier(self):
        for w in self.engs:
            if (not w.last_sig) and w.last_ins is not None:
                w.signal_for(w.nseq - 1)
        for w in self.engs:
            for b in self.dsems:
                k = id(b.dsem)
                if b.dval > 0 and w.known.get(k, 0) < b.dval:
                    w.e.wait_ge(b.dsem, b.dval)
                    w.known[k] = b.dval
                    self.nwaits += 1
            for w2 in self.engs:
                if w2 is w or w2.count == 0:
                    continue
                k = id(w2.sem)
                if w.known.get(k, 0) < w2.count:
                    w.e.wait_ge(w2.sem, w2.count)
                    w.known[k] = w2.count
                    self.nwaits += 1


class Ring:
    def __init__(self, items):
        self.items = items
        self.i = 0

    def next(self):
        it = self.items[self.i % len(self.items)]
        self.i += 1
        return it


def _rs(r, variant):
    if variant == "joined":
        return min(max(r - 4, 0), 120)
    if r < 64:
        return min(max(r - 4, 0), 56)
    return 64 + min(max(r - 64 - 4, 0), 56)


def _pair_entries(s, tiles, variant):
    out = []
    for j, t in enumerate(tiles):
        if t is None:
            continue
        for a in range(2):
            kr = 2 * t + a
            for b in range(2):
                r = 2 * s + b
                rs = _rs(r, variant)
                if rs <= kr <= rs + 7:
                    rr = kr - r + 7
                    assert 0 <= rr <= 14
                    out.append((j, a, b, rr))
    return out


SPECIAL = [0, 1, 30, 31, 32, 33, 62, 63]


def _tiles_for(s):
    if s in SPECIAL:
        return [t if 0 <= t <= 63 else None for t in range(s - 3, s + 4)]
    return list(range(s - 2, s + 3))


def build_program(phases="ABCDE", dbg=False):
    nc = bass.Bass("TRN2", target_bir_lowering=False)

    def din(name, shape, dt=F32):
        return nc.dram_tensor(name, list(shape), dt, kind="ExternalInput").ap()

    def dscr(name, shape, dt):
        return nc.dram_tensor(name, list(shape), dt, kind="Internal").ap()

    x_in = din("x", [NTOK, D])
    w_qkv = din("w_qkv", [D, 3 * D])
    w_o = din("w_o", [D, D])
    w_in0 = din("w_in0", [D, DFF])
    w_out0 = din("w_out0", [DFF, D])
    w_r = din("w_r", [D, 6 * D])
    w_ro = din("w_ro", [2 * D, D])
    w_in1 = din("w_in1", [D, DFF])
    w_out1 = din("w_out1", [DFF, D])
    nrm_col_d = din("nrm_col", [128, 64])
    nrm_fin_d = din("nrm_fin", [128, D])
    gnw_col_d = din("gnw_col", [128, 32])
    rpbf_d = din("rpbf", [64, 465])
    dec_d = din("dec", [128, 16])
    ctab_d = din("ctab", [128, NCTAB])
    cos_d = din("cosT", [128, NTOK])
    sin_d = din("sinT", [128, NTOK])
    flag_d = din("flag", [128, 1])
    y_out = nc.dram_tensor("y", [NTOK, D], F32, kind="ExternalOutput").ap()

    wb_qkv = nc.dram_tensor("wb_qkv", [D, 3 * D], BF16, kind="Internal").ap()
    wb_o = nc.dram_tensor("wb_o", [D, D], BF16, kind="Internal").ap()
    wb_in0 = nc.dram_tensor("wb_in0", [D, DFF], BF16, kind="Internal").ap()
    wb_out0 = nc.dram_tensor("wb_out0", [DFF, D], BF16, kind="Internal").ap()
    wb_r = nc.dram_tensor("wb_r", [D, 6 * D], BF16, kind="Internal").ap()
    wb_ro = nc.dram_tensor("wb_ro", [2 * D, D], BF16, kind="Internal").ap()
    wb_in1 = nc.dram_tensor("wb_in1", [D, DFF], BF16, kind="Internal").ap()
    wb_out1 = nc.dram_tensor("wb_out1", [DFF, D], BF16, kind="Internal").ap()

    s_qT = dscr("s_qT", [D, NTOK], BF16)
    s_kT = dscr("s_kT", [D, NTOK], BF16)
    s_v = dscr("s_v", [NTOK, D], BF16)
    s_o = dscr("s_o", [NTOK, D], BF16)
    s_x2 = dscr("s_x2", [NTOK, D], F32)
    s_rqT = dscr("s_rqT", [D, NTOK], BF16)
    s_rkT = dscr("s_rkT", [D, NTOK], BF16)
    s_rv = dscr("s_rv", [NTOK, 2 * D], BF16)
    s_rg = dscr("s_rg", [NTOK, 2 * D], BF16)
    s_yb = dscr("s_yb", [NTOK, 1024], F32)
    s_yn = dscr("s_yn", [NTOK, 2 * D], BF16)
    s_erp = nc.dram_tensor("s_erp", [1024, 127], BF16, kind="Internal").ap()
    s_G = nc.dram_tensor("s_G", [64, 1024, 64], BF16, kind="Internal").ap()

    with ExitStack() as st:
        fw = FW(nc, st)
        pe, act, dve, pool, sp = fw.pe, fw.act, fw.dve, fw.pool, fw.sp

        uid = [0]

        def sb(name, shape, dt, stack=st):
            uid[0] += 1
            return stack.enter_context(nc.sbuf_tensor("%s_%d" % (name, uid[0]), list(shape), dt))

        def ps(name, shape, dt, stack=st):
            uid[0] += 1
            return stack.enter_context(nc.psum_tensor("%s_%d" % (name, uid[0]), list(shape), dt))

        ident = sb("ident", [128, 128], BF16)
        identf = sb("identf", [128, 128], F32)
        nrm_col = sb("nrm_col_s", [128, 64], F32)
        gnw_col = sb("gnw_col_s", [128, 32], F32)
        flag = sb("flag_s", [128, 1], F32)
        ctab = sb("ctab_s", [128, NCTAB], F32)
        epsn = sb("epsn", [128, 1], F32)
        epsg = sb("epsg", [128, 1], F32)
        Bc = fw.buf("consts")
        fw.op(pool, lambda: nc.gpsimd.memset(identf[:], 0.0), writes=[Bc])
        fw.op(pool, lambda: nc.gpsimd.affine_select(out=identf[:], in_=identf[:], pattern=[[-1, 128]],
                                                     compare_op=ALU.not_equal, fill=1.0, base=0,
                                                     channel_multiplier=1), reads=[Bc], writes=[Bc])
        fw.op(pool, lambda: nc.gpsimd.memset(epsn[:], NORM_EPS), writes=[Bc])
        fw.op(pool, lambda: nc.gpsimd.memset(epsg[:], GN_EPS), writes=[Bc])
        fw.op(dve, lambda: nc.vector.tensor_copy(out=ident[:], in_=identf[:]), reads=[Bc], writes=[Bc])
        Bl = fw.buf("cload")
        for t_s, t_d in ((nrm_col, nrm_col_d), (gnw_col, gnw_col_d), (flag, flag_d), (ctab, ctab_d)):
            fw.dma(sp, t_s[:], t_d, owner=Bl, writes=[Bc])

        Bw = {}

        def cast_weight(key, src, dst):
            b = fw.buf("wc_" + key)
            Bw[key] = b
            if 'nocast2' in KB and key != 'qkv':
                return
            K, N = src.shape
            for k0 in range(0, K, 2048):
                for n0 in range(0, N, 2048):
                    fw.dma(pool, dst[k0:k0 + 2048, n0:n0 + 2048], src[k0:k0 + 2048, n0:n0 + 2048],
                           owner=b, writes=[b], ndesc=4096)

        wring = Ring([])
        pm = Ring([])
        ptr = Ring([])
        stg = Ring([])
        u2T = Ring([])
        rtmp = Ring([])
        xt = []
        hb = []
        TA = [None, None]
        TB = [None, None]
        S = {}
        Bjunk = fw.buf("junk")
        Bss = fw.buf("ss")

        def open_common(full, need_stg=True, nptr=4):
            cs_ = ExitStack()
            pm.items = [(ps("pm%d" % i, [128, 512], F32, cs_), fw.buf("pm%d" % i)) for i in range(4)]
            ptr_t = [ps("ptr%d" % i, [128, 512], BF16, cs_) for i in range(nptr)]
            ptr.items = [(ptr_t[i][:, :], fw.buf("ptr%d" % i)) for i in range(nptr)]
            S["junk"] = sb("junk", [128, D], BF16, cs_)
            if full:
                wring.items = [(sb("wt%d" % i, [128, 16, 512], BF16, cs_), fw.buf("wt%d" % i)) for i in range(3)]
                xt[:] = [(sb("xt%d" % i, [128, D], F32, cs_), fw.buf("xt%d" % i)) for i in range(4)]
                hb[:] = [(sb("hb%d" % i, [128, D], BF16, cs_), fw.buf("hb%d" % i)) for i in range(4)]
                S["ss"] = sb("ss", [128, 4], F32, cs_)
                S["rs"] = sb("rs", [128, 4], F32, cs_)
                S["rstd"] = sb("rstd", [128, 4], F32, cs_)
                TA[0], TA[1] = sb("TA", [128, 16, 512], BF16, cs_), fw.buf("TA")
                TB[0], TB[1] = sb("TB", [128, 16, 512], BF16, cs_), fw.buf("TB")
                if need_stg:
                    stg.items = [(sb("stg%d" % i, [128, 512], BF16, cs_), fw.buf("stg%d" % i)) for i in range(6)]
                u2T.items = [(sb("u2T%d" % i, [128, 16, 512], BF16, cs_), fw.buf("u2T%d" % i)) for i in range(2)]
                rtmp.items = [(sb("rtmp%d" % i, [128, 512], F32, cs_), fw.buf("rtmp%d" % i)) for i in range(2)]
            return cs_

        def load_w(key, wdram, k0, n0):
            t, b = wring.next()
            fw.dma(sp, t[:], wdram[k0:k0 + 2048, n0:n0 + 512].rearrange("(kc p) n -> p kc n", p=128),
                   owner=b, reads=[Bw[key]], writes=[b])
            return t, b

        evac_ctr = [0]

        def evac(out, in_, reads, writes):
            evac_ctr[0] += 1
            if evac_ctr[0] % 2:
                fw.op(act, lambda: nc.scalar.activation(out=out, in_=in_, func=AF.Copy), reads=reads, writes=writes)
            else:
                fw.op(dve, lambda: nc.vector.tensor_copy(out=out, in_=in_), reads=reads, writes=writes)

        def rms_rstd(src_tiles, eps_t, scale):
            for i in range(4):
                t, b = src_tiles[i]
                fw.op(act, lambda: nc.scalar.activation(out=S["junk"][:], in_=t[:], func=AF.Square,
                                                        accum_out=S["ss"][:, i:i + 1]),
                      reads=[b], writes=[Bjunk, Bss])
            fw.op(act, lambda: nc.scalar.activation(out=S["junk"][:, 0:8], in_=epsn[:].to_broadcast([128, 8]), func=AF.Copy),
                  reads=[Bc], writes=[Bjunk, Bss])
            fw.op(act, lambda: nc.scalar.activation(out=S["rs"][:], in_=S["ss"][:], func=AF.Sqrt, scale=scale,
                                                    bias=eps_t[:, 0:1]), reads=[Bss], writes=[Bss])
            fw.op(dve, lambda: nc.vector.reciprocal(out=S["rstd"][:], in_=S["rs"][:]), reads=[Bss], writes=[Bss])

        def transpose_to(dst, src_tiles, nchunks, colscale, col0):
            dT, dB = dst
            for kc in range(1 if 'onekc' in KB else nchunks):
                pt, pb = ptr.next()
                for i in range(4):
                    t, b = src_tiles[i]
                    fw.op(pe, lambda: nc.tensor.transpose(out=pt[:, i * 128:(i + 1) * 128],
                                                          in_=t[:, kc * 128:(kc + 1) * 128], identity=ident[:]),
                          reads=[b, Bc], writes=[pb])
                if 'noevac' in KB:
                    continue
                if colscale is None:
                    evac(dT[:, kc, :], pt, [pb], [dB])
                elif (kc % 2 or 'actonly' in KB) and 'dveonly' not in KB:
                    fw.op(act, lambda: nc.scalar.activation(out=dT[:, kc, :], in_=pt, func=(AF.Identity if 'ident' in KB else AF.Copy),
                                                            scale=colscale[:, col0 + kc:col0 + kc + 1]),
                          reads=[pb, Bc], writes=[dB])
                else:
                    fw.op(dve, lambda: nc.vector.tensor_scalar(out=dT[:, kc, :], in0=pt,
                                                               scalar1=colscale[:, col0 + kc:col0 + kc + 1],
                                                               scalar2=None, op0=ALU.mult),
                          reads=[pb, Bc], writes=[dB])

        def norm_T(dst, ncol):
            rms_rstd(xt, epsn, 1.0 / D)
            for i in range(4):
                fw.op(dve, lambda: nc.vector.tensor_scalar(out=hb[i][0][:], in0=xt[i][0][:],
                                                           scalar1=S["rstd"][:, i:i + 1], scalar2=None, op0=ALU.mult),
                      reads=[xt[i][1], Bss], writes=[hb[i][1]])
            transpose_to(dst, hb, 16, None if 'noscale' in KB else nrm_col, ncol * 16)

        def load_x(src, blk):
            for i in range(4):
                r0 = blk * 512 + i * 128
                fw.dma(sp, xt[i][0][:], src[r0:r0 + 128, :], owner=xt[i][1], writes=[xt[i][1]])


        def proj_featmajor(src, key, wdram, col_tiles, dsts, blk, post=None):
            sT, sB = src
            for wt in col_tiles:
                wtile, wbuf = load_w(key, wdram, 0, wt * 512)
                for oc in range(4):
                    pt, pb = pm.next()
                    for kc in range(16):
                        fw.op(pe, lambda: nc.tensor.matmul(pt[:], lhsT=wtile[:, kc, oc * 128:(oc + 1) * 128],
                                                           rhs=sT[:, kc, :], start=(kc == 0), stop=(kc == 15)),
                              reads=[wbuf, sB], writes=[pb])
                    post(wt, oc, pt, pb)

        def mlp(src, key_in, wdin, key_out, wdout):
            sT, sB = src
            for part in range(4):
                uT, uB = u2T.next()
                for wq in range(4):
                    wtile, wbuf = load_w(key_in, wdin, 0, part * 2048 + wq * 512)
                    for oc in range(4):
                        pt, pb = pm.next()
                        for kc in range(16):
                            fw.op(pe, lambda: nc.tensor.matmul(pt[:], lhsT=wtile[:, kc, oc * 128:(oc + 1) * 128],
                                                               rhs=sT[:, kc, :], start=(kc == 0), stop=(kc == 15)),
                                  reads=[wbuf, sB], writes=[pb])
                        rt, rb = rtmp.next()
                        fw.op(act, lambda: nc.scalar.activation(out=rt[:], in_=pt[:], func=AF.Relu),
                              reads=[pb], writes=[rb])
                        fw.op(pool, lambda: nc.gpsimd.tensor_tensor(out=uT[:, wq * 4 + oc, :], in0=rt[:], in1=rt[:],
                                                                    op=ALU.mult), reads=[rb], writes=[uB])
                for n in range(4):
                    wtile, wbuf = load_w(key_out, wdout, part * 2048, n * 512)
                    for i in range(4):
                        pt, pb = pm.next()
                        for kc in range(16):
                            fw.op(pe, lambda: nc.tensor.matmul(pt[:], lhsT=uT[:, kc, i * 128:(i + 1) * 128],
                                                               rhs=wtile[:, kc, :], start=(kc == 0), stop=(kc == 15)),
                                  reads=[wbuf, uB], writes=[pb])
                        fw.op(dve, lambda: nc.vector.tensor_tensor(out=xt[i][0][:, n * 512:(n + 1) * 512],
                                                                   in0=xt[i][0][:, n * 512:(n + 1) * 512],
                                                                   in1=pt[:], op=ALU.add),
                              reads=[pb, xt[i][1]], writes=[xt[i][1]])

        def proj_tokmajor_add(src, nk, key, wdram):
            sT_list = src
            for n in range(4):
                for kg in range(nk):
                    wtile, wbuf = load_w(key, wdram, kg * 2048, n * 512)
                    sT, sB = sT_list[kg]
                    for i in range(4):
                        pt, pb = pm.next()
                        for kc in range(16):
                            fw.op(pe, lambda: nc.tensor.matmul(pt[:], lhsT=sT[:, kc, i * 128:(i + 1) * 128],
                                                               rhs=wtile[:, kc, :], start=(kc == 0), stop=(kc == 15)),
                                  reads=[wbuf, sB], writes=[pb])
                        fw.op(dve, lambda: nc.vector.tensor_tensor(out=xt[i][0][:, n * 512:(n + 1) * 512],
                                                                   in0=xt[i][0][:, n * 512:(n + 1) * 512],
                                                                   in1=pt[:], op=ALU.add),
                              reads=[pb, xt[i][1]], writes=[xt[i][1]])

        cast_weight("qkv", w_qkv, wb_qkv)
        if "A" in phases:
            cmn = open_common(True)
            for blk in range(NBLK):
                t0 = blk * 512
                load_x(x_in, blk)
                if 'a1' in KB:
                    continue
                if 'a2' in KB:
                    rms_rstd(xt, epsn, 1.0 / D)
                    if 'a2b' in KB:
                        for i in range(4):
                            fw.op(dve, lambda: nc.vector.tensor_scalar(out=hb[i][0][:], in0=xt[i][0][:],
                                                                       scalar1=S["rstd"][:, i:i + 1], scalar2=None, op0=ALU.mult),
                                  reads=[xt[i][1], Bss], writes=[hb[i][1]])
                    continue
                norm_T(TA, 0)
                if 'a3' in KB:
                    continue

                def post_qk(wt, oc, pt, pb):
                    stt, stb = stg.next()
                    evac(stt[:], pt[:], [pb], [stb])
                    dst = s_qT if wt < 4 else s_kT
                    f0 = (wt % 4) * 512 + oc * 128
                    fw.dma(pool, dst[f0:f0 + 128, t0:t0 + 512], stt[:], owner=stb, reads=[stb])

                proj_featmajor(TA, "qkv", wb_qkv, range(8), None, blk, post=post_qk)
                for wt in range(8, 12):
                    wtile, wbuf = load_w("qkv", wb_qkv, 0, wt * 512)
                    for i in range(4):
                        pt, pb = pm.next()
                        for kc in range(16):
                            fw.op(pe, lambda: nc.tensor.matmul(pt[:], lhsT=TA[0][:, kc, i * 128:(i + 1) * 128],
                                                               rhs=wtile[:, kc, :], start=(kc == 0), stop=(kc == 15)),
                                  reads=[wbuf, TA[1]], writes=[pb])
                        stt, stb = stg.next()
                        evac(stt[:], pt[:], [pb], [stb])
                        fw.dma(pool, s_v[t0 + i * 128:t0 + (i + 1) * 128, (wt - 8) * 512:(wt - 7) * 512], stt[:],
                               owner=stb, reads=[stb])
            fw.barrier()
            cmn.close()

        cast_weight("o", w_o, wb_o)
        cast_weight("in0", w_in0, wb_in0)
        cast_weight("out0", w_out0, wb_out0)
        cast_weight("r", w_r, wb_r)

        if "B" in phases:
            with ExitStack() as sB_:
                rp = sb("rp", [64, 465], F32, sB_)
                rpe = sb("rpe", [64, 15, 31], BF16, sB_)
                zt = sb("zt", [128, 8 * 127], BF16, sB_)
                Brp = fw.buf("rp")
                Bz = fw.buf("zt")
                Berp = fw.buf("erp")
                BG = fw.buf("G")
                fw.dma(sp, rp[:], rpbf_d, owner=Brp, writes=[Brp])
                fw.op(act, lambda: nc.scalar.activation(out=rpe[:].rearrange("p a b -> p (a b)"), in_=rp[:], func=AF.Exp),
                      reads=[Brp], writes=[Brp])
                fw.op(dve, lambda: nc.vector.memset(zt[:], 0.0), writes=[Bz])
                fw.dma(sp, s_erp.rearrange("(p j) c -> p (j c)", p=128), zt[:], owner=Bz, reads=[Bz], writes=[Berp])
                fw.dma(sp, s_erp.rearrange("(h r) c -> h r c", r=16)[:, 0:15, 48:79], rpe[:], owner=Brp,
                       reads=[Brp, Berp], writes=[Berp])
                for kc in range(64):
                    fw.dma(sp, s_G[kc], s_erp[:, 63 - kc:127 - kc], owner=BG, reads=[Berp], writes=[BG])

                qh = Ring([(sb("qh%d" % i, [32, NTOK], BF16, sB_), fw.buf("qh%d" % i)) for i in range(2)])
                kh = Ring([(sb("kh%d" % i, [32, NTOK], BF16, sB_), fw.buf("kh%d" % i)) for i in range(2)])
                vraw = (sb("vraw", [128, 64, 128], BF16, sB_), fw.buf("vraw"))
                vaug = Ring([(sb("vaug%d" % i, [128, 64, 4, 34], BF16, sB_), fw.buf("vaug%d" % i)) for i in range(2)])
                G2 = Ring([(sb("G2_%d" % i, [128, 16, 64], BF16, sB_), fw.buf("G2_%d" % i)) for i in range(2)])
                Eint = Ring([(sb("Eint%d" % i, [128, 5, 128], BF16, sB_), fw.buf("Eint%d" % i)) for i in range(2)])
                Esp = Ring([(sb("Esp%d" % i, [128, 8, 7, 128], BF16, sB_), [fw.buf("Esp%d_%d" % (i, k)) for k in range(8)])
                            for i in range(2)])
                Etmp = Ring([(sb("Etmp%d" % i, [128, 7, 128], BF16, sB_), fw.buf("Etmp%d" % i)) for i in range(2)])
                exr = Ring([(sb("ex%d" % i, [128, 896], BF16, sB_), fw.buf("ex%d" % i)) for i in range(4)])
                pTr = Ring([(sb("pT%d" % i, [128, 896], BF16, sB_), fw.buf("pT%d" % i)) for i in range(4)])
                ostg = Ring([(sb("ostg%d" % i, [128, 64, 128], BF16, sB_), fw.buf("ostg%d" % i)) for i in range(2)])
                rec = Ring([(sb("rec%d" % i, [128, 8], F32, sB_), fw.buf("rec%d" % i)) for i in range(2)])
                psc_t = [ps("psc%d" % i, [128, 1024], F32, sB_) for i in range(3)]
                psc = Ring([(psc_t[i], fw.buf("psc%d" % i)) for i in range(3)])
                po = Ring([(ps("po%d" % i, [128, 8, 64], F32, sB_), fw.buf("po%d" % i)) for i in range(2)])
                cm_b = ctab[:, C_CMASK:C_CMASK + 64].unsqueeze(1).to_broadcast([128, 16, 64])
                for t_, b_ in vaug.items:
                    fw.op(pool, lambda: nc.gpsimd.memset(t_[:], 1.0), writes=[b_])

                def build_E(eng, dstT, dstB, entries, g2t, g2b, ops):
                    byjb = {}
                    for (j, a, b, rr) in entries:
                        byjb.setdefault((j, b), {})[a] = rr
                    for (j, b), d_ in sorted(byjb.items()):
                        if 0 in d_ and 1 in d_:
                            assert d_[1] == d_[0] + 1
                            o_ap = dstT[:, j, b * 64:(b + 1) * 64]
                            i_ap = g2t[:, d_[0], :]
                        elif 0 in d_:
                            o_ap = dstT[0:64, j, b * 64:(b + 1) * 64]
                            i_ap = g2t[0:64, d_[0], :]
                        else:
                            assert d_[1] >= 1
                            o_ap = dstT[64:128, j, b * 64:(b + 1) * 64]
                            i_ap = g2t[64:128, d_[1] - 1, :]
                        if eng is act:
                            ops.append(lambda o_ap=o_ap, i_ap=i_ap: fw.op(
                                act, lambda: nc.scalar.activation(out=o_ap, in_=i_ap, func=AF.Copy), reads=[g2b], writes=[dstB]))
                        else:
                            ops.append(lambda o_ap=o_ap, i_ap=i_ap: fw.op(
                                dve, lambda: nc.vector.tensor_copy(out=o_ap, in_=i_ap), reads=[g2b], writes=[dstB]))

                vstate = {}

                def prep_head(h):
                    g, hh = divmod(h, 4)
                    c = {}
                    ops = []
                    if hh == 0:
                        fw.dma(sp, vraw[0][:], s_v[:, g * 128:(g + 1) * 128].rearrange("(t p) d -> p t d", p=128),
                               owner=vraw[1], writes=[vraw[1]])
                        va, vab = vaug.next()
                        fw.op(pool, lambda: nc.gpsimd.tensor_copy(
                            out=va[:, :, :, 0:32], in_=vraw[0][:].rearrange("p t (h d) -> p t h d", h=4)),
                            reads=[vraw[1]], writes=[vab])
                        vstate["va"] = (va, vab)
                        vstate["og"] = ostg.next()
                    c["va"], c["vab"] = vstate["va"]
                    c["og"], c["ogb"] = vstate["og"]
                    qt_, qb_ = qh.next()
                    kt_, kb_ = kh.next()
                    fw.dma(sp, qt_[:], s_qT[h * 32:(h + 1) * 32, :], owner=qb_, writes=[qb_])
                    fw.dma(sp, kt_[:], s_kT[h * 32:(h + 1) * 32, :], owner=kb_, writes=[kb_])
                    c["q"], c["qb"], c["k"], c["kb"] = qt_, qb_, kt_, kb_
                    g2t, g2b = G2.next()
                    fw.dma(sp, g2t[0:64, :, :], s_G[:, h * 16:(h + 1) * 16, :], owner=g2b, reads=[BG], writes=[g2b])
                    fw.dma(sp, g2t[64:128, 0:15, :], s_G[:, h * 16 + 1:(h + 1) * 16, :], owner=g2b, reads=[BG], writes=[g2b])
                    fw.op(pool, lambda: nc.gpsimd.tensor_tensor(out=g2t[:, 0:15, :], in0=g2t[:, 0:15, :], in1=cm_b[:, 0:15, :], op=ALU.mult),
                          reads=[g2b, Bc], writes=[g2b])
                    ei, eib = Eint.next()
                    es, esbs = Esp.next()
                    fw.op(pool, lambda: nc.gpsimd.memset(ei[:], 0.0), writes=[eib])
                    fw.op(pool, lambda: nc.gpsimd.memset(es[:], 0.0), writes=list(esbs))
                    build_E(dve, ei[:], eib, _pair_entries(10, _tiles_for(10), "joined"), g2t, g2b, ops)
                    for si, s in enumerate(SPECIAL):
                        tl = _tiles_for(s)
                        e_split = _pair_entries(s, tl, "split")
                        e_join = _pair_entries(s, tl, "joined")
                        if sorted(e_split) == sorted(e_join):
                            build_E(dve, es[:, si], esbs[si], e_split, g2t, g2b, ops)
                        else:
                            et, etb = Etmp.next()
                            ops.append(lambda et=et, etb=etb: fw.op(pool, lambda: nc.gpsimd.memset(et[:], 0.0), writes=[etb]))
                            build_E(dve, es[:, si], esbs[si], e_split, g2t, g2b, ops)
                            build_E(dve, et[:], etb, e_join, g2t, g2b, ops)
                            ops.append(lambda et=et, etb=etb, si=si: fw.op(
                                dve, lambda: nc.vector.tensor_tensor(out=et[:], in0=et[:], in1=es[:, si], op=ALU.subtract),
                                reads=[esbs[si], etb], writes=[etb]))
                            ops.append(lambda et=et, etb=etb, si=si: fw.op(
                                dve, lambda: nc.vector.scalar_tensor_tensor(out=es[:, si], in0=et[:], scalar=flag[:, 0:1],
                                                                            in1=es[:, si], op0=ALU.mult, op1=ALU.add),
                                reads=[etb, esbs[si], Bc], writes=[esbs[si]]))
                    c["ei"], c["eib"], c["es"], c["esb"] = ei, eib, es, esbs
                    return c, ops

                def scores(c, s):
                    tl = _tiles_for(s)
                    nt = len(tl)
                    if s in SPECIAL:
                        Et = c["es"][:, SPECIAL.index(s)].rearrange("p j q -> p (j q)")
                        Eb = c["esb"][SPECIAL.index(s)]
                    else:
                        Et = c["ei"][:].rearrange("p j q -> p (j q)")
                        Eb = c["eib"]
                    pst, psb = psc.next()
                    for j, t in enumerate(tl):
                        tt = t if t is not None else 0
                        fw.op(pe, lambda: nc.tensor.matmul(pst[:, j * 128:(j + 1) * 128],
                                                           lhsT=c["k"][:, tt * 128:(tt + 1) * 128],
                                                           rhs=c["q"][:, s * 128:(s + 1) * 128], start=True, stop=True),
                              reads=[c["kb"], c["qb"]], writes=[psb])
                    ext, exb = exr.next()
                    fw.op(act, lambda: nc.scalar.activation(out=ext[:, 0:nt * 128], in_=pst[:, 0:nt * 128],
                                                            func=AF.Exp, scale=NA_SCALE),
                          reads=[psb], writes=[exb])
                    ptt, ptb = pTr.next()
                    fw.op(dve, lambda: nc.vector.tensor_tensor(out=ptt[:, 0:nt * 128], in0=ext[:, 0:nt * 128], in1=Et,
                                                               op=ALU.mult), reads=[exb, Eb], writes=[ptb])
                    return ptt, ptb, tl

                pstate = {}

                def pv(c, s, hh, ptt, ptb, tl):
                    slot = s % 8
                    if slot == 0:
                        pstate["po"] = po.next()
                    pot, pob = pstate["po"]
                    js = [j for j, t in enumerate(tl) if t is not None]
                    for j in js:
                        fw.op(pe, lambda: nc.tensor.matmul(pot[:, slot, 0:34], lhsT=ptt[:, j * 128:(j + 1) * 128],
                                                           rhs=c["va"][:, tl[j], hh, :], start=(j == js[0]),
                                                           stop=(j == js[-1])),
                              reads=[ptb, c["vab"]], writes=[pob])
                    if slot == 7:
                        rt_, rb_ = rec.next()
                        fw.op(dve, lambda: nc.vector.reciprocal(out=rt_[:], in_=pot[:, :, 32]), reads=[pob], writes=[rb_])
                        fw.op(dve, lambda: nc.vector.tensor_tensor(
                            out=c["og"][:, s - 7:s + 1, hh * 32:(hh + 1) * 32], in0=pot[:, :, 0:32],
                            in1=rt_[:].unsqueeze(2).to_broadcast([128, 8, 32]), op=ALU.mult),
                            reads=[pob, rb_], writes=[c["ogb"]])

                nxt, ops0 = prep_head(0)
                for o_ in ops0:
                    o_()
                for h in range(64):
                    g, hh = divmod(h, 4)
                    c = nxt
                    pops = []
                    if h + 1 < 64:
                        nxt, pops = prep_head(h + 1)
                    per = (len(pops) + 59) // 60
                    pend = [scores(c, 0), scores(c, 1)]
                    for s in range(64):
                        cur = pend.pop(0)
                        if s + 2 < 64:
                            pend.append(scores(c, s + 2))
                        pv(c, s, hh, *cur)
                        for o_ in pops[:per]:
                            o_()
                        pops = pops[per:]
                    for o_ in pops:
                        o_()
                    if hh == 3:
                        fw.dma(pool, s_o[:, g * 128:(g + 1) * 128].rearrange("(t p) d -> p t d", p=128), c["og"][:],
                               owner=c["ogb"], reads=[c["ogb"]], ndesc=8192)
            fw.barrier()

        cast_weight("ro", w_ro, wb_ro)
        cast_weight("in1", w_in1, wb_in1)
        cast_weight("out1", w_out1, wb_out1)
        if "C" in phases:
            cmn = open_common(True)
            with ExitStack() as sC_:
                cs = [(sb("cos%d" % i, [128, 512], F32, sC_), sb("sin%d" % i, [128, 512], F32, sC_), fw.buf("cs%d" % i)) for i in range(2)]
                rot = Ring([(sb("rot%d" % i, [128, 512], F32, sC_), fw.buf("rot%d" % i)) for i in range(4)])
                qsave = Ring([(sb("qsv%d" % i, [128, 512], F32, sC_), fw.buf("qsv%d" % i)) for i in range(2)])
                gt = Ring([(sb("gt%d" % i, [128, 512], F32, sC_), fw.buf("gt%d" % i)) for i in range(2)])
                for blk in range(NBLK):
                    t0 = blk * 512
                    for i in range(4):
                        fw.dma(sp, hb[i][0][:], s_o[t0 + i * 128:t0 + (i + 1) * 128, :], owner=hb[i][1], writes=[hb[i][1]])
                    load_x(x_in, blk)
                    cst, snt, csb = cs[blk % 2]
                    fw.dma(sp, cst[:], cos_d[:, t0:t0 + 512], owner=csb, writes=[csb])
                    fw.dma(sp, snt[:], sin_d[:, t0:t0 + 512], owner=csb, writes=[csb])
                    transpose_to(TA, hb, 16, None, 0)
                    proj_tokmajor_add([TA], 1, "o", wb_o)
                    norm_T(TB, 1)
                    mlp(TB, "in0", wb_in0, "out0", wb_out0)
                    for i in range(4):
                        fw.dma(pool, s_x2[t0 + i * 128:t0 + (i + 1) * 128, :], xt[i][0][:], owner=xt[i][1], reads=[xt[i][1]])
                    norm_T(TA, 2)
                    saved = {}

                    def post_rqk(wt, oc, pt, pb):
                        chunk = (wt % 4) * 4 + oc
                        dst = s_rqT if wt < 4 else s_rkT
                        if chunk % 2 == 0:
                            qs, qsb = qsave.next()
                            evac(qs[:], pt[:], [pb], [qsb])
                            saved["x1"] = (qs, qsb)
                        else:
                            q1, q1b = saved["x1"]
                            t1, t1b = rot.next()
                            t2, t2b = rot.next()
                            fw.op(dve, lambda: nc.vector.tensor_tensor(out=t1[:], in0=pt[:], in1=snt[:], op=ALU.mult),
                                  reads=[pb, csb], writes=[t1b])
                            fw.op(dve, lambda: nc.vector.tensor_tensor(out=t2[:], in0=pt[:], in1=cst[:], op=ALU.mult),
                                  reads=[pb, csb], writes=[t2b])
                            t3, t3b = rot.next()
                            t4, t4b = rot.next()
                            fw.op(pool, lambda: nc.gpsimd.tensor_tensor(out=t3[:], in0=q1[:], in1=cst[:], op=ALU.mult),
                                  reads=[q1b, csb], writes=[t3b])
                            fw.op(pool, lambda: nc.gpsimd.tensor_tensor(out=t4[:], in0=q1[:], in1=snt[:], op=ALU.mult),
                                  reads=[q1b, csb], writes=[t4b])
                            s1, s1b = stg.next()
                            s2, s2b = stg.next()
                            fw.op(pool, lambda: nc.gpsimd.tensor_tensor(out=s1[:], in0=t3[:], in1=t1[:], op=ALU.subtract),
                                  reads=[t3b, t1b], writes=[s1b])
                            fw.op(pool, lambda: nc.gpsimd.tensor_tensor(out=s2[:], in0=t4[:], in1=t2[:], op=ALU.add),
                                  reads=[t4b, t2b], writes=[s2b])
                            f0 = (chunk - 1) * 128
                            fw.dma(pool, dst[f0:f0 + 128, t0:t0 + 512], s1[:], owner=s1b, reads=[s1b])
                            fw.dma(pool, dst[f0 + 128:f0 + 256, t0:t0 + 512], s2[:], owner=s2b, reads=[s2b])

                    proj_featmajor(TA, "r", wb_r, range(8), None, blk, post=post_rqk)
                    for wt in range(8, 24):
                        wtile, wbuf = load_w("r", wb_r, 0, wt * 512)
                        for i in range(4):
                            pt, pb = pm.next()
                            for kc in range(16):
                                fw.op(pe, lambda: nc.tensor.matmul(pt[:], lhsT=TA[0][:, kc, i * 128:(i + 1) * 128],
                                                                   rhs=wtile[:, kc, :], start=(kc == 0), stop=(kc == 15)),
                                      reads=[wbuf, TA[1]], writes=[pb])
                            stt, stb = stg.next()
                            if wt < 16:
                                evac(stt[:], pt[:], [pb], [stb])
                                fw.dma(pool, s_rv[t0 + i * 128:t0 + (i + 1) * 128, (wt - 8) * 512:(wt - 7) * 512], stt[:],
                                       owner=stb, reads=[stb])
                            else:
                                fw.op(act, lambda: nc.scalar.activation(out=stt[:], in_=pt[:], func=AF.Silu),
                                      reads=[pb], writes=[stb])
                                fw.dma(pool, s_rg[t0 + i * 128:t0 + (i + 1) * 128, (wt - 16) * 512:(wt - 15) * 512], stt[:],
                                       owner=stb, reads=[stb])
            fw.barrier()
            cmn.close()

        if "D" in phases:
            cmn = open_common(False, nptr=1)
            with ExitStack() as sD_:
                dec = sb("dec", [128, 16], F32, sD_)
                lg = sb("lg", [128, 16], F32, sD_)
                MT = sb("MT", [128, 8, 128], F32, sD_)
                mtmp = sb("mtmp", [128, 128], F32, sD_)
                qdf = sb("qdf", [128, 8, 128], F32, sD_)
                qdb = sb("qdb", [128, 8, 128], F32, sD_)
                kdf = sb("kdf", [128, 8], F32, sD_)
                kdb = sb("kdb", [128, 8], F32, sD_)
                cdf = sb("cdf", [128, 8], F32, sD_)
                cdb = sb("cdb", [128, 8], F32, sD_)
                Bt = fw.buf("rtabs")
                fw.dma(sp, dec[:], dec_d, owner=Bt, writes=[Bt])
                fw.op(act, lambda: nc.scalar.activation(out=lg[:], in_=dec[:], func=AF.Exp), reads=[Bt], writes=[Bt])
                fw.op(dve, lambda: nc.vector.tensor_scalar(out=lg[:], in0=lg[:], scalar1=-1.0, scalar2=None, op0=ALU.mult),
                      reads=[Bt], writes=[Bt])
                for hd in range(8):
                    lf = lg[:, hd:hd + 1]
                    lb = lg[:, 8 + hd:9 + hd]
                    fw.op(act, lambda: nc.scalar.activation(out=MT[:, hd, :], in_=ctab[:, C_DPOS:C_DPOS + 128], func=AF.Exp, scale=lf),
                          reads=[Bt, Bc], writes=[Bt])
                    fw.op(dve, lambda: nc.vector.scalar_tensor_tensor(out=MT[:, hd, :], in0=MT[:, hd, :], scalar=RET_SCALE,
                                                                      in1=ctab[:, C_MPOS:C_MPOS + 128], op0=ALU.mult, op1=ALU.mult),
                          reads=[Bt, Bc], writes=[Bt])
                    fw.op(act, lambda: nc.scalar.activation(out=mtmp[:], in_=ctab[:, C_DNEG:C_DNEG + 128], func=AF.Exp, scale=lb),
                          reads=[Bt, Bc], writes=[Bt])
                    fw.op(dve, lambda: nc.vector.scalar_tensor_tensor(out=mtmp[:], in0=mtmp[:], scalar=RET_SCALE,
                                                                      in1=ctab[:, C_MNEG:C_MNEG + 128], op0=ALU.mult, op1=ALU.mult),
                          reads=[Bt, Bc], writes=[Bt])
                    fw.op(dve, lambda: nc.vector.tensor_tensor(out=MT[:, hd, :], in0=MT[:, hd, :], in1=mtmp[:], op=ALU.add),
                          reads=[Bt], writes=[Bt])
                    fw.op(act, lambda: nc.scalar.activation(out=qdf[:, hd, :], in_=ctab[:, C_ROW1:C_ROW1 + 128], func=AF.Exp, scale=lf),
                          reads=[Bt, Bc], writes=[Bt])
                    fw.op(act, lambda: nc.scalar.activation(out=qdb[:, hd, :], in_=ctab[:, C_ROW2:C_ROW2 + 128], func=AF.Exp, scale=lb),
                          reads=[Bt, Bc], writes=[Bt])
                    fw.op(act, lambda: nc.scalar.activation(out=kdf[:, hd:hd + 1], in_=ctab[:, C_K1:C_K1 + 1], func=AF.Exp, scale=lf),
                          reads=[Bt, Bc], writes=[Bt])
                    fw.op(act, lambda: nc.scalar.activation(out=kdb[:, hd:hd + 1], in_=ctab[:, C_K2:C_K2 + 1], func=AF.Exp, scale=lb),
                          reads=[Bt, Bc], writes=[Bt])
                    fw.op(act, lambda: nc.scalar.activation(out=cdf[:, hd:hd + 1], in_=ctab[:, C_128:C_128 + 1], func=AF.Exp, scale=lf),
                          reads=[Bt, Bc], writes=[Bt])
                    fw.op(act, lambda: nc.scalar.activation(out=cdb[:, hd:hd + 1], in_=ctab[:, C_128:C_128 + 1], func=AF.Exp, scale=lb),
                          reads=[Bt, Bc], writes=[Bt])
                fw.op(dve, lambda: nc.vector.tensor_scalar(out=kdf[:], in0=kdf[:], scalar1=RET_SCALE, scalar2=None, op0=ALU.mult),
                      reads=[Bt], writes=[Bt])
                fw.op(dve, lambda: nc.vector.tensor_scalar(out=kdb[:], in0=kdb[:], scalar1=RET_SCALE, scalar2=None, op0=ALU.mult),
                      reads=[Bt], writes=[Bt])

                HS = []
                for u in range(2):
                    HS.append(dict(
                        grp=Ring([(sb("gq%d_%d" % (u, i), [128, 2, 1024], BF16, sD_), sb("gk%d_%d" % (u, i), [128, 2, 1024], BF16, sD_),
                                   sb("gv%d_%d" % (u, i), [128, 8, 512], BF16, sD_), fw.buf("grp%d_%d" % (u, i))) for i in range(2)]),
                        qsc=Ring([(sb("qsc%d_%d" % (u, i), [128, 2, 1024], BF16, sD_), fw.buf("qsc%d_%d" % (u, i))) for i in range(2)]),
                        state=(sb("state%d" % u, [128, 2, 512], F32, sD_), fw.buf("state%d" % u)),
                        stbf=(sb("stbf%d" % u, [128, 2, 512], BF16, sD_), fw.buf("stbf%d" % u)),
                        u=u))
                ktl = Ring([(sb("ktl%d" % i, [128, 256], BF16, sD_), fw.buf("ktl%d" % i)) for i in range(3)])
                smk = Ring([(sb("smk%d" % i, [128, 128], BF16, sD_), fw.buf("smk%d" % i)) for i in range(3)])
                ybo = Ring([(sb("ybo%d" % i, [128, 512], F32, sD_), fw.buf("ybo%d" % i)) for i in range(3)])
                ybi = Ring([(sb("ybi%d" % i, [128, 512], F32, sD_), fw.buf("ybi%d" % i)) for i in range(4)])
                ysum = Ring([(sb("ysum%d" % i, [128, 512], F32, sD_), fw.buf("ysum%d" % i)) for i in range(4)])
                yno = Ring([(sb("yno%d" % i, [128, 512], BF16, sD_), fw.buf("yno%d" % i)) for i in range(3)])
                gst = Ring([(sb("gst%d" % i, [128, 8, 2], F32, sD_), fw.buf("gst%d" % i)) for i in range(3)])
                pS = Ring([(pm.items[0][0][:, k * 128:(k + 1) * 128], fw.buf("pS%d" % k)) for k in range(4)])
                pY = Ring([pm.items[1], pm.items[2]])
                pSt = Ring([(pm.items[3][0], fw.buf("pstA")), (ps("pst1", [128, 512], F32, sD_), fw.buf("pstB")),
                            (ps("pst2", [128, 512], F32, sD_), fw.buf("pstC")), (ps("pst3", [128, 512], F32, sD_), fw.buf("pstD"))])
                ptr.items = [(ptr.items[0][0][:, i * 256:i * 256 + 256], fw.buf("ptrD%d" % i)) for i in range(2)]

                def load_group(hs, hd, gi):
                    gq, gk, gv, gb = hs["grp"].next()
                    c0 = gi * 1024
                    fw.dma(sp, gq[:], s_rqT[hd * 256:(hd + 1) * 256, c0:c0 + 1024].rearrange("(x p) t -> p x t", p=128),
                           owner=gb, writes=[gb])
                    fw.dma(sp, gk[:], s_rkT[hd * 256:(hd + 1) * 256, c0:c0 + 1024].rearrange("(x p) t -> p x t", p=128),
                           owner=gb, writes=[gb])
                    fw.dma(sp, gv[:], s_rv[c0:c0 + 1024, hd * 512:(hd + 1) * 512].rearrange("(c p) e -> p c e", p=128),
                           owner=gb, writes=[gb])
                    hs["g"] = (gq, gk, gv, gb)

                def scale_q(hs, qd_tab, hd):
                    gq, gk, gv, gb = hs["g"]
                    qs, qsb = hs["qsc"].next()
                    fw.op(pool, lambda: nc.gpsimd.tensor_tensor(
                        out=qs[:].rearrange("p x (c i) -> p (x c) i", i=128),
                        in0=gq[:].rearrange("p x (c i) -> p (x c) i", i=128),
                        in1=qd_tab[:, hd, :].unsqueeze(1).to_broadcast([128, 16, 128]), op=ALU.mult),
                        reads=[gb, Bt], writes=[qsb])
                    hs["qs"] = (qs, qsb)

                def state_T(hs, cl, kd, hd):
                    gq, gk, gv, gb = hs["g"]
                    ptk, ptkb = ptr.next()
                    for x in range(2):
                        fw.op(pe, lambda: nc.tensor.transpose(out=ptk[:, x * 128:(x + 1) * 128],
                                                              in_=gk[:, x, cl * 128:(cl + 1) * 128], identity=ident[:]),
                              reads=[gb, Bc], writes=[ptkb])
                    kt2, kt2b = ktl.next()
                    fw.op(act, lambda: nc.scalar.activation(out=kt2[:], in_=ptk[:, 0:256], func=AF.Copy, scale=kd[:, hd:hd + 1]),
                          reads=[ptkb, Bt], writes=[kt2b])
                    hs["kt2"] = (kt2, kt2b)

                def state_update(hs, cl, cd, hd):
                    gq, gk, gv, gb = hs["g"]
                    state, stbf = hs["state"], hs["stbf"]
                    kt2, kt2b = hs["kt2"]
                    for x in range(2):
                        pst_, pstb_ = pSt.next()
                        fw.op(pe, lambda: nc.tensor.matmul(pst_[:], lhsT=kt2[:, x * 128:(x + 1) * 128], rhs=gv[:, cl, :],
                                                           start=True, stop=True), reads=[kt2b, gb], writes=[pstb_])
                        fw.op(dve, lambda: nc.vector.scalar_tensor_tensor(out=state[0][:, x, :], in0=state[0][:, x, :],
                                                                          scalar=cd[:, hd:hd + 1], in1=pst_[:],
                                                                          op0=ALU.mult, op1=ALU.add),
                              reads=[pstb_, state[1], Bt], writes=[state[1]])
                    fw.op(act, lambda: nc.scalar.activation(out=stbf[0][:], in_=state[0][:], func=AF.Copy),
                          reads=[state[1]], writes=[stbf[1]])

                def reset_state(hs):
                    fw.op(dve, lambda: nc.vector.memset(hs["state"][0][:], 0.0), writes=[hs["state"][1]])
                    fw.op(pool, lambda: nc.gpsimd.memset(hs["stbf"][0][:], 0.0), writes=[hs["stbf"][1]])

                def carry_boundary(hs):
                    state, stbf = hs["state"], hs["stbf"]
                    fw.op(dve, lambda: nc.vector.tensor_scalar(out=state[0][:], in0=state[0][:], scalar1=flag[:, 0:1],
                                                               scalar2=None, op0=ALU.mult),
                          reads=[state[1], Bc], writes=[state[1]])
                    fw.op(act, lambda: nc.scalar.activation(out=stbf[0][:], in_=state[0][:], func=AF.Copy),
                          reads=[state[1]], writes=[stbf[1]])

                def su_mm(hs, cl):
                    gq, gk, gv, gb = hs["g"]
                    kt2, kt2b = hs["kt2"]
                    hs["pst"] = []
                    for x in range(2):
                        pst_, pstb_ = pSt.next()
                        fw.op(pe, lambda: nc.tensor.matmul(pst_[:], lhsT=kt2[:, x * 128:(x + 1) * 128], rhs=gv[:, cl, :],
                                                           start=True, stop=True), reads=[kt2b, gb], writes=[pstb_])
                        hs["pst"].append((pst_, pstb_))

                def su_acc(hs, cd, hd):
                    state = hs["state"]
                    for x in range(2):
                        pst_, pstb_ = hs["pst"][x]
                        fw.op(dve, lambda: nc.vector.scalar_tensor_tensor(out=state[0][:, x, :], in0=state[0][:, x, :],
                                                                          scalar=cd[:, hd:hd + 1], in1=pst_[:],
                                                                          op0=ALU.mult, op1=ALU.add),
                              reads=[pstb_, state[1], Bt], writes=[state[1]])

                def su_bf(hs):
                    state, stbf = hs["state"], hs["stbf"]
                    fw.op(act, lambda: nc.scalar.activation(out=stbf[0][:], in_=state[0][:], func=AF.Copy),
                          reads=[state[1]], writes=[stbf[1]])

                def y_inter(hs, cl, first):
                    qs, qsb = hs["qs"]
                    stbf = hs["stbf"]
                    yt, ytb = hs["yt"]
                    for x in range(2):
                        fw.op(pe, lambda: nc.tensor.matmul(yt[:], lhsT=qs[:, x, cl * 128:(cl + 1) * 128],
                                                           rhs=stbf[0][:, x, :], start=(first and x == 0), stop=(x == 1)),
                              reads=[qsb, stbf[1]], writes=[ytb])

                def bwd_round(heads, c, cl):
                    for hs, hd in heads:
                        if c == 31:
                            carry_boundary(hs)
                        state_T(hs, cl, kdb, hd)
                    for hs, hd in heads:
                        hs["yt"] = pY.next()
                        y_inter(hs, cl, True)
                        su_mm(hs, cl)
                    for hs, hd in heads:
                        su_acc(hs, cdb, hd)
                    for hs, hd in heads:
                        su_bf(hs)
                    for hs, hd in heads:
                        yt, ytb = hs["yt"]
                        yo, yob = ybo.next()
                        evac(yo[:], yt[:], [ytb], [yob])
                        fw.dma(pool, s_yb[c * 128:(c + 1) * 128, hs["u"] * 512:(hs["u"] + 1) * 512], yo[:], owner=yob, reads=[yob])

                def fwd_round(heads, c, cl):
                    rnd = {"gs": gst.next(), "items": []}
                    for hs, hd in heads:
                        gq, gk, gv, gb = hs["g"]
                        if c == 32:
                            carry_boundary(hs)
                        yi, yib = ybi.next()
                        fw.dma(sp, yi[:], s_yb[c * 128:(c + 1) * 128, hs["u"] * 512:(hs["u"] + 1) * 512], owner=yib, writes=[yib])
                        hs["yi"] = (yi, yib)
                        sT_, sTb = pS.next()
                        for x in range(2):
                            fw.op(pe, lambda: nc.tensor.matmul(sT_, lhsT=gk[:, x, cl * 128:(cl + 1) * 128],
                                                               rhs=gq[:, x, cl * 128:(cl + 1) * 128],
                                                               start=(x == 0), stop=(x == 1)),
                                  reads=[gb], writes=[sTb])
                        hs["sT"] = (sT_, sTb)
                        state_T(hs, cl, kdf, hd)
                    for hs, hd in heads:
                        gq, gk, gv, gb = hs["g"]
                        sT_, sTb = hs["sT"]
                        sm, smb = smk.next()
                        fw.op(dve, lambda: nc.vector.tensor_tensor(out=sm[:], in0=sT_, in1=MT[:, hd, :], op=ALU.mult),
                              reads=[sTb, Bt], writes=[smb])
                        hs["sm"] = (sm, smb)
                    for hs, hd in heads:
                        gq, gk, gv, gb = hs["g"]
                        sm, smb = hs["sm"]
                        hs["yt"] = pY.next()
                        yt, ytb = hs["yt"]
                        fw.op(pe, lambda: nc.tensor.matmul(yt[:], lhsT=sm[:], rhs=gv[:, cl, :], start=True, stop=False),
                              reads=[smb, gb], writes=[ytb])
                        y_inter(hs, cl, False)
                        su_mm(hs, cl)
                    for hs, hd in heads:
                        su_acc(hs, cdf, hd)
                    for hs, hd in heads:
                        su_bf(hs)
                    for hs, hd in heads:
                        yt, ytb = hs["yt"]
                        yi, yib = hs["yi"]
                        ysm, ysb = ysum.next()
                        fw.op(dve, lambda: nc.vector.tensor_tensor(out=ysm[:], in0=yt[:], in1=yi[:], op=ALU.add),
                              reads=[ytb, yib], writes=[ysb])
                        gs, gsb = rnd["gs"]
                        u = hs["u"]
                        fw.op(act, lambda: nc.scalar.activation(out=S["junk"][:, 0:512], in_=ysm[:], func=AF.Copy,
                                                                accum_out=gs[:, 0, u:u + 1]), reads=[ysb], writes=[Bjunk, gsb])
                        fw.op(act, lambda: nc.scalar.activation(out=S["junk"][:, 0:512], in_=ysm[:], func=AF.Square,
                                                                accum_out=gs[:, 1, u:u + 1]), reads=[ysb], writes=[Bjunk, gsb])
                        rnd["items"].append((hs, hd, c, ysm, ysb))
                    gn_tail(rnd)

                def gn_tail(rnd):
                    gs, gsb = rnd["gs"]
                    fw.op(act, lambda: nc.scalar.activation(out=S["junk"][:, 0:8], in_=epsg[:].to_broadcast([128, 8]), func=AF.Copy),
                          reads=[Bc], writes=[Bjunk, gsb])
                    fw.op(dve, lambda: nc.vector.tensor_scalar(out=gs[:, 2, :], in0=gs[:, 0, :], scalar1=1.0 / 512, scalar2=None,
                                                               op0=ALU.mult), reads=[gsb], writes=[gsb])
                    fw.op(dve, lambda: nc.vector.tensor_tensor(out=gs[:, 3, :], in0=gs[:, 2, :], in1=gs[:, 2, :], op=ALU.mult),
                          reads=[gsb], writes=[gsb])
                    fw.op(dve, lambda: nc.vector.scalar_tensor_tensor(out=gs[:, 4, :], in0=gs[:, 1, :], scalar=1.0 / 512,
                                                                      in1=gs[:, 3, :], op0=ALU.mult, op1=ALU.subtract),
                          reads=[gsb], writes=[gsb])
                    fw.op(act, lambda: nc.scalar.activation(out=gs[:, 5, :], in_=gs[:, 4, :], func=AF.Sqrt, bias=epsg[:, 0:1]),
                          reads=[gsb, Bc], writes=[gsb])
                    fw.op(dve, lambda: nc.vector.reciprocal(out=gs[:, 6, :], in_=gs[:, 5, :]), reads=[gsb], writes=[gsb])
                    fw.op(dve, lambda: nc.vector.scalar_tensor_tensor(out=gs[:, 7, :], in0=gs[:, 2, :], scalar=-1.0, in1=gs[:, 6, :],
                                                                      op0=ALU.mult, op1=ALU.mult), reads=[gsb], writes=[gsb])
                    for hs, hd, c, ysm, ysb in rnd["items"]:
                        u = hs["u"]
                        yn_, ynb = yno.next()
                        fw.op(pool, lambda: nc.gpsimd.tensor_scalar(out=yn_[:], in0=ysm[:], scalar1=gs[:, 6, u:u + 1],
                                                                    scalar2=gs[:, 7, u:u + 1], op0=ALU.mult, op1=ALU.add),
                              reads=[ysb, gsb], writes=[ynb])
                        fw.dma(pool, s_yn[c * 128:(c + 1) * 128, hd * 512:(hd + 1) * 512], yn_[:], owner=ynb, reads=[ynb])

                def prefetch(heads, gi, qd):
                    for hs, hd in heads:
                        cur_g, cur_q = hs.get("g"), hs.get("qs")
                        load_group(hs, hd, gi)
                        scale_q(hs, qd, hd)
                        hs["g_n"], hs["qs_n"] = hs["g"], hs["qs"]
                        hs["g"], hs["qs"] = cur_g, cur_q

                def swap_in(heads):
                    for hs, hd in heads:
                        hs["g"], hs["qs"] = hs["g_n"], hs["qs_n"]

                for hp in range(4):
                    heads = [(HS[0], 2 * hp), (HS[1], 2 * hp + 1)]
                    for hs, hd in heads:
                        reset_state(hs)
                    prefetch(heads, 7, qdb)
                    for gi in range(7, -1, -1):
                        swap_in(heads)
                        for cl in range(7, -1, -1):
                            if cl == 5 and gi > 0:
                                prefetch(heads, gi - 1, qdb)
                            bwd_round(heads, gi * 8 + cl, cl)
                    fw.barrier()
                    for hs, hd in heads:
                        reset_state(hs)
                    prefetch(heads, 0, qdf)
                    for gi in range(8):
                        swap_in(heads)
                        for cl in range(8):
                            if cl == 2 and gi < 7:
                                prefetch(heads, gi + 1, qdf)
                            fwd_round(heads, gi * 8 + cl, cl)
                    fw.barrier()
            cmn.close()

        if "E" in phases:
            cmn = open_common(True, need_stg=False)
            with ExitStack() as sE_:
                gg = [(sb("gg%d" % i, [128, D], BF16, sE_), fw.buf("gg%d" % i)) for i in range(4)]
                nfin = sb("nfin", [128, D], F32, sE_)
                fw.dma(sp, nfin[:], nrm_fin_d, owner=Bl, writes=[Bc])
                for blk in range(NBLK):
                    t0 = blk * 512
                    load_x(s_x2, blk)
                    for half, dstT in enumerate((TA, TB)):
                        for i in range(4):
                            r0 = t0 + i * 128
                            fw.dma(sp, hb[i][0][:], s_yn[r0:r0 + 128, half * D:(half + 1) * D], owner=hb[i][1], writes=[hb[i][1]])
                            fw.dma(sp, gg[i][0][:], s_rg[r0:r0 + 128, half * D:(half + 1) * D], owner=gg[i][1], writes=[gg[i][1]])
                        for i in range(4):
                            if i % 2:
                                fw.op(dve, lambda: nc.vector.tensor_tensor(out=hb[i][0][:], in0=hb[i][0][:], in1=gg[i][0][:], op=ALU.mult),
                                      reads=[hb[i][1], gg[i][1]], writes=[hb[i][1]])
                            else:
                                fw.op(pool, lambda: nc.gpsimd.tensor_tensor(out=hb[i][0][:], in0=hb[i][0][:], in1=gg[i][0][:], op=ALU.mult),
                                      reads=[hb[i][1], gg[i][1]], writes=[hb[i][1]])
                        transpose_to(dstT, hb, 16, gnw_col, half * 16)
                    proj_tokmajor_add([TA, TB], 2, "ro", wb_ro)
                    norm_T(TA, 3)
                    mlp(TA, "in1", wb_in1, "out1", wb_out1)
                    rms_rstd(xt, epsn, 1.0 / D)
                    for i in range(4):
                        ut, ub = u2T.items[i // 2]
                        ov = ut[:].rearrange("p a b -> p (a b)").bitcast(F32)[:, (i % 2) * D:(i % 2 + 1) * D]
                        fw.op(dve, lambda: nc.vector.scalar_tensor_tensor(out=ov, in0=xt[i][0][:], scalar=S["rstd"][:, i:i + 1],
                                                                          in1=nfin[:], op0=ALU.mult, op1=ALU.mult),
                              reads=[xt[i][1], Bss, Bc], writes=[ub])
                        fw.dma(pool, y_out[t0 + i * 128:t0 + (i + 1) * 128, :], ov, owner=ub, reads=[ub])
            fw.barrier()
            cmn.close()
        fw.barrier()
        if dbg and 'nodbg' not in KB:
            Bd = fw.buf("dbgout")
            for nm, t_, fm in (("qT", s_qT, True), ("kT", s_kT, True), ("v", s_v, False), ("o", s_o, False),
                               ("x2", s_x2, False), ("rqT", s_rqT, True), ("rkT", s_rkT, True), ("rv", s_rv, False),
                               ("rg", s_rg, False), ("yn", s_yn, False)):
                segs = [(0, 256), (1024, 1152), (1536, 1664), (3968, 4224)]
                tot = sum(e - b_ for b_, e in segs)
                if fm:
                    dd = nc.dram_tensor("dbg_" + nm, [t_.shape[0], tot], t_.dtype, kind="ExternalOutput").ap()
                else:
                    dd = nc.dram_tensor("dbg_" + nm, [tot, t_.shape[1]], t_.dtype, kind="ExternalOutput").ap()
                o_ = 0
                for b_, e in segs:
                    if fm:
                        fw.dma(sp, dd[:, o_:o_ + e - b_], t_[:, b_:e], owner=Bd)
                    else:
                        fw.dma(sp, dd[o_:o_ + e - b_, :], t_[b_:e, :], owner=Bd)
                    o_ += e - b_
            fw.barrier()
        stats = dict(nwaits=fw.nwaits, **{w.name: w.nseq for w in fw.engs})
    return nc, stats


def _const_tables():
    ct = np.zeros((128, NCTAB), np.float32)
    j = np.arange(128)[:, None].astype(np.float32)
    i = np.arange(128)[None, :].astype(np.float32)
    ct[:, C_DPOS:C_DPOS + 128] = np.maximum(i - j, 0)
    ct[:, C_DNEG:C_DNEG + 128] = np.maximum(j - i, 0)
    ct[:, C_MPOS:C_MPOS + 128] = (i >= j)
    ct[:, C_MNEG:C_MNEG + 128] = (j > i)
    ct[:, C_ROW1:C_ROW1 + 128] = np.broadcast_to(i + 1.0, (128, 128))
    ct[:, C_ROW2:C_ROW2 + 128] = np.broadcast_to(128.0 - i, (128, 128))
    ct[:, C_K1] = 127.0 - np.arange(128)
    ct[:, C_K2] = np.arange(128)
    ct[:, C_128] = 128.0
    kc = np.arange(64)[:, None]
    qc = np.arange(64)[None, :]
    cs = np.clip(qc - 8, 0, 48)
    cm = ((kc >= cs) & (kc < cs + 16)).astype(np.float32)
    ct[0:64, C_CMASK:C_CMASK + 64] = cm
    ct[64:128, C_CMASK:C_CMASK + 64] = cm
    return ct


def _rot_tables(pos):
    half = 128
    freqs = (np.float32(10000.0) ** (-np.arange(half, dtype=np.float32) / np.float32(half))).astype(np.float32)
    ang = pos.astype(np.float32)[None, :] * freqs[:, None]
    return np.cos(ang).astype(np.float32), np.sin(ang).astype(np.float32)


_CACHE = {}


def kernel(x_prompt, x_sample, norm_mix, na_w_qkv, na_rpb, na_w_o, ret_w_qkvg, ret_decay_fwd,
           ret_decay_bwd, ret_gn_w, ret_w_o, norm_mlp, mlp_w_in, mlp_w_out, norm_final):
    f = lambda a: np.ascontiguousarray(np.asarray(a, dtype=np.float32))
    x_prompt, x_sample = f(x_prompt), f(x_sample)
    if "nc" not in _CACHE:
        _CACHE["nc"] = build_program()[0]
    nc = _CACHE["nc"]
    nrm = np.stack([f(norm_mix)[0], f(norm_mlp)[0], f(norm_mix)[1], f(norm_mlp)[1]], 0)
    nrm_col = np.ascontiguousarray(nrm.reshape(4, 16, 128).transpose(2, 0, 1).reshape(128, 64))
    nrm_fin = np.ascontiguousarray(np.broadcast_to(f(norm_final)[None, :], (128, D)))
    gnw_col = np.ascontiguousarray(f(ret_gn_w)[0].reshape(32, 128).T)
    rpbf = np.ascontiguousarray(f(na_rpb)[0][:, :, ::-1].reshape(64, 465))
    dec = np.ascontiguousarray(np.broadcast_to(np.concatenate([f(ret_decay_fwd)[0], f(ret_decay_bwd)[0]])[None, :], (128, 16)))
    ctab = _const_tables()
    cos_j, sin_j = _rot_tables(np.arange(NTOK))
    cos_s, sin_s = _rot_tables(np.concatenate([np.arange(4096), np.arange(4096)]))
    common = {
        "w_qkv": f(na_w_qkv)[0], "w_o": f(na_w_o)[0], "w_in0": f(mlp_w_in)[0], "w_out0": f(mlp_w_out)[0],
        "w_r": f(ret_w_qkvg)[0], "w_ro": f(ret_w_o)[0], "w_in1": f(mlp_w_in)[1], "w_out1": f(mlp_w_out)[1],
        "nrm_col": nrm_col, "nrm_fin": nrm_fin, "gnw_col": gnw_col, "rpbf": rpbf, "dec": dec, "ctab": ctab,
    }
    xp = np.ascontiguousarray(x_prompt.reshape(NTOK, D))
    in_maps = []
    for c in range(NCORES):
        m = dict(common)
        if c < 4:
            m["x"] = x_sample[c]
            joined = 1.0
        else:
            m["x"] = xp
            joined = 0.0
        m["cosT"], m["sinT"] = (cos_j, sin_j) if joined else (cos_s, sin_s)
        m["flag"] = np.full((128, 1), joined, np.float32)
        in_maps.append(m)
    if _CACHE.get("return_maps"):
        return in_maps
    res = run_bass_kernel_spmd(nc, in_maps, core_ids=list(range(NCORES)))
    y_sample = np.stack([np.asarray(res.results[c]["y"], dtype=np.float32) for c in range(4)], 0)
    y_prompt = np.asarray(res.results[4]["y"], dtype=np.float32).reshape(2, 4096, D)
    return (y_prompt, y_sample)
```
